# Optimizing a Trainium2 kernel written in Bass

```python
import jax, jax.numpy as jnp
from jax import lax
import numpy as np

D_MODEL = 1024
BATCH = 8
SEQ = 2048
DEPTH = 1
DEC_BATCH = 128
DEC_SEQ = 4
PAST_LEN = 16384
PAGE_SIZE = 128

POOL_WIDTH = D_MODEL // 4
POOL_WINDOWS = (2, 4, 8, 16)
POOL_GROUPS = len(POOL_WINDOWS)
POOL_GDIM = POOL_WIDTH // POOL_GROUPS
POOL_BUF = max(POOL_WINDOWS) - 1
HG_HEADS = 4
HG_KDIM = 128
HG_WIDTH = HG_HEADS * HG_KDIM
HG_VDIM = D_MODEL // 2 // HG_HEADS
HG_VWIDTH = HG_HEADS * HG_VDIM
HG_CHUNK = 64
MEM_LEN = 256
MEM_HEADS = 4
MEM_HDIM = D_MODEL // 4 // MEM_HEADS
MEM_WIDTH = MEM_HEADS * MEM_HDIM
N_BRANCH = 3
D_FF = 2816
CONV_W = 3

LN_EPS = 1e-5
RMS_EPS = 1e-6
DN_ALPHA = (2 * DEPTH) ** 0.25
DN_BETA = (8 * DEPTH) ** -0.25

SPLIT_SIZES = (POOL_WIDTH, HG_WIDTH, HG_WIDTH, HG_VWIDTH, HG_VWIDTH, MEM_WIDTH)
SPLIT_POINTS = tuple(int(s) for s in np.cumsum(SPLIT_SIZES))
IN_COLS = sum(SPLIT_SIZES) + N_BRANCH * D_MODEL

kernel_name = 'hybrid_pool_hgrn2_memxattn_convffn_step'


def layer_norm(x, g, b):
    xf = x.astype(jnp.float32)
    mu = jnp.mean(xf, axis=-1, keepdims=True)
    var = jnp.mean(jnp.square(xf - mu), axis=-1, keepdims=True)
    return ((xf - mu) * lax.rsqrt(var + LN_EPS) * g + b).astype(x.dtype)


def rms_norm(x, g):
    xf = x.astype(jnp.float32)
    return xf * lax.rsqrt(jnp.mean(jnp.square(xf), axis=-1, keepdims=True) + RMS_EPS) * g


def multiscale_pool(u, buf, pos0, w_grp, scale):
    bsz, seq_len, _ = u.shape
    full = jnp.concatenate([buf.astype(u.dtype), u], axis=1)
    csum = jnp.cumsum(full.astype(jnp.float32), axis=1)
    csum = jnp.concatenate([jnp.zeros((bsz, 1, POOL_WIDTH), jnp.float32), csum], axis=1)
    end = csum[:, POOL_BUF + 1:POOL_BUF + 1 + seq_len]
    pos = pos0 + jnp.arange(seq_len)
    pooled = []
    for g, w in enumerate(POOL_WINDOWS):
        sl = slice(g * POOL_GDIM, (g + 1) * POOL_GDIM)
        start = csum[:, POOL_BUF + 1 - w:POOL_BUF + 1 - w + seq_len, sl]
        count = jnp.minimum(pos + 1, w).astype(jnp.float32)[None, :, None]
        pooled.append((end[..., sl] - start) / count)
    diff = (jnp.concatenate(pooled, axis=-1) - u.astype(jnp.float32)).astype(u.dtype)
    diff = diff.reshape(bsz, seq_len, POOL_GROUPS, POOL_GDIM)
    mixed = jnp.einsum('blgc,gcd->blgd', diff, w_grp).reshape(bsz, seq_len, POOL_WIDTH) * scale
    return mixed, full[:, -POOL_BUF:]


def hgrn2_chunkwise(q, logf, k, v, s0):
    bsz, seq_len, n_heads, _ = q.shape
    dv = v.shape[-1]
    chunk = min(HG_CHUNK, seq_len)
    n_chunks = -(-seq_len // chunk)
    pad = n_chunks * chunk - seq_len

    def to_chunks(a):
        a = jnp.pad(a, ((0, 0), (0, pad), (0, 0), (0, 0)))
        return a.reshape(bsz, n_chunks, chunk, n_heads, a.shape[-1]).transpose(1, 0, 3, 2, 4)

    qc, lfc, kc, vc = to_chunks(q), to_chunks(logf), to_chunks(k), to_chunks(v)
    causal = jnp.tril(jnp.ones((chunk, chunk), bool))[:, :, None]

    def step(state, inp):
        qi, lfi, ki, vi = inp
        b = jnp.cumsum(lfi, axis=2)
        gap = b[:, :, :, None, :] - b[:, :, None, :, :]
        decay = jnp.exp(jnp.where(causal, gap, -jnp.inf))
        scores = jnp.einsum('bhtd,bhsd,bhtsd->bhts', qi, ki, decay)
        out = jnp.einsum('bhts,bhse->bhte', scores, vi) + jnp.einsum('bhtd,bhde->bhte', qi * jnp.exp(b), state)
        b_last = b[:, :, -1:, :]
        new_state = jnp.exp(b_last[:, :, 0, :, None]) * state + jnp.einsum('bhsd,bhse->bhde', ki * jnp.exp(b_last - b), vi)
        return new_state, out

    s_final, out = lax.scan(step, s0, (qc, lfc, kc, vc))
    out = out.transpose(1, 0, 3, 2, 4).reshape(bsz, n_chunks * chunk, n_heads, dv)[:, :seq_len]
    return out, s_final


def memory_attention(q, mem_k, mem_v):
    s = jnp.einsum('blhd,bmhd->bhlm', q, mem_k).astype(jnp.float32) * (MEM_HDIM ** -0.5)
    p = jax.nn.softmax(s, axis=-1).astype(mem_v.dtype)
    return jnp.einsum('bhlm,bmhd->blhd', p, mem_v)


def conv_ffn(h, buf, w_gate, w_up, conv_w, conv_b, w_down):
    seq_len = h.shape[1]
    a = h @ w_gate
    u = h @ w_up
    full = jnp.concatenate([buf.astype(a.dtype), a], axis=1)
    c = conv_b + full[:, 0:seq_len] * conv_w[0]
    for j in range(1, CONV_W):
        c = c + full[:, j:j + seq_len] * conv_w[j]
    act = jax.nn.gelu(c.astype(jnp.float32), approximate=False).astype(h.dtype)
    return (act * u) @ w_down, full[:, -(CONV_W - 1):]


def decoder_layer(x, pool_buf, hg_state, conv_buf, mem_k, mem_v, pos0, lb, p):
    bsz, seq_len, _ = x.shape
    dt = x.dtype
    u_a, q_b, f_b, i_b, g_b, q_c, gate_pre = jnp.split(x @ p['w_in'], SPLIT_POINTS, axis=-1)
    y_a, new_pool = multiscale_pool(u_a, pool_buf, pos0, p['w_pool_grp'], p['pool_scale'])
    f_pre = f_b.astype(jnp.float32)
    f = lb + (1.0 - lb) * jax.nn.sigmoid(f_pre)
    k = (1.0 - lb) * jax.nn.sigmoid(-f_pre)
    ks = (bsz, seq_len, HG_HEADS, HG_KDIM)
    vs = (bsz, seq_len, HG_HEADS, HG_VDIM)
    o_b, s_new = hgrn2_chunkwise(jax.nn.silu(q_b.astype(jnp.float32)).reshape(ks), jnp.log(f).reshape(ks),
                                 k.reshape(ks), i_b.astype(jnp.float32).reshape(vs), hg_state.astype(jnp.float32))
    o_b = rms_norm(o_b, p['hg_norm_g'].reshape(HG_HEADS, HG_VDIM)).reshape(bsz, seq_len, HG_VWIDTH)
    y_b = (o_b * jax.nn.silu(g_b.astype(jnp.float32))).astype(dt)
    y_c = memory_attention(q_c.reshape(bsz, seq_len, MEM_HEADS, MEM_HDIM), mem_k, mem_v).reshape(bsz, seq_len, MEM_WIDTH)
    gates = jax.nn.sigmoid(gate_pre.astype(jnp.float32)).astype(dt).reshape(bsz, seq_len, N_BRANCH, D_MODEL)
    merged = (gates[:, :, 0] * (y_a @ p['w_br_pool'])
              + gates[:, :, 1] * (y_b @ p['w_br_hg'])
              + gates[:, :, 2] * (y_c @ p['w_br_mem']))
    h = layer_norm(DN_ALPHA * x + merged @ p['w_out'], p['ln1_g'], p['ln1_b'])
    ff, new_conv = conv_ffn(h, conv_buf, p['w_gate'], p['w_up'], p['conv_w'], p['conv_b'], p['w_down'])
    y = layer_norm(DN_ALPHA * h + ff, p['ln2_g'], p['ln2_b'])
    return y, new_pool, s_new.astype(hg_state.dtype), new_conv


def setup_inputs(seed: int = 0) -> dict:
    key = jax.random.key(seed)
    ks = jax.random.split(key, 32)

    def nrm(k, shape, s):
        return jax.random.normal(k, shape, jnp.float32) * s

    return {
        'x_prompt': nrm(ks[0], (BATCH, SEQ, D_MODEL), 1.0),
        'x_sample': nrm(ks[1], (DEC_BATCH, DEC_SEQ, D_MODEL), 1.0),
        'state_pool': nrm(ks[2], (DEPTH, DEC_BATCH, POOL_BUF, POOL_WIDTH), 1.0),
        'state_hgrn': nrm(ks[3], (DEPTH, DEC_BATCH, HG_HEADS, HG_KDIM, HG_VDIM), 0.5),
        'state_ffn_conv': nrm(ks[4], (DEPTH, DEC_BATCH, CONV_W - 1, D_FF), 1.0),
        'cache_mem_k': nrm(ks[5], (DEPTH, DEC_BATCH, MEM_LEN, MEM_HEADS, MEM_HDIM), 1.0),
        'cache_mem_v': nrm(ks[6], (DEPTH, DEC_BATCH, MEM_LEN, MEM_HEADS, MEM_HDIM), 1.0),
        'mem_prompt': nrm(ks[7], (BATCH, MEM_LEN, D_MODEL), 1.0),
        'lb_logits': nrm(ks[8], (DEPTH + 1, HG_WIDTH), 0.1),
        'w_in': nrm(ks[9], (DEPTH, D_MODEL, IN_COLS), D_MODEL ** -0.5),
        'w_pool_grp': nrm(ks[10], (DEPTH, POOL_GROUPS, POOL_GDIM, POOL_GDIM), POOL_GDIM ** -0.5),
        'pool_scale': 1.0 + nrm(ks[11], (DEPTH, POOL_WIDTH), 0.02),
        'hg_norm_g': 1.0 + nrm(ks[12], (DEPTH, HG_VWIDTH), 0.02),
        'w_mem_k': nrm(ks[13], (DEPTH, D_MODEL, MEM_WIDTH), D_MODEL ** -0.5),
        'w_mem_v': nrm(ks[14], (DEPTH, D_MODEL, MEM_WIDTH), D_MODEL ** -0.5),
        'w_br_pool': nrm(ks[15], (DEPTH, POOL_WIDTH, D_MODEL), POOL_WIDTH ** -0.5),
        'w_br_hg': nrm(ks[16], (DEPTH, HG_VWIDTH, D_MODEL), HG_VWIDTH ** -0.5),
        'w_br_mem': nrm(ks[17], (DEPTH, MEM_WIDTH, D_MODEL), MEM_WIDTH ** -0.5),
        'w_out': nrm(ks[18], (DEPTH, D_MODEL, D_MODEL), DN_BETA * D_MODEL ** -0.5),
        'ln1_g': 1.0 + nrm(ks[19], (DEPTH, D_MODEL), 0.02),
        'ln1_b': nrm(ks[20], (DEPTH, D_MODEL), 0.02),
        'w_gate': nrm(ks[21], (DEPTH, D_MODEL, D_FF), D_MODEL ** -0.5),
        'w_up': nrm(ks[22], (DEPTH, D_MODEL, D_FF), D_MODEL ** -0.5),
        'conv_w': nrm(ks[23], (DEPTH, CONV_W, D_FF), 0.5),
        'conv_b': nrm(ks[24], (DEPTH, D_FF), 0.02),
        'w_down': nrm(ks[25], (DEPTH, D_FF, D_MODEL), DN_BETA * D_FF ** -0.5),
        'ln2_g': 1.0 + nrm(ks[26], (DEPTH, D_MODEL), 0.02),
        'ln2_b': nrm(ks[27], (DEPTH, D_MODEL), 0.02),
    }


def reference(x_prompt, x_sample, state_pool, state_hgrn, state_ffn_conv, cache_mem_k, cache_mem_v, mem_prompt,
              lb_logits, w_in, w_pool_grp, pool_scale, hg_norm_g, w_mem_k, w_mem_v, w_br_pool, w_br_hg, w_br_mem,
              w_out, ln1_g, ln1_b, w_gate, w_up, conv_w, conv_b, w_down, ln2_g, ln2_b):
    lower_bounds = jnp.cumsum(jax.nn.softmax(lb_logits.astype(jnp.float32), axis=0), axis=0)
    dt = x_prompt.dtype
    n_prompt = x_prompt.shape[0]
    n_mem = mem_prompt.shape[1]
    hp, hs = x_prompt, x_sample
    pool_p, hgrn_p, conv_p, memk_p, memv_p = [], [], [], [], []
    pool_s, hgrn_s, conv_s = [], [], []
    for l in range(DEPTH):
        p = {'w_in': w_in[l], 'w_pool_grp': w_pool_grp[l], 'pool_scale': pool_scale[l], 'hg_norm_g': hg_norm_g[l],
             'w_br_pool': w_br_pool[l], 'w_br_hg': w_br_hg[l], 'w_br_mem': w_br_mem[l], 'w_out': w_out[l],
             'ln1_g': ln1_g[l], 'ln1_b': ln1_b[l], 'w_gate': w_gate[l], 'w_up': w_up[l], 'conv_w': conv_w[l],
             'conv_b': conv_b[l], 'w_down': w_down[l], 'ln2_g': ln2_g[l], 'ln2_b': ln2_b[l]}
        lb = lower_bounds[l]
        mk = (mem_prompt @ w_mem_k[l]).reshape(n_prompt, n_mem, MEM_HEADS, MEM_HDIM)
        mv = (mem_prompt @ w_mem_v[l]).reshape(n_prompt, n_mem, MEM_HEADS, MEM_HDIM)
        hp, np_pool, np_hgrn, np_conv = decoder_layer(
            hp, jnp.zeros((n_prompt, POOL_BUF, POOL_WIDTH), dt),
            jnp.zeros((n_prompt, HG_HEADS, HG_KDIM, HG_VDIM), dt),
            jnp.zeros((n_prompt, CONV_W - 1, D_FF), dt), mk, mv, 0, lb, p)
        pool_p.append(np_pool)
        hgrn_p.append(np_hgrn)
        conv_p.append(np_conv)
        memk_p.append(mk)
        memv_p.append(mv)
        hs, ns_pool, ns_hgrn, ns_conv = decoder_layer(
            hs, state_pool[l], state_hgrn[l], state_ffn_conv[l], cache_mem_k[l], cache_mem_v[l], PAST_LEN, lb, p)
        pool_s.append(ns_pool)
        hgrn_s.append(ns_hgrn)
        conv_s.append(ns_conv)
    return (hp, hs, jnp.stack(pool_p), jnp.stack(hgrn_p), jnp.stack(conv_p), jnp.stack(memk_p), jnp.stack(memv_p),
            jnp.stack(pool_s), jnp.stack(hgrn_s), jnp.stack(conv_s))
```

```python
import numpy as np
from contextlib import ExitStack
import concourse.bass as bass
import concourse.mybir as mybir
from concourse.bass_utils import run_bass_kernel_spmd

F32 = mybir.dt.float32
BF16 = mybir.dt.bfloat16
ALU = mybir.AluOpType
AF = mybir.ActivationFunctionType
AX = mybir.AxisListType

NCORES = 8
D = 1024
SEQ = 2048
NS = 16
ST = 4
NTOK = SEQ + NS * ST
DFF = 2816
NFF = DFF // 128
ALPHA = float(2.0 ** 0.25)
LN_EPS = 1e-5
RMS_EPS = 1e-6
TA = 256
TB = 512


class Buf:
    def __init__(self, k, t, name):
        self.k = k
        self.t = t
        self.name = name
        self.w = None
        self.r = []
        self.dsem = None
        self.dcnt = 0
        self.excl = False

    def __getitem__(self, idx):
        return self.t[idx]


class Eng:
    def __init__(self, name, eng, sem):
        self.name = name
        self.eng = eng
        self.sem = sem
        self.cnt = 0
        self.waited = {}


class RR:
    def __init__(self):
        import threading
        self.th = threading
        self.cv = threading.Condition()
        self.turn = 0
        self.alive = []
        self.idx = {}

    def _advance(self):
        n = len(self.alive)
        for d in range(1, n + 1):
            j = (self.turn + d) % n
            if self.alive[j]:
                self.turn = j
                return
        self.turn = -1

    def switch(self):
        i = self.idx.get(self.th.get_ident())
        if i is None:
            return
        self.cnt[i] += 1
        if self.cnt[i] % self.wts[i]:
            return
        with self.cv:
            self._advance()
            self.cv.notify_all()
            while self.turn != i:
                self.cv.wait()

    def run(self, fns, wts=None):
        errs = []
        self.wts = list(wts) if wts else [1] * len(fns)
        self.cnt = [0] * len(fns)
        self.alive = [True] * len(fns)
        self.turn = 0
        self.idx = {}

        def worker(i, fn):
            self.idx[self.th.get_ident()] = i
            with self.cv:
                while self.turn != i:
                    self.cv.wait()
            try:
                fn()
            except BaseException as ex:
                errs.append(ex)
            finally:
                with self.cv:
                    self.alive[i] = False
                    if self.turn == i:
                        self._advance()
                    self.cv.notify_all()
        ths = [self.th.Thread(target=worker, args=(i, f)) for i, f in enumerate(fns)]
        for t in ths:
            t.start()
        for t in ths:
            t.join()
        self.idx = {}
        if errs:
            raise errs[0]


class K:
    def __init__(self, nc, es):
        self.nc = nc
        self.es = es
        self.E = {}
        for name, eng in (("pe", nc.tensor), ("act", nc.scalar), ("dve", nc.vector),
                          ("pool", nc.gpsimd), ("sp", nc.sync)):
            self.E[name] = Eng(name, eng, es.enter_context(nc.semaphore("sem_" + name)))
        self.dma_bufs = []
        self.nsem = 5
        self.psn = 0
        self.stopped = False
        self.rr = RR()
        import threading
        self.tls = threading.local()

    def sb(self, es, name, shape, dt):
        return Buf(self, es.enter_context(self.nc.sbuf_tensor(name, list(shape), dt)), name)

    def alias(self, b, name):
        return Buf(self, b.t, name)

    def dsem_of(self, b):
        if b.dsem is None:
            b.dsem = self.es.enter_context(self.nc.semaphore("ds_" + b.name))
            self.nsem += 1
            self.dma_bufs.append(b)
        return b.dsem

    def _wait(self, e, tok):
        if tok is None:
            return
        sem, val = tok
        key = id(sem)
        if e.waited.get(key, 0) >= val:
            return
        import os as _os
        if sem is e.sem and _os.environ.get("NOSELF"):
            return
        if sem is e.sem and e.name in ("pe", "act"):
            return
        e.eng.wait_ge(sem, val)
        e.waited[key] = val

    def _deps(self, e, reads, writes):
        toks = []
        for b in reads:
            toks.append(b.w)
            if b.excl:
                toks += [t for t in b.r if t[0] is not e.sem]
        for b in writes:
            toks.append(b.w)
            toks += b.r
        best = {}
        for t in toks:
            if t is not None and (id(t[0]) not in best or best[id(t[0])][1] < t[1]):
                best[id(t[0])] = t
        for t in best.values():
            self._wait(e, t)

    def op(self, en, reads, writes, fn):
        if self.stopped:
            return None
        e = self.E[en]
        self._deps(e, reads, writes)
        ins = fn(e.eng)
        e.cnt += 1
        ins.then_inc(e.sem, 1)
        tok = (e.sem, e.cnt)
        for b in reads:
            b.r.append(tok)
        for b in writes:
            b.w = tok
            b.r = []
        self.rr.switch()
        return ins

    def dma(self, qn, out, in_, reads, writes, n=1, **kw):
        if self.stopped:
            return
        e = self.E[qn]
        self._deps(e, reads, writes)
        tb = writes[0] if writes else reads[0]
        sem = self.dsem_of(tb)
        outs = out if isinstance(out, list) else [out]
        ins = in_ if isinstance(in_, list) else [in_]
        for o, i in zip(outs, ins):
            e.eng.dma_start(out=o, in_=i, **kw).then_inc(sem, 16)
            tb.dcnt += 16
        tok = (sem, tb.dcnt)
        for b in reads:
            b.r.append(tok)
        for b in writes:
            b.w = tok
            b.r = []
        self.rr.switch()

    def barrier(self, bufs=()):
        if self.stopped:
            return
        toks = [(e.sem, e.cnt) for e in self.E.values() if e.cnt > 0]
        for b in self.dma_bufs:
            if b.dcnt:
                toks.append((b.dsem, b.dcnt))
        for e in self.E.values():
            for t in toks:
                if t[0] is e.sem:
                    continue
                self._wait(e, t)


def _build(dbg=False):
    nc = bass.Bass("TRN2", target_bir_lowering=False)
    outer = ExitStack()
    with outer:
        _emit(nc, outer)
    return nc


class _Stop(Exception):
    pass


def _emit(nc, outer):
    import os
    k = K(nc, outer)
    kstop = int(os.environ.get("KSTOP", "0"))
    stage = [0]

    def ckpt(name):
        stage[0] += 1
        if kstop and stage[0] >= kstop and not k.stopped:
            print("STOP at stage", stage[0], name)
            k.stopped = True
    _emit2(nc, outer, k, ckpt)
    k.stopped = False
    k.barrier()
    print("instr counts:", {n: e.cnt for n, e in k.E.items()}, "sems:", k.nsem)


def _emit2(nc, outer, k, ckpt):

    def din(name, shape):
        return nc.dram_tensor(name, list(shape), F32, kind="ExternalInput").ap()

    def dout(name, shape):
        return nc.dram_tensor(name, list(shape), F32, kind="ExternalOutput").ap()

    xp = din("xp", [SEQ, D])
    xs = din("xs", [NS * ST, D])
    spool = din("spool", [NS, 15, 256])
    shg = din("shg", [NS, 4, 128, 128])
    sconv = din("sconv", [NS * 2, DFF])
    cmk = din("cmk", [NS, 256, 256])
    cmv = din("cmv", [NS, 256, 256])
    memp = din("memp", [256, D])
    lbl = din("lbl", [2, 512])
    w_in = din("w_in", [D, 5632])
    w_grp = din("w_grp", [4, 64, 64])
    pool_scale = din("pool_scale", [256])
    hg_g = din("hg_g", [512])
    w_mem_k = din("w_mem_k", [D, 256])
    w_mem_v = din("w_mem_v", [D, 256])
    w_br_pool = din("w_br_pool", [256, D])
    w_br_hg = din("w_br_hg", [512, D])
    w_br_mem = din("w_br_mem", [256, D])
    w_out = din("w_out", [D, D])
    ln1_g = din("ln1_g", [D])
    ln1_b = din("ln1_b", [D])
    w_gate = din("w_gate", [D, DFF])
    w_up = din("w_up", [D, DFF])
    conv_w = din("conv_w", [3, DFF])
    conv_b = din("conv_b", [DFF])
    w_down = din("w_down", [DFF, D])
    ln2_g = din("ln2_g", [D])
    ln2_b = din("ln2_b", [D])
    c_ident = din("c_ident", [128, 128])
    c_ones = din("c_ones", [128, 128])
    c_rmask_p = din("c_rmask_p", [128, TA])
    c_rmask_s = din("c_rmask_s", [128, 64])
    c_cmask_p = din("c_cmask_p", [128, 128])
    c_cmask_s = din("c_cmask_s", [64, 64])
    c_invcnt = din("c_invcnt", [128, 2, 16])
    c_seqm_tm = din("c_seqm_tm", [64, 16])
    c_seqm_fm = din("c_seqm_fm", [128, 16, 64])

    yp = dout("yp", [SEQ, D])
    ys = dout("ys", [NS * ST, D])
    npool_p = dout("npool_p", [15, 256])
    nhg_p = dout("nhg_p", [4, 128, 128])
    nconv_p = dout("nconv_p", [2, DFF])
    nmk_p = dout("nmk_p", [256, 256])
    nmv_p = dout("nmv_p", [256, 256])
    npool_s = dout("npool_s", [NS, 15, 256])
    nhg_s = dout("nhg_s", [NS, 4, 128, 128])
    nconv_s = dout("nconv_s", [NS, 2, DFF])
    h_scr_t = nc.dram_tensor("h_scr", [NTOK, D], F32)
    h_scr = h_scr_t.ap()

    es0 = outer
    ident = k.sb(es0, "ident", [128, 128], BF16)
    ident32 = k.sb(es0, "ident32", [128, 128], F32)
    ones32 = k.sb(es0, "ones32", [128, 128], F32)
    epsln = k.sb(es0, "epsln", [128, 1], F32)
    epsrms = k.sb(es0, "epsrms", [128, 1], F32)
    hscr = Buf(k, h_scr_t, "hscr")
    esA = ExitStack()
    esA.__enter__()
    xT_scr_t = nc.dram_tensor("xT_scr", [128, 8, NTOK], BF16)
    yc_scr_t = nc.dram_tensor("yc_scr", [128, 8, NTOK], BF16)
    xT_scr, yc_scr = xT_scr_t.ap(), yc_scr_t.ap()
    xscr = Buf(k, xT_scr_t, "xscr")
    yscr = Buf(k, yc_scr_t, "yscr")

    PS = []
    PSB = []
    psbn = [0]

    def set_psum(es, nf, nb, tag):
        PS.clear()
        for i in range(8):
            PS.append(Buf(k, es.enter_context(nc.psum_tensor("ps%s%d" % (tag, i), [128, 512], F32)), "ps%s%d" % (tag, i)))
            PS[-1].excl = True
        ldT_cache.clear()

    class Chain:
        def __init__(self, banks):
            self.banks = banks
            self.n = 0

    def use_chain(banks):
        k.tls.chain = Chain(banks)

    def cur_banks():
        ch = getattr(k.tls, "chain", None)
        if ch is None:
            ch = k.tls.chain = Chain(list(PS))
        return ch

    def nps():
        ch = cur_banks()
        b = ch.banks[ch.n % len(ch.banks)]
        ch.n += 1
        return b

    npsb = nps

    def bfv(ps):
        return ps.t[:].bitcast(BF16)

    ldT_cache = {}

    k.dma("pool", ident[:], c_ident, [], [ident])
    k.dma("sp", ident32[:], c_ident, [], [ident32])
    k.dma("sp", ones32[:], c_ones, [], [ones32])
    k.op("dve", [], [epsln], lambda e: e.memset(epsln[:], LN_EPS))
    k.op("dve", [], [epsrms], lambda e: e.memset(epsrms[:], RMS_EPS))

    ldT_cache = {}

    ckpt("consts")
    import os as _os3
    KDBG = bool(_os3.environ.get("KDBG"))

    def tap(name, buf, ap):
        if not KDBG:
            return
        d = nc.dram_tensor("dbg_" + name, list(ap.shape), F32, kind="ExternalOutput").ap()
        k.dma("pool", d, ap, [buf], [])

    def load_T_dma(es, src_rows, ntok, tag, qn="pool"):
        if tag not in ldT_cache:
            ldT_cache[tag] = k.sb(es, "ldT_" + tag, [128, D], BF16)
        tb = ldT_cache[tag]
        k.dma(qn, tb[0:ntok, :], src_rows, [], [tb])

    def load_T(es, src_rows, dst, col0, ntok, tag, qn="pool", dma=True):
        if dma:
            load_T_dma(es, src_rows, ntok, tag, qn)
        tb = ldT_cache[tag]
        ps = npsb()
        pv = bfv(ps)
        for c in range(8):
            k.op("pe", [tb, ident], [ps], lambda e, c=c: e.transpose(
                out=pv[:, c * 128:c * 128 + ntok], in_=tb[0:ntok, c * 128:(c + 1) * 128],
                identity=ident[0:ntok, 0:ntok]))
        k.op("act", [ps], [dst], lambda e: e.copy(
            out=dst[:, :, col0:col0 + ntok],
            in_=pv.rearrange("p (c t) -> p c t", c=8)[:, :, 0:ntok]))

    with ExitStack() as es:
        set_psum(es, 6, 2, "a")
        use_chain(list(PS))
        w_inA = k.sb(es, "w_inA", [128, 8, 2560], BF16)
        wmk = k.sb(es, "wmk", [128, 8, 256], BF16)
        wmv = k.sb(es, "wmv", [128, 8, 256], BF16)
        wgrp = k.sb(es, "wgrp", [128, 2, 128], BF16)
        lbt = k.sb(es, "lbt", [128, 2, 4], F32)
        lbv = k.sb(es, "lbv", [128, 4], F32)
        omlv = k.sb(es, "omlv", [128, 4], F32)
        nomlv = k.sb(es, "nomlv", [128, 4], F32)
        pscale = k.sb(es, "pscale", [128, 2], F32)
        hgam = k.sb(es, "hgam", [128, 4], F32)
        rmask_p = k.sb(es, "rmask_p", [128, TA], F32)
        rmask_s = k.sb(es, "rmask_s", [128, 64], F32)
        cmask_p = k.sb(es, "cmask_p", [128, 128], F32)
        cmask_s = k.sb(es, "cmask_s", [64, 64], F32)
        invcnt = k.sb(es, "invcnt", [128, 2, 16], F32)
        seqm_tm = k.sb(es, "seqm_tm", [64, 16], F32)
        seqm_fm = k.sb(es, "seqm_fm", [128, 16, 64], BF16)

        w_inA_g = [k.alias(w_inA, "w_inA_g%d" % g) for g in range(5)]
        k.dma("pool", wmk[:], w_mem_k.rearrange("(c p) n -> p c n", p=128), [], [wmk])
        k.dma("pool", wmv[:], w_mem_v.rearrange("(c p) n -> p c n", p=128), [], [wmv])
        k.op("dve", [], [wgrp], lambda e: e.memset(wgrp[:], 0.0))
        k.dma("pool", [wgrp[(g % 2) * 64:(g % 2) * 64 + 64, g // 2, (g % 2) * 64:(g % 2) * 64 + 64] for g in range(4)],
              [w_grp[g] for g in range(4)], [], [wgrp])
        with nc.allow_non_contiguous_dma(reason="tiny per-partition scalar tables"):
            k.dma("sp", lbt[:], lbl.rearrange("r (h p) -> p r h", p=128), [], [lbt])
            k.dma("sp", pscale[:], pool_scale.rearrange("(c p) -> p c", p=128), [], [pscale])
            k.dma("sp", hgam[:], hg_g.rearrange("(c p) -> p c", p=128), [], [hgam])
        k.dma("sp", rmask_p[:], c_rmask_p, [], [rmask_p])
        k.dma("sp", rmask_s[:], c_rmask_s, [], [rmask_s])
        k.dma("sp", cmask_p[:], c_cmask_p, [], [cmask_p])
        k.dma("sp", cmask_s[:], c_cmask_s, [], [cmask_s])
        k.dma("sp", invcnt[:], c_invcnt, [], [invcnt])
        k.dma("sp", seqm_tm[:], c_seqm_tm, [], [seqm_tm])
        k.dma("pool", seqm_fm[:], c_seqm_fm, [], [seqm_fm])
        k.op("dve", [lbt], [lbv], lambda e: e.tensor_tensor(out=lbv[:], in0=lbt[:, 0, :], in1=lbt[:, 1, :], op=ALU.subtract))
        k.op("act", [lbv], [lbv], lambda e: e.activation(out=lbv[:], in_=lbv[:], func=AF.Sigmoid))
        k.op("dve", [lbv], [omlv], lambda e: e.tensor_scalar(out=omlv[:], in0=lbv[:], scalar1=-1.0, scalar2=1.0, op0=ALU.mult, op1=ALU.add))
        k.op("dve", [omlv], [nomlv], lambda e: e.tensor_scalar(out=nomlv[:], in0=omlv[:], scalar1=-1.0, scalar2=None, op0=ALU.mult))

        ckpt("a1 weights")
        memT = k.sb(es, "memT", [128, 8, 256], BF16)
        ktp = k.sb(es, "ktp", [128, 4, 256], BF16)
        vpd = k.sb(es, "vpd", [128, 2, 4, 128], BF16)
        mko = k.sb(es, "mko", [128, 2, 512], F32)
        import os as _os
        for mc in range(int(_os.environ.get("LTN", "2"))):
            load_T(es, memp[mc * 128:(mc + 1) * 128, :], memT, mc * 128, 128, "mem%d" % mc)
        ckpt("memT")
        k.op("dve", [], [ktp], lambda e: e.memset(ktp[:], 0.0))
        k.op("dve", [], [vpd], lambda e: e.memset(vpd[:], 0.0))
        STG4 = int(_os.environ.get("STG4", "9"))
        for mc in range(2):
            ps = nps()
            if STG4 < 2:
                continue
            for c in range(8):
                k.op("pe", [memT, wmk], [ps], lambda e, c=c: e.matmul(
                    ps[:, 0:256], lhsT=memT[:, c, mc * 128:(mc + 1) * 128], rhs=wmk[:, c, :], start=(c == 0), stop=(c == 7)))
            for c in range(8):
                k.op("pe", [memT, wmv], [ps], lambda e, c=c: e.matmul(
                    ps[:, 256:512], lhsT=memT[:, c, mc * 128:(mc + 1) * 128], rhs=wmv[:, c, :], start=(c == 0), stop=(c == 7)))
            if STG4 < 3:
                continue
            k.op("act", [ps], [mko], lambda e: e.copy(out=mko[:, mc, :], in_=ps[:, :]))
            if STG4 < 4:
                continue
            for h in range(4):
                k.op("dve", [ps] + ([mko] if _os.environ.get("SERIAL") else []), [vpd], lambda e, h=h: e.tensor_copy(
                    out=vpd[:, mc, h, (h % 2) * 64:(h % 2) * 64 + 64], in_=ps[:, 256 + h * 64:256 + (h + 1) * 64]))
        ckpt("mem TM")
        k.dma("sp", nmk_p.rearrange("(c p) n -> p c n", p=128), mko[:, :, 0:256], [mko], [])
        k.dma("sp", nmv_p.rearrange("(c p) n -> p c n", p=128), mko[:, :, 256:512], [mko], [])
        ckpt("mem out")
        for j in range(2):
            ps = nps()
            for c in range(8):
                k.op("pe", [memT, wmk], [ps], lambda e, c=c: e.matmul(
                    ps[:, 0:256], lhsT=wmk[:, c, j * 128:(j + 1) * 128], rhs=memT[:, c, :], start=(c == 0), stop=(c == 7)))
            for hh in range(2):
                h = 2 * j + hh
                k.op("dve", [ps], [ktp], lambda e, h=h, hh=hh: e.tensor_copy(
                    out=ktp[hh * 64:(hh + 1) * 64, h, :], in_=ps[hh * 64:(hh + 1) * 64, 0:256]))

        ckpt("memkv")
        ub = [k.sb(es, "ub%d" % i, [128, 2, 15 + TA], F32) for i in range(2)]
        sw = [k.sb(es, "sw%d" % i, [128, 2, 15 + TA], F32) for i in range(2)]
        dif = k.sb(es, "dif", [128, 2, TA], BF16)
        class NS_:
            pass
        BB = [NS_(), NS_()]
        for i_, b_ in enumerate(BB):
            b_.xt = k.sb(es, "xTs%d" % i_, [128, 8, TA], BF16)
            b_.yc = k.sb(es, "ycs%d" % i_, [128, 8, TA], BF16)
            b_.yc_a = k.alias(b_.yc, "ycs%d_a" % i_)
            b_.yc_b = k.alias(b_.yc, "ycs%d_b" % i_)
            b_.yc_c = k.alias(b_.yc, "ycs%d_c" % i_)
            b_.sgm = k.sb(es, "sgm%d" % i_, [128, 4, TA], F32)
            b_.qs = k.sb(es, "qs%d" % i_, [128, 4, TA], F32)
            b_.gs = k.sb(es, "gs%d" % i_, [128, 4, TA], F32)
            b_.qcT = k.sb(es, "qcT%d" % i_, [128, 2, TA], BF16)
            b_.sgm_h = [k.alias(b_.sgm, "sgm%d_h%d" % (i_, h_)) for h_ in range(4)]
            b_.qs_h = [k.alias(b_.qs, "qs%d_h%d" % (i_, h_)) for h_ in range(4)]
            b_.gs_h = [k.alias(b_.gs, "gs%d_h%d" % (i_, h_)) for h_ in range(4)]
            b_.vtm = [k.sb(es, "vtm%d_%d" % (i_, t_), [128, 512], BF16) for t_ in range(2)]
        NPF = 8
        sst = [k.sb(es, "sst%d" % i_, [128, 4, 128], F32) for i_ in range(NPF)]
        kst = [k.sb(es, "kst%d" % i_, [128, 2, 256], BF16) for i_ in range(NPF)]
        vst = [k.sb(es, "vst%d" % i_, [128, 2, 256], BF16) for i_ in range(NPF)]
        khm2 = [k.sb(es, "khm%d" % i_, [64, 512], BF16) for i_ in range(2)]
        qms2 = [k.sb(es, "qms%d" % i_, [128, 2, 64], BF16) for i_ in range(2)]
        eb = k.sb(es, "eb", [128, 4, TA], F32)
        eb_h = [k.alias(eb, "eb_h%d" % h_) for h_ in range(4)]
        lf = [k.sb(es, "lf%d" % i, [128, TA], F32) for i in range(2)]
        kk = [k.sb(es, "kk%d" % i, [128, TA], F32) for i in range(2)]
        enb = [k.sb(es, "enb%d" % i, [128, TA], F32) for i in range(2)]
        qtil = k.sb(es, "qtil", [128, 4, TA], BF16)
        ktil = k.sb(es, "ktil", [128, 4, TA], BF16)
        khat = k.sb(es, "khat", [128, 4, TA], BF16)
        khtm = [k.sb(es, "khtm%d" % i, [128, 512], BF16) for i in range(2)]
        scm = [k.sb(es, "scm%d" % i, [128, 4, 128], BF16) for i in range(2)]
        oT = k.sb(es, "oT", [128, 4, TA], F32)
        osq = k.sb(es, "osq", [128, 4, TA], F32)
        rstd = osq
        S32 = k.sb(es, "S32", [128, 4, 128], F32)
        Sbfs = [k.sb(es, "Sbf%d" % i, [128, 4, 128], BF16) for i in range(2)]
        sbi = [0]
        p32 = k.sb(es, "p32", [128, 4, 256], F32)
        pn = k.sb(es, "pn", [128, 4, 256], BF16)
        pT = k.sb(es, "pT", [128, 8, 128], BF16)
        stat = k.sb(es, "stat", [128, 16], F32)
        utm = k.sb(es, "utm", [128, 256], F32)

        def a1_supertile(st_i, tok0, T, sample):
            TT = 64 if sample else 128
            ntt = T // TT
            C = ST if sample else 64
            B = BB[st_i % 2]
            u = ub[st_i % 2]
            up = ub[(st_i + 1) % 2]

            def proj(cc):
                ps = nps()
                for c in range(8):
                    k.op("pe", [B.xt, w_inA_g[cc // 4]], [ps], lambda e, c=c: e.matmul(
                        ps[:, 0:T], lhsT=w_inA[:, c, cc * 128:(cc + 1) * 128], rhs=B.xt[:, c, 0:T], start=(c == 0), stop=(c == 7)))
                return ps

            def c_X():
                cur[0] = B
                src = xs if sample else xp
                for tt in range(ntt):
                    r0 = (tok0 - (SEQ if sample else 0)) + tt * TT
                    load_T(es, src[r0:r0 + TT, :], B.xt, tt * TT, TT, "x%d" % (tt % 2), dma=(st_i != 0))
                k.dma("sp", xT_scr[:, :, tok0:tok0 + T], B.xt[:, :, 0:T], [B.xt], [xscr])
                yield
                if not sample:
                    if tok0 == 0:
                        k.op("pool", [], [u], lambda e: e.memset(u[:, :, 0:15], 0.0))
                    else:
                        k.op("pool", [up], [u], lambda e: e.tensor_copy(out=u[:, :, 0:15], in_=up[:, :, TA:TA + 15]))
                    for c2 in range(2):
                        ps = proj(c2)
                        k.op("act", [ps], [u], lambda e, c2=c2: e.copy(out=u[:, c2, 15:15 + T], in_=ps[:, 0:T]))
                    yield
                for h in range(4):
                    psq = proj(2 + h)
                    k.op("act", [psq], [B.qs_h[h]], lambda e, h=h: e.activation(out=B.qs[:, h, 0:T], in_=psq[:, 0:T], func=AF.Sigmoid))
                    k.op("dve", [psq, B.qs_h[h]], [B.qs_h[h]], lambda e, h=h: e.tensor_tensor(out=B.qs[:, h, 0:T], in0=B.qs[:, h, 0:T], in1=psq[:, 0:T], op=ALU.mult))
                    psf = proj(6 + h)
                    k.op("act", [psf], [B.sgm_h[h]], lambda e, h=h: e.activation(out=B.sgm[:, h, 0:T], in_=psf[:, 0:T], func=AF.Sigmoid))
                    psg = proj(14 + h)
                    k.op("act", [psg], [B.gs_h[h]], lambda e, h=h: e.activation(out=B.gs[:, h, 0:T], in_=psg[:, 0:T], func=AF.Sigmoid))
                    k.op("dve", [psg, B.gs_h[h]], [B.gs_h[h]], lambda e, h=h: e.tensor_tensor(out=B.gs[:, h, 0:T], in0=B.gs[:, h, 0:T], in1=psg[:, 0:T], op=ALU.mult))
                yield
                for tt in range(ntt):
                    ps = nps()
                    for c in range(8):
                        k.op("pe", [B.xt, w_inA_g[2], w_inA_g[3]], [ps], lambda e, c=c, tt=tt: e.matmul(
                            ps[0:TT, :], lhsT=B.xt[:, c, tt * TT:(tt + 1) * TT], rhs=w_inA[:, c, 1280:1792], start=(c == 0), stop=(c == 7)))
                    k.op("act", [ps], [B.vtm[tt]], lambda e, tt=tt: e.copy(out=B.vtm[tt][0:TT, :], in_=ps[0:TT, :]))
                yield
                for j in range(2):
                    ps = proj(18 + j)
                    k.op("dve", [ps], [B.qcT], lambda e, j=j: e.tensor_copy(out=B.qcT[:, j, 0:T], in_=ps[:, 0:T]))
                yield
                return

            sgm, qs, gs, qcT, vtm = B.sgm, B.qs, B.gs, B.qcT, B.vtm
            def c_pool_attn():
                cur[0] = B
                if not sample:
                    L = 15 + T
                    s2, s4 = sw[0], sw[1]
                    k.op("dve", [u], [s2], lambda e: e.tensor_tensor(out=s2[:, :, 0:L - 1], in0=u[:, :, 0:L - 1], in1=u[:, :, 1:L], op=ALU.add))
                    k.op("dve", [s2], [s4], lambda e: e.tensor_tensor(out=s4[:, :, 0:L - 3], in0=s2[:, :, 0:L - 3], in1=s2[:, :, 2:L - 1], op=ALU.add))
                    k.op("dve", [s2, u], [dif], lambda e: e.scalar_tensor_tensor(
                        out=dif[0:64, 0, 0:T], in0=s2[0:64, 0, 14:14 + T], scalar=0.5, in1=u[0:64, 0, 15:15 + T], op0=ALU.mult, op1=ALU.subtract))
                    k.op("dve", [s4, u], [dif], lambda e: e.scalar_tensor_tensor(
                        out=dif[64:128, 0, 0:T], in0=s4[64:128, 0, 12:12 + T], scalar=0.25, in1=u[64:128, 0, 15:15 + T], op0=ALU.mult, op1=ALU.subtract))
                    k.op("dve", [s4, s2], [s2], lambda e: e.tensor_tensor(out=s2[:, 1, 0:L - 7], in0=s4[:, 1, 0:L - 7], in1=s4[:, 1, 4:L - 3], op=ALU.add))
                    k.op("dve", [s2, u], [dif], lambda e: e.scalar_tensor_tensor(
                        out=dif[0:64, 1, 0:T], in0=s2[0:64, 1, 8:8 + T], scalar=0.125, in1=u[0:64, 1, 15:15 + T], op0=ALU.mult, op1=ALU.subtract))
                    k.op("dve", [s2, s4], [s4], lambda e: e.tensor_tensor(out=s4[:, 1, 0:L - 15], in0=s2[:, 1, 0:L - 15], in1=s2[:, 1, 8:L - 7], op=ALU.add))
                    k.op("dve", [s4, u], [dif], lambda e: e.scalar_tensor_tensor(
                        out=dif[64:128, 1, 0:T], in0=s4[64:128, 1, 0:T], scalar=0.0625, in1=u[64:128, 1, 15:15 + T], op0=ALU.mult, op1=ALU.subtract))
                    if tok0 == 0:
                        for (p0, c2, sbuf, off) in ((0, 0, s2, 14), (64, 0, s4, 12), (0, 1, s2, 8), (64, 1, s4, 0)):
                            k.op("dve", [sbuf, invcnt], [sbuf], lambda e, p0=p0, c2=c2, sbuf=sbuf, off=off: e.tensor_tensor(
                                out=sbuf[p0:p0 + 64, c2, off:off + 16], in0=sbuf[p0:p0 + 64, c2, off:off + 16],
                                in1=invcnt[p0:p0 + 64, c2, :], op=ALU.mult))
                            k.op("dve", [sbuf, u], [dif], lambda e, p0=p0, c2=c2, sbuf=sbuf, off=off: e.tensor_tensor(
                                out=dif[p0:p0 + 64, c2, 0:16], in0=sbuf[p0:p0 + 64, c2, off:off + 16],
                                in1=u[p0:p0 + 64, c2, 15:31], op=ALU.subtract))
                    for c2 in range(2):
                        ps = nps()
                        k.op("pe", [dif, wgrp], [ps], lambda e, c2=c2: e.matmul(ps[:, 0:T], lhsT=wgrp[:, c2, :], rhs=dif[:, c2, 0:T], start=True, stop=True))
                        k.op("act", [ps, pscale], [B.yc_a], lambda e, c2=c2: e.activation(
                            out=B.yc[:, c2, 0:T], in_=ps[:, 0:T], func=AF.Copy, scale=pscale[:, c2:c2 + 1]))
                    if tok0 + T == SEQ:
                        ps = nps()
                        for c in range(8):
                            k.op("pe", [B.xt, w_inA_g[0]], [ps], lambda e, c=c: e.matmul(
                                ps[0:15, 0:256], lhsT=B.xt[:, c, T - 15:T], rhs=w_inA[:, c, 0:256], start=(c == 0), stop=(c == 7)))
                        k.op("act", [ps], [utm], lambda e: e.copy(out=utm[0:15, :], in_=ps[0:15, 0:256]))
                        k.dma("sp", npool_p, utm[0:15, :], [utm], [])

                if sample:
                    sample_pool(proj)
                if not sample:
                    for tt in range(ntt):
                        tc0 = tt * TT
                        pss = [nps(), nps()]
                        for h in range(4):
                            k.op("pe", [qcT, ktp], [pss[h // 2]], lambda e, h=h: e.matmul(
                                pss[h // 2][0:TT, (h % 2) * 256:(h % 2) * 256 + 256], lhsT=qcT[:, h // 2, tc0:tc0 + TT], rhs=ktp[:, h, :],
                                start=True, stop=True))
                        for j in range(2):
                            k.op("dve", [pss[j]], [stat], lambda e, j=j: e.tensor_reduce(
                                out=stat[0:TT, 2 * j:2 * j + 2], in_=pss[j][0:TT, :].rearrange("p (h m) -> p h m", h=2), axis=AX.X, op=ALU.max))
                        k.op("dve", [stat], [stat], lambda e: e.tensor_scalar(
                            out=stat[0:TT, 4:8], in0=stat[0:TT, 0:4], scalar1=-0.125, scalar2=None, op0=ALU.mult))
                        for h in range(4):
                            k.op("act", [pss[h // 2], stat], [p32, stat], lambda e, h=h: e.activation(
                                out=p32[0:TT, h, :], in_=pss[h // 2][0:TT, (h % 2) * 256:(h % 2) * 256 + 256], func=AF.Exp,
                                scale=0.125, bias=stat[0:TT, 4 + h:5 + h], accum_out=stat[0:TT, 8 + h:9 + h]))
                        k.op("dve", [stat], [stat], lambda e: e.reciprocal(out=stat[0:TT, 12:16], in_=stat[0:TT, 8:12]))
                        k.op("dve", [p32, stat], [pn], lambda e: e.tensor_tensor(
                            out=pn[0:TT, :, :], in0=p32[0:TT, :, :], in1=stat[0:TT, 12:16].unsqueeze(2).to_broadcast([TT, 4, 256]), op=ALU.mult))
                        ps = npsb()
                        pv = bfv(ps)
                        for h in range(4):
                            for mc in range(2):
                                i = h * 2 + mc
                                k.op("pe", [pn, ident], [ps], lambda e, h=h, mc=mc, i=i: e.transpose(
                                    out=pv[:, i * 128:i * 128 + TT], in_=pn[0:TT, h, mc * 128:(mc + 1) * 128], identity=ident[0:TT, 0:TT]))
                        k.op("act", [ps], [pT], lambda e: e.copy(out=pT[:, :, 0:TT], in_=pv.rearrange("p (i t) -> p i t", i=8)[:, :, 0:TT]))
                        ps_y = nps()
                        for j in range(2):
                            n = 0
                            for hh in range(2):
                                h = 2 * j + hh
                                for mc in range(2):
                                    k.op("pe", [vpd, pT], [ps_y], lambda e, h=h, mc=mc, j=j, n=n: e.matmul(
                                        ps_y[:, j * 128:j * 128 + TT], lhsT=vpd[:, mc, h, :], rhs=pT[:, h * 2 + mc, 0:TT],
                                        start=(n == 0), stop=(n == 3)))
                                    n += 1
                        k.op("dve", [ps_y], [B.yc_c], lambda e: e.tensor_copy(
                            out=B.yc[:, 6:8, tc0:tc0 + TT], in_=ps_y[:, 0:256].rearrange("p (j t) -> p j t", j=2)[:, :, 0:TT]))
                        yield

                if sample:
                    sample_attn()
                yield
            def c_hgrn():
                cur[0] = B
                rmask = rmask_s if sample else rmask_p
                yield
                nch = T // C
                for h in range(4):
                    if h == 2:
                        yield
                    l, kq, en = lf[h % 2], kk[h % 2], enb[h % 2]
                    k.op("act", [B.sgm_h[h], omlv, lbv], [l], lambda e, h=h, l=l: e.activation(
                        out=l[:, 0:T], in_=sgm[:, h, 0:T], func=AF.Ln, scale=omlv[:, h:h + 1], bias=lbv[:, h:h + 1]))
                    k.op("dve", [B.sgm_h[h], omlv, nomlv], [kq], lambda e, h=h, kq=kq: e.tensor_scalar(
                        out=kq[:, 0:T], in0=sgm[:, h, 0:T], scalar1=nomlv[:, h:h + 1], scalar2=omlv[:, h:h + 1], op0=ALU.mult, op1=ALU.add))
                    k.op("dve", [l, rmask], [l], lambda e, l=l: e.tensor_tensor_scan(
                        out=l[:, 0:T], data0=rmask[:, 0:T], data1=l[:, 0:T], initial=0.0, op0=ALU.mult, op1=ALU.add))
                    k.op("act", [l], [eb_h[h]], lambda e, h=h, l=l: e.activation(out=eb[:, h, 0:T], in_=l[:, 0:T], func=AF.Exp))
                    k.op("act", [l], [en], lambda e, l=l, en=en: e.activation(out=en[:, 0:T], in_=l[:, 0:T], func=AF.Exp, scale=-1.0))
                    k.op("pool", [B.qs_h[h], eb_h[h]], [qtil], lambda e, h=h: e.tensor_tensor(out=qtil[:, h, 0:T], in0=qs[:, h, 0:T], in1=eb[:, h, 0:T], op=ALU.mult))
                    k.op("dve", [kq, en], [kq], lambda e, kq=kq, en=en: e.tensor_tensor(out=kq[:, 0:T], in0=kq[:, 0:T], in1=en[:, 0:T], op=ALU.mult))
                    k.op("act", [kq], [ktil], lambda e, h=h, kq=kq: e.copy(out=ktil[:, h, 0:T], in_=kq[:, 0:T]))
                    k.op("dve", [kq, eb_h[h]], [khat], lambda e, h=h, kq=kq: e.tensor_tensor(
                        out=khat[:, h, 0:T].rearrange("p (c s) -> p c s", s=C),
                        in0=kq[:, 0:T].rearrange("p (c s) -> p c s", s=C),
                        in1=eb[:, h, C - 1:T:C].unsqueeze(2).to_broadcast([128, nch, C]), op=ALU.mult))
                yield
                for tt in range(ntt):
                    ps = npsb()
                    pv = bfv(ps)
                    for h in range(4):
                        k.op("pe", [khat, ident], [ps], lambda e, h=h, tt=tt: e.transpose(
                            out=pv[0:TT, h * 128:(h + 1) * 128], in_=khat[:, h, tt * TT:(tt + 1) * TT], identity=ident[:, :]))
                    k.op("act", [ps], [khtm[tt]], lambda e, tt=tt: e.copy(out=khtm[tt][0:TT, :], in_=pv[0:TT, 0:512]))
                yield
                cmask = cmask_s if sample else cmask_p
                for tt in range(ntt):
                    if tt:
                        yield
                    tc0 = tt * TT
                    ps_sc = nps()
                    for h in range(4):
                        k.op("pe", [ktil, qtil], [ps_sc], lambda e, h=h: e.matmul(
                            ps_sc[0:TT, h * 128:h * 128 + TT], lhsT=ktil[:, h, tc0:tc0 + TT], rhs=qtil[:, h, tc0:tc0 + TT], start=True, stop=True))
                    sm = scm[tt % 2]
                    k.op("dve", [ps_sc, cmask], [sm], lambda e, sm=sm: e.tensor_tensor(
                        out=sm[0:TT, :, 0:TT], in0=ps_sc[0:TT, :].rearrange("p (h t) -> p h t", h=4)[:, :, 0:TT],
                        in1=cmask[0:TT, 0:TT].unsqueeze(1).to_broadcast([TT, 4, TT]), op=ALU.mult))
                    ps_o = nps()
                    if not sample:
                        first = (tok0 == 0 and tt == 0)
                        ps_us = [nps(), nps()]
                        for c in range(2):
                            for h in range(4):
                                k.op("pe", [khtm[tt], vtm[tt]], [ps_us[c]], lambda e, h=h, c=c: e.matmul(
                                    ps_us[c][:, h * 128:(h + 1) * 128], lhsT=khtm[tt][c * 64:(c + 1) * 64, h * 128:(h + 1) * 128],
                                    rhs=vtm[tt][c * 64:(c + 1) * 64, h * 128:(h + 1) * 128], start=True, stop=True))
                        for c in range(2):
                            has_inter = not (first and c == 0)
                            Scur = Sbfs[sbi[0] % 2]
                            Snxt = Sbfs[(sbi[0] + 1) % 2]
                            sbi[0] += 1
                            for h in range(4):
                                k.op("pe", [vtm[tt], sm], [ps_o], lambda e, h=h, c=c: e.matmul(
                                    ps_o[:, h * 128 + c * 64:h * 128 + (c + 1) * 64], lhsT=vtm[tt][0:TT, h * 128:(h + 1) * 128],
                                    rhs=sm[0:TT, h, c * 64:(c + 1) * 64], start=True, stop=(not has_inter)))
                                if has_inter:
                                    k.op("pe", [Scur, qtil], [ps_o], lambda e, h=h, c=c: e.matmul(
                                        ps_o[:, h * 128 + c * 64:h * 128 + (c + 1) * 64], lhsT=Scur[:, h, :],
                                        rhs=qtil[:, h, tc0 + c * 64:tc0 + (c + 1) * 64], start=False, stop=True))
                            ps_u = ps_us[c]
                            if first and c == 0:
                                k.op("dve", [ps_u], [S32], lambda e: e.tensor_copy(out=S32[:], in_=ps_u[:, :].rearrange("p (h e) -> p h e", h=4)))
                            else:
                                cl = tc0 + c * 64 + 63
                                k.op("dve", [S32] + eb_h, [S32], lambda e, cl=cl: e.tensor_tensor(
                                    out=S32[:], in0=S32[:], in1=eb[:, :, cl:cl + 1].to_broadcast([128, 4, 128]), op=ALU.mult))
                                k.op("dve", [S32, ps_u], [S32], lambda e: e.tensor_tensor(
                                    out=S32[:], in0=S32[:], in1=ps_u[:, :].rearrange("p (h e) -> p h e", h=4), op=ALU.add))
                            k.op("act", [S32], [Snxt], lambda e: e.copy(out=Snxt[:], in_=S32[:]))
                    if sample:
                        sample_hgrn(ps_o, sm)
                    k.op("act", [ps_o], [oT], lambda e: e.copy(
                        out=oT[:, :, tc0:tc0 + TT], in_=ps_o[:, :].rearrange("p (h t) -> p h t", h=4)[:, :, 0:TT]))
                if (not sample) and tok0 + T == SEQ:
                    k.dma("sp", nhg_p.rearrange("h d e -> d h e"), S32[:], [S32], [])
                if tok0 == 0:
                    tap("oT", oT, oT[:, :, 0:T])
                    tap("qtil", qtil, qtil[:, :, 0:T])
                    tap("scm", scm[1], scm[1][:, :, :])
                    tap("gs", gs, gs[:, :, 0:T])
                yield
                k.op("act", [oT], [osq], lambda e: e.activation(out=osq[:, :, 0:T], in_=oT[:, :, 0:T], func=AF.Square))
                hp = min(4, 512 // T)
                for h0 in range(0, 4, hp):
                    ps = nps()
                    for hh in range(hp):
                        h = h0 + hh
                        k.op("pe", [osq, ones32], [ps], lambda e, h=h, hh=hh: e.matmul(
                            ps[:, hh * T:(hh + 1) * T], lhsT=ones32[:, :], rhs=osq[:, h, 0:T], start=True, stop=True))
                    k.op("act", [ps, epsrms], [rstd], lambda e, h0=h0: e.activation(
                        out=rstd[:, h0:h0 + hp, 0:T], in_=ps[:, 0:hp * T].rearrange("p (h t) -> p h t", h=hp), func=AF.Ln, bias=epsrms[:, 0:1]))
                k.op("act", [rstd], [rstd], lambda e: e.activation(out=rstd[:, :, 0:T], in_=rstd[:, :, 0:T], func=AF.Exp, scale=-0.5))
                k.op("dve", [oT, rstd], [oT], lambda e: e.tensor_tensor(out=oT[:, :, 0:T], in0=oT[:, :, 0:T], in1=rstd[:, :, 0:T], op=ALU.mult))
                for h in range(4):
                    k.op("dve", [oT, B.gs_h[h], hgam], [B.yc_b], lambda e, h=h: e.scalar_tensor_tensor(
                        out=B.yc[:, 2 + h, 0:T], in0=oT[:, h, 0:T], scalar=hgam[:, h:h + 1], in1=gs[:, h, 0:T], op0=ALU.mult, op1=ALU.mult))

                yield
                yield
            def c_fin():
                k.dma("sp", yc_scr[:, :, tok0:tok0 + T], B.yc[:, :, 0:T], [B.yc, B.yc_a, B.yc_b, B.yc_c], [yscr])
            return c_X, c_pool_attn, c_hgrn, c_fin

        class _Cur:
            def __getitem__(self, i):
                return k.tls.curB

            def __setitem__(self, i, v):
                k.tls.curB = v
        cur = _Cur()
        def sample_pool(proj):
            T = NS * ST
            L = 15 + ST
            us = p32[:].rearrange("p a b -> p (a b)")[:, 0:2 * NS * L].rearrange("p (c b l) -> p c b l", c=2, b=NS)
            s2 = mko[:].rearrange("p a b -> p (a b)")[:, 0:2 * NS * L].rearrange("p (c b l) -> p c b l", c=2, b=NS)
            s4 = sw[0][:].rearrange("p a b -> p (a b)")[:, 0:2 * NS * 16].rearrange("p (c b l) -> p c b l", c=2, b=NS)
            for half in range(2):
                k.dma("sp", utm[0:120, :], spool[half * 8:(half + 1) * 8].rearrange("b r c -> (b r) c"), [], [utm])
                for c2 in range(2):
                    ps = nps()
                    k.op("pe", [utm, ident32], [ps], lambda e, c2=c2: e.transpose(
                        out=ps[:, 0:120], in_=utm[0:120, c2 * 128:(c2 + 1) * 128], identity=ident32[0:120, 0:120]))
                    k.op("dve", [ps], [p32], lambda e, c2=c2, half=half: e.tensor_copy(
                        out=us[:, c2, half * 8:(half + 1) * 8, 0:15], in_=ps[:, 0:120].rearrange("p (b r) -> p b r", r=15)))
            for c2 in range(2):
                ps = proj(c2)
                k.op("act", [ps], [p32], lambda e, c2=c2: e.copy(out=us[:, c2, :, 15:L], in_=ps[:, 0:T].rearrange("p (b t) -> p b t", t=ST)))
            k.op("dve", [p32], [mko], lambda e: e.tensor_tensor(out=s2[:, :, :, 0:L - 1], in0=us[:, :, :, 0:L - 1], in1=us[:, :, :, 1:L], op=ALU.add))
            k.op("dve", [mko], [sw[0]], lambda e: e.tensor_tensor(out=s4[:, :, :, 0:L - 3], in0=s2[:, :, :, 0:L - 3], in1=s2[:, :, :, 2:L - 1], op=ALU.add))
            dv = dif[:, :, 0:T].rearrange("p c (b t) -> p c b t", t=ST)
            k.op("dve", [mko, p32], [dif], lambda e: e.scalar_tensor_tensor(
                out=dv[0:64, 0], in0=s2[0:64, 0, :, 14:14 + ST], scalar=0.5, in1=us[0:64, 0, :, 15:L], op0=ALU.mult, op1=ALU.subtract))
            k.op("dve", [sw[0], p32], [dif], lambda e: e.scalar_tensor_tensor(
                out=dv[64:128, 0], in0=s4[64:128, 0, :, 12:12 + ST], scalar=0.25, in1=us[64:128, 0, :, 15:L], op0=ALU.mult, op1=ALU.subtract))
            k.op("dve", [sw[0], mko], [mko], lambda e: e.tensor_tensor(out=s2[:, 1, :, 0:L - 7], in0=s4[:, 1, :, 0:L - 7], in1=s4[:, 1, :, 4:L - 3], op=ALU.add))
            k.op("dve", [mko, p32], [dif], lambda e: e.scalar_tensor_tensor(
                out=dv[0:64, 1], in0=s2[0:64, 1, :, 8:8 + ST], scalar=0.125, in1=us[0:64, 1, :, 15:L], op0=ALU.mult, op1=ALU.subtract))
            k.op("dve", [mko, sw[0]], [sw[0]], lambda e: e.tensor_tensor(out=s4[:, 1, :, 0:L - 15], in0=s2[:, 1, :, 0:L - 15], in1=s2[:, 1, :, 8:L - 7], op=ALU.add))
            k.op("dve", [sw[0], p32], [dif], lambda e: e.scalar_tensor_tensor(
                out=dv[64:128, 1], in0=s4[64:128, 1, :, 0:ST], scalar=0.0625, in1=us[64:128, 1, :, 15:L], op0=ALU.mult, op1=ALU.subtract))
            for c2 in range(2):
                ps = nps()
                k.op("pe", [dif, wgrp], [ps], lambda e, c2=c2: e.matmul(ps[:, 0:T], lhsT=wgrp[:, c2, :], rhs=dif[:, c2, 0:T], start=True, stop=True))
                k.op("act", [ps, pscale], [cur[0].yc_a], lambda e, c2=c2: e.activation(
                    out=cur[0].yc[:, c2, 0:T], in_=ps[:, 0:T], func=AF.Copy, scale=pscale[:, c2:c2 + 1]))
            k.dma("sp", npool_s[:, 0:11, :], spool[:, 4:15, :], [], [utm])
            ps = nps()
            for c in range(8):
                k.op("pe", [cur[0].xt, w_inA_g[0]], [ps], lambda e, c=c: e.matmul(
                    ps[0:T, 0:256], lhsT=cur[0].xt[:, c, 0:T], rhs=w_inA[:, c, 0:256], start=(c == 0), stop=(c == 7)))
            k.op("act", [ps], [utm], lambda e: e.copy(out=utm[0:T, :], in_=ps[0:T, 0:256]))
            k.dma("sp", [npool_s[:, 11 + t, :] for t in range(ST)], [utm[t:T:ST, :] for t in range(ST)], [utm], [])

        def sample_hgrn(ps_o, sm):
            T = NS * ST
            vtm = cur[0].vtm
            k.op("pool", [], [sm], lambda e: e.memset(sm[64:128, :, :], 0.0))
            for j in range(NS):
                Sb = sst[j % NPF]
                Sv = Sb[:]
                Sbf_j = Sbfs[j % 2]
                km = khm2[j % 2]
                k.op("act", [Sb], [Sbf_j], lambda e: e.copy(out=Sbf_j[:], in_=Sv))
                for h in range(4):
                    k.op("pe", [vtm[0], sm], [ps_o], lambda e, h=h: e.matmul(
                        ps_o[:, h * 128 + ST * j:h * 128 + ST * (j + 1)], lhsT=vtm[0][:, h * 128:(h + 1) * 128],
                        rhs=sm[:, h, ST * j:ST * (j + 1)], start=True, stop=False))
                    k.op("pe", [Sbf_j, qtil], [ps_o], lambda e, h=h: e.matmul(
                        ps_o[:, h * 128 + ST * j:h * 128 + ST * (j + 1)], lhsT=Sbf_j[:, h, :],
                        rhs=qtil[:, h, ST * j:ST * (j + 1)], start=False, stop=True))
                k.op("dve", [khtm[0], seqm_tm], [km], lambda e: e.tensor_scalar(
                    out=km[0:T, :], in0=khtm[0][0:T, :], scalar1=seqm_tm[:, j:j + 1], scalar2=None, op0=ALU.mult))
                cands_ = [b_ for b_ in cur_banks().banks if b_ is not ps_o]
                ps_u = cands_[j % len(cands_)]
                for h in range(4):
                    k.op("pe", [km, vtm[0]], [ps_u], lambda e, h=h: e.matmul(
                        ps_u[:, h * 128:(h + 1) * 128], lhsT=km[0:T, h * 128:(h + 1) * 128], rhs=vtm[0][0:T, h * 128:(h + 1) * 128],
                        start=True, stop=True))
                cl = ST * j + ST - 1
                k.op("dve", [Sb] + eb_h, [Sb], lambda e: e.tensor_tensor(
                    out=Sv, in0=Sv, in1=eb[:, :, cl:cl + 1].to_broadcast([128, 4, 128]), op=ALU.mult))
                k.op("dve", [Sb, ps_u], [Sb], lambda e: e.tensor_tensor(
                    out=Sv, in0=Sv, in1=ps_u[:, :].rearrange("p (h e) -> p h e", h=4), op=ALU.add))
                k.dma("sp", nhg_s[j].rearrange("h d e -> d h e"), Sv, [Sb], [])
                if j + NPF < NS:
                    k.dma("sp", Sv, shg[j + NPF].rearrange("h d e -> d h e"), [], [Sb])

        def sample_attn():
            T = NS * ST
            qcT = cur[0].qcT
            ktps = [ktp, pT]
            ktv = [ktp[:], pT[:].rearrange("p (h a) t -> p h (a t)", h=4)]
            vps = [vpd[:], memT[:, 0:4, :].rearrange("p a b -> p (a b)").rearrange("p (m h c) -> p m h c", m=2, h=4)]
            vpb = [vpd, memT]
            k.op("pool", [], [pT], lambda e: e.memset(pT[:], 0.0))
            k.op("pool", [], [memT], lambda e: e.memset(memT[:], 0.0))
            bk_ = cur_banks().banks
            pss = bk_[0:4]
            for j in range(NS):
                sgb = kst[j % NPF]
                sg_ = sgb[:]
                psb = bk_[4]
                pv = bfv(psb)
                for jj in range(2):
                    for mc in range(2):
                        k.op("pe", [sgb, ident], [psb], lambda e, jj=jj, mc=mc: e.transpose(
                            out=pv[:, (jj * 2 + mc) * 128:(jj * 2 + mc + 1) * 128], in_=sg_[:, mc, jj * 128:(jj + 1) * 128], identity=ident[:, :]))
                if j + NPF < NS:
                    k.dma("pool", sg_, cmk[j + NPF].rearrange("(c p) n -> p c n", p=128), [], [sgb])
                kt = ktv[j % 2]
                for h in range(4):
                    hh = h % 2
                    k.op("act", [psb], [ktps[j % 2]], lambda e, h=h, hh=hh: e.copy(
                        out=kt[hh * 64:(hh + 1) * 64, h, :], in_=pv[hh * 64:(hh + 1) * 64, (h // 2) * 256:(h // 2) * 256 + 256]))
                qmb = qms2[j % 2]
                qm = qmb[:]
                k.op("dve", [qcT, seqm_fm], [qmb], lambda e: e.tensor_tensor(
                    out=qm, in0=qcT[:, :, 0:T], in1=seqm_fm[:, j, :].unsqueeze(1).to_broadcast([128, 2, T]), op=ALU.mult))
                for h in range(4):
                    k.op("pe", [qmb, ktps[j % 2]], [pss[h]], lambda e, h=h: e.matmul(
                        pss[h][0:T, 0:256], lhsT=qm[:, h // 2, :], rhs=kt[:, h, :], start=(j == 0), stop=(j == NS - 1)))
            for h in range(4):
                k.op("dve", [pss[h]], [stat], lambda e, h=h: e.tensor_reduce(
                    out=stat[0:T, h:h + 1], in_=pss[h][0:T, 0:256], axis=AX.X, op=ALU.max))
            k.op("dve", [stat], [stat], lambda e: e.tensor_scalar(
                out=stat[0:T, 4:8], in0=stat[0:T, 0:4], scalar1=-0.125, scalar2=None, op0=ALU.mult))
            for h in range(4):
                k.op("act", [pss[h], stat], [p32, stat], lambda e, h=h: e.activation(
                    out=p32[0:T, h, :], in_=pss[h][0:T, 0:256], func=AF.Exp,
                    scale=0.125, bias=stat[0:T, 4 + h:5 + h], accum_out=stat[0:T, 8 + h:9 + h]))
            k.op("dve", [stat], [stat], lambda e: e.reciprocal(out=stat[0:T, 12:16], in_=stat[0:T, 8:12]))
            k.op("dve", [p32, stat], [pn], lambda e: e.tensor_tensor(
                out=pn[0:T, :, :], in0=p32[0:T, :, :], in1=stat[0:T, 12:16].unsqueeze(2).to_broadcast([T, 4, 256]), op=ALU.mult))
            psb = bk_[4]
            pv = bfv(psb)
            for h in range(4):
                for mc in range(2):
                    i = h * 2 + mc
                    k.op("pe", [pn, ident], [psb], lambda e, h=h, mc=mc, i=i: e.transpose(
                        out=pv[:, i * 128:i * 128 + T], in_=pn[0:T, h, mc * 128:(mc + 1) * 128], identity=ident[0:T, 0:T]))
            k.op("act", [psb], [pT], lambda e: e.copy(out=pT[:, :, 0:T], in_=pv.rearrange("p (i t) -> p i t", i=8)[:, :, 0:T]))
            ps_y = bk_[0]
            for j in range(NS):
                sgb = vst[j % NPF]
                sg_ = sgb[:]
                vp = vps[j % 2]
                for mc in range(2):
                    k.op("dve", [sgb], [vpb[j % 2]], lambda e, mc=mc: e.tensor_copy(
                        out=vp[:, mc, 0:4:2, 0:64], in_=sg_[:, mc, :].rearrange("p (h d) -> p h d", h=4)[:, 0:4:2, :]))
                    k.op("act", [sgb], [vpb[j % 2]], lambda e, mc=mc: e.copy(
                        out=vp[:, mc, 1:4:2, 64:128], in_=sg_[:, mc, :].rearrange("p (h d) -> p h d", h=4)[:, 1:4:2, :]))
                if j + NPF < NS:
                    k.dma("pool", sg_, cmv[j + NPF].rearrange("(c p) n -> p c n", p=128), [], [sgb])
                for jj in range(2):
                    n = 0
                    for hh in range(2):
                        h = 2 * jj + hh
                        for mc in range(2):
                            k.op("pe", [vpb[j % 2], pT], [ps_y], lambda e, h=h, mc=mc, jj=jj, n=n: e.matmul(
                                ps_y[:, jj * 128 + ST * j:jj * 128 + ST * (j + 1)], lhsT=vp[:, mc, h, :],
                                rhs=pT[:, h * 2 + mc, ST * j:ST * (j + 1)], start=(n == 0), stop=(n == 3)))
                            n += 1
            k.op("dve", [ps_y], [cur[0].yc_c], lambda e: e.tensor_copy(
                out=cur[0].yc[:, 6:8, 0:T], in_=ps_y[:, 0:256].rearrange("p (j t) -> p j t", j=2)[:, :, 0:T]))

        tiles = [(i_, i_ * TA, TA, False) for i_ in range(SEQ // TA)] + [(SEQ // TA, SEQ, NS * ST, True)]
        chains = [a1_supertile(*tl) for tl in tiles]

        def runner(gen_fn, banks):
            def f():
                use_chain(banks)
                for _ in gen_fn():
                    pass
            return f
        BX, BP, BH = [PS[0], PS[1]], [PS[2], PS[3], PS[4]], [PS[5], PS[6], PS[7]]
        for tt_ in range(2):
            load_T_dma(es, xp[tt_ * 128:(tt_ + 1) * 128, :], 128, "x%d" % tt_)
        for g in (0, 1, 3, 2, 4):
            k.dma("pool", w_inA[:, :, g * 512:(g + 1) * 512], w_in[:, g * 512:(g + 1) * 512].rearrange("(c p) n -> p c n", p=128), [], [w_inA_g[g]])
        k.rr.run([runner(chains[0][0], BX)])
        for j_ in range(NPF):
            k.dma("sp", sst[j_][:], shg[j_].rearrange("h d e -> d h e"), [], [sst[j_]])
            k.dma("pool", kst[j_][:], cmk[j_].rearrange("(c p) n -> p c n", p=128), [], [kst[j_]])
            k.dma("pool", vst[j_][:], cmv[j_].rearrange("(c p) n -> p c n", p=128), [], [vst[j_]])
        for i_, tl in enumerate(tiles):
            last = (i_ + 1 == len(tiles))
            fns = [runner(chains[i_][2], BH), runner(chains[i_][1], (BX + BP) if last else BP)]
            if not last:
                fns.append(runner(chains[i_ + 1][0], BX))
            k.rr.run(fns, [2, 1, 1][:len(fns)])
            chains[i_][3]()
            ckpt("a1 st%d" % i_)
        k.barrier()

    def layer_norm(es_, res, TT, lng, lnbb, stat, outb):
        k.op("dve", [res], [stat], lambda e: e.bn_stats(out=stat[0:TT, 0:6], in_=res[0:TT, 0:512]))
        k.op("dve", [res], [stat], lambda e: e.bn_stats(out=stat[0:TT, 6:12], in_=res[0:TT, 512:1024]))
        k.op("dve", [stat], [stat], lambda e: e.bn_aggr(out=stat[0:TT, 12:14], in_=stat[0:TT, 0:12]))
        k.op("act", [stat, epsln], [stat], lambda e: e.activation(
            out=stat[0:TT, 14:15], in_=stat[0:TT, 13:14], func=AF.Sqrt, bias=epsln[0:TT, 0:1]))
        k.op("dve", [stat], [stat], lambda e: e.reciprocal(out=stat[0:TT, 15:16], in_=stat[0:TT, 14:15]))
        k.op("dve", [res, stat, lng], [res], lambda e: e.scalar_tensor_tensor(
            out=res[0:TT, :], in0=res[0:TT, :], scalar=stat[0:TT, 12:13], in1=lng[0:TT, :], op0=ALU.subtract, op1=ALU.mult))
        k.op("dve", [res, stat, lnbb], [outb], lambda e: e.scalar_tensor_tensor(
            out=outb[0:TT, :], in0=res[0:TT, :], scalar=stat[0:TT, 15:16], in1=lnbb[0:TT, :], op0=ALU.mult, op1=ALU.add))

    with ExitStack() as es:
        set_psum(es, 8, 0, "b")
        use_chain(list(PS))
        w_inG = k.sb(es, "w_inG", [128, 8, 3072], BF16)
        w_br = k.sb(es, "w_br", [128, 8, D], BF16)
        w_o = k.sb(es, "w_o", [128, 8, D], BF16)
        lng = k.sb(es, "lng", [128, D], F32)
        lnbb = k.sb(es, "lnbb", [128, D], F32)
        NG = 4
        w_inG_g = [k.alias(w_inG, "w_inG_g%d" % g) for g in range(NG)]
        w_br_g = [k.alias(w_br, "w_br_g%d" % g) for g in range(NG)]
        for g in range(NG):
            cs = slice(g * 256, (g + 1) * 256)
            k.dma("pool", [w_br[:, 0:2, cs], w_br[:, 2:6, cs], w_br[:, 6:8, cs]],
                  [w_br_pool[:, cs].rearrange("(c p) n -> p c n", p=128), w_br_hg[:, cs].rearrange("(c p) n -> p c n", p=128),
                   w_br_mem[:, cs].rearrange("(c p) n -> p c n", p=128)], [], [w_br_g[g]])
            k.dma("pool", [w_inG[:, :, b * 1024 + g * 256:b * 1024 + (g + 1) * 256] for b in range(3)],
                  [w_in[:, 2560 + b * 1024 + g * 256:2560 + b * 1024 + (g + 1) * 256].rearrange("(c p) n -> p c n", p=128) for b in range(3)],
                  [], [w_inG_g[g]])
        k.dma("pool", w_o[:], w_out.rearrange("(c p) n -> p c n", p=128), [], [w_o])
        xt2 = [k.sb(es, "xt2_%d" % i, [128, 8, TB], BF16) for i in range(2)]
        yc2 = [k.sb(es, "yc2_%d" % i, [128, 8, TB], BF16) for i in range(2)]
        k.dma("sp", lng[:], ln1_g.partition_broadcast(128), [], [lng])
        k.dma("sp", lnbb[:], ln1_b.partition_broadcast(128), [], [lnbb])
        x32 = [k.sb(es, "x32_%d" % i, [128, D], F32) for i in range(2)]
        gsg = [k.sb(es, "gsg%d" % i, [128, TB], F32) for i in range(6)]
        t1 = [k.sb(es, "t1_%d" % i, [128, TB], F32) for i in range(2)]
        t2 = [k.sb(es, "t2_%d" % i, [128, TB], F32) for i in range(2)]
        mTs = [k.sb(es, "mT%d" % i, [128, 8, TB], BF16) for i in range(2)]
        mTs_c = [[k.alias(mTs[i], "mT%d_c%d" % (i, c_)) for c_ in range(8)] for i in range(2)]
        res = [k.sb(es, "res%d" % i, [128, D], F32) for i in range(2)]
        hout = [k.sb(es, "hout%d" % i, [128, D], F32) for i in range(2)]
        stat2 = [k.sb(es, "stat2_%d" % i, [128, 16], F32) for i in range(2)]
        cnt = [0, 0]
        BRC = ((0, 2), (2, 6), (6, 8))

        def a2_load(idx, tok0, T):
            k.dma("sp", xt2[idx % 2][:, :, 0:T], xT_scr[:, :, tok0:tok0 + T], [xscr], [xt2[idx % 2]])
            k.dma("sp", yc2[idx % 2][:, :, 0:T], yc_scr[:, :, tok0:tok0 + T], [yscr], [yc2[idx % 2]])

        def a2_jloop(idx, tok0, T, sample):
            xt_, yc_ = xt2[idx % 2], yc2[idx % 2]
            mT = mTs[idx % 2]
            for j in range(8):
                g3 = [gsg[(cnt[0] % 2) * 3 + b] for b in range(3)]
                ta, tb2 = t1[cnt[0] % 2], t2[cnt[0] % 2]
                cnt[0] += 1
                for b in range(3):
                    ps = nps()
                    for c in range(8):
                        k.op("pe", [xt_, w_inG_g[j // 2]], [ps], lambda e, c=c: e.matmul(
                            ps[:, 0:T], lhsT=w_inG[:, c, b * 1024 + j * 128:b * 1024 + (j + 1) * 128], rhs=xt_[:, c, 0:T],
                            start=(c == 0), stop=(c == 7)))
                    k.op("act", [ps], [g3[b]], lambda e, b=b: e.activation(out=g3[b][:, 0:T], in_=ps[:, 0:T], func=AF.Sigmoid))
                psb3 = []
                for b, (c0, c1) in enumerate(BRC):
                    ps = nps()
                    for c in range(c0, c1):
                        k.op("pe", [yc_, w_br_g[j // 2]], [ps], lambda e, c=c: e.matmul(
                            ps[:, 0:T], lhsT=w_br[:, c, j * 128:(j + 1) * 128], rhs=yc_[:, c, 0:T], start=(c == c0), stop=(c == c1 - 1)))
                    psb3.append(ps)
                k.op("dve", [g3[0], psb3[0]], [ta], lambda e: e.tensor_tensor(out=ta[:, 0:T], in0=g3[0][:, 0:T], in1=psb3[0][:, 0:T], op=ALU.mult))
                k.op("dve", [g3[1], psb3[1]], [tb2], lambda e: e.tensor_tensor(out=tb2[:, 0:T], in0=g3[1][:, 0:T], in1=psb3[1][:, 0:T], op=ALU.mult))
                k.op("pool", [ta, tb2], [ta], lambda e: e.tensor_tensor(out=ta[:, 0:T], in0=ta[:, 0:T], in1=tb2[:, 0:T], op=ALU.add))
                k.op("dve", [g3[2], psb3[2]], [tb2], lambda e: e.tensor_tensor(out=tb2[:, 0:T], in0=g3[2][:, 0:T], in1=psb3[2][:, 0:T], op=ALU.mult))
                k.op("pool", [ta, tb2], [mTs_c[idx % 2][j]], lambda e: e.tensor_tensor(out=mT[:, j, 0:T], in0=ta[:, 0:T], in1=tb2[:, 0:T], op=ALU.add))

        def a2_tail(idx, tok0, T, sample):
            TT = 64 if sample else 128
            ntt = T // TT
            mT = mTs[idx % 2]
            for tt in range(ntt):
                xb = x32[cnt[1] % 2]
                rb = res[cnt[1] % 2]
                hb = hout[cnt[1] % 2]
                sb2 = stat2[cnt[1] % 2]
                cnt[1] += 1
                r0 = tok0 + tt * TT
                src = xs[r0 - SEQ:r0 - SEQ + TT, :] if sample else xp[r0:r0 + TT, :]
                k.dma("sp", xb[0:TT, :], src, [], [xb])
                for half in range(2):
                    ps = nps()
                    for c in range(8):
                        k.op("pe", [mTs_c[idx % 2][c], w_o], [ps], lambda e, c=c: e.matmul(
                            ps[0:TT, :], lhsT=mT[:, c, tt * TT:(tt + 1) * TT], rhs=w_o[:, c, half * 512:(half + 1) * 512],
                            start=(c == 0), stop=(c == 7)))
                    k.op("dve", [xb, ps], [rb], lambda e, half=half: e.scalar_tensor_tensor(
                        out=rb[0:TT, half * 512:(half + 1) * 512], in0=xb[0:TT, half * 512:(half + 1) * 512], scalar=ALPHA,
                        in1=ps[0:TT, :], op0=ALU.mult, op1=ALU.add))
                layer_norm(es, rb, TT, lng, lnbb, sb2, hb)
                k.dma("pool", h_scr[r0:r0 + TT, :], hb[0:TT, :], [hb], [hscr])

        tiles2 = [(i_, i_ * TB, TB, False) for i_ in range(SEQ // TB)] + [(SEQ // TB, SEQ, NS * ST, True)]
        a2_load(0, tiles2[0][1], tiles2[0][2])
        a2_load(1, tiles2[1][1], tiles2[1][2])

        def chain(fn, banks, *a):
            def f():
                use_chain(banks)
                fn(*a)
            return f
        BJ, BT = list(PS[0:6]), list(PS[6:8])
        k.rr.run([chain(a2_jloop, BJ, *tiles2[0])])
        for i_, tl in enumerate(tiles2):
            fns = [chain(a2_tail, BT, *tl)]
            if i_ + 1 < len(tiles2):
                fns.append(chain(a2_jloop, BJ, *tiles2[i_ + 1]))
            k.rr.run(fns, [1, 3][:len(fns)])
            if i_ + 2 < len(tiles2):
                a2_load(i_ + 2, tiles2[i_ + 2][1], tiles2[i_ + 2][2])
            ckpt("a2 st%d" % i_)
        k.barrier()
    esA.close()

    with ExitStack() as es:
        set_psum(es, 7, 1, "c")
        use_chain(list(PS))
        wg = k.sb(es, "wg", [128, 8, DFF], BF16)
        wu = k.sb(es, "wu", [128, 8, DFF], BF16)
        wd = k.sb(es, "wd", [128, NFF, D], BF16)
        lng = k.sb(es, "lng2", [128, D], F32)
        lnbb = k.sb(es, "lnbb2", [128, D], F32)
        cw = k.sb(es, "cw", [128, 3, NFF], F32)
        cb = k.sb(es, "cb", [128, NFF], F32)
        hTs = [k.sb(es, "hT%d" % i, [128, 8, TB], BF16) for i in range(2)]
        for tt_ in range(TB // 128):
            load_T(es, h_scr[tt_ * 128:(tt_ + 1) * 128, :], hTs[0], tt_ * 128, 128, "h%d" % (tt_ % 2))
        FG = [(0, 2), (2, 6), (6, 14), (14, NFF)]
        wg_g = [k.alias(wg, "wg_g%d" % g) for g in range(len(FG))]
        wu_g = [k.alias(wu, "wu_g%d" % g) for g in range(len(FG))]
        ffg = {}
        for g, (a_, b_) in enumerate(FG):
            for c_ in range(a_, b_):
                ffg[c_] = g
            k.dma("pool", wg[:, :, a_ * 128:b_ * 128], w_gate[:, a_ * 128:b_ * 128].rearrange("(c p) n -> p c n", p=128), [], [wg_g[g]])
            k.dma("pool", wu[:, :, a_ * 128:b_ * 128], w_up[:, a_ * 128:b_ * 128].rearrange("(c p) n -> p c n", p=128), [], [wu_g[g]])
        k.dma("pool", wd[:], w_down.rearrange("(c p) n -> p c n", p=128), [], [wd])
        k.dma("sp", lng[:], ln2_g.partition_broadcast(128), [], [lng])
        k.dma("sp", lnbb[:], ln2_b.partition_broadcast(128), [], [lnbb])
        with nc.allow_non_contiguous_dma(reason="per-partition conv taps"):
            k.dma("sp", cw[:], conv_w.rearrange("j (c p) -> p j c", p=128), [], [cw])
            k.dma("sp", cb[:], conv_b.rearrange("(c p) -> p c", p=128), [], [cb])
        ab = [k.sb(es, "ab%d" % i, [128, 2 + TB], F32) for i in range(2)]
        cbuf = [k.sb(es, "cbuf%d" % i, [128, TB], F32) for i in range(2)]
        halo = k.sb(es, "halo", [128, NFF, 2], F32)
        actT = k.sb(es, "actT", [128, NFF, TB], BF16)
        actT_c = [k.alias(actT, "actT_c%d" % c_) for c_ in range(NFF)]
        res = [k.sb(es, "resb%d" % i, [128, D], F32) for i in range(2)]
        stat2 = [k.sb(es, "stat3_%d" % i, [128, 16], F32) for i in range(2)]
        ctm = k.sb(es, "ctm", [64, 512], F32)
        sab = k.sb(es, "sab", [128, NFF, NS * 2], F32)
        cntb = [0, 0]
        k.op("pool", [], [halo], lambda e: e.memset(halo[:], 0.0))

        def b_supertile(idx, tok0, T, sample, nxt):
            TT = 64 if sample else 128
            ntt = T // TT
            hT = hTs[idx % 2]
            if sample:
                NB, TL = NS, ST
            else:
                NB, TL = 1, T
            L = TL + 2
            pend = []
            if nxt is not None:
                n_tok0, n_T, n_sample = nxt
                n_TT = 64 if n_sample else 128
                pend = [(n_tok0 + t_ * n_TT, t_, n_TT) for t_ in range(n_T // n_TT)]

            def pf_dma(i):
                if i < len(pend):
                    r0_, t_, n_ = pend[i]
                    load_T_dma(es, h_scr[r0_:r0_ + n_, :], n_, "h%d" % (t_ % 2))

            def pf_tr(i):
                if i < len(pend):
                    r0_, t_, n_ = pend[i]
                    load_T(es, h_scr[r0_:r0_ + n_, :], hTs[(idx + 1) % 2], t_ * n_, n_, "h%d" % (t_ % 2), dma=False)
            pf_dma(0)
            pf_dma(1)
            for c in range(NFF):
                a_ = ab[cntb[0] % 2]
                cb_ = cbuf[cntb[0] % 2]
                cntb[0] += 1
                ps_a = nps()
                for kk_ in range(8):
                    k.op("pe", [hT, wg_g[ffg[c]]], [ps_a], lambda e, kk_=kk_: e.matmul(
                        ps_a[:, 0:T], lhsT=wg[:, kk_, c * 128:(c + 1) * 128], rhs=hT[:, kk_, 0:T], start=(kk_ == 0), stop=(kk_ == 7)))
                ps_u = nps()
                for kk_ in range(8):
                    k.op("pe", [hT, wu_g[ffg[c]]], [ps_u], lambda e, kk_=kk_: e.matmul(
                        ps_u[:, 0:T], lhsT=wu[:, kk_, c * 128:(c + 1) * 128], rhs=hT[:, kk_, 0:T], start=(kk_ == 0), stop=(kk_ == 7)))
                av = a_[:, 0:NB * L].rearrange("p (b l) -> p b l", b=NB)
                cv = cb_[:, 0:T].rearrange("p (b l) -> p b l", b=NB)
                pav = ps_a[:, 0:T].rearrange("p (b l) -> p b l", b=NB)
                if sample:
                    k.op("pool", [sab], [a_], lambda e: e.tensor_copy(
                        out=av[:, :, 0:2], in_=sab[:, c, :].rearrange("p (b r) -> p b r", r=2)))
                else:
                    k.op("pool", [halo], [a_], lambda e: e.tensor_copy(out=a_[:, 0:2], in_=halo[:, c, :]))
                k.op("act", [ps_a], [a_], lambda e: e.copy(out=av[:, :, 2:L], in_=pav))
                k.op("act", [ps_a, cw, cb], [cb_], lambda e: e.activation(
                    out=cb_[:, 0:T], in_=ps_a[:, 0:T], func=AF.Identity, scale=cw[:, 2, c:c + 1], bias=cb[:, c:c + 1]))
                k.op("dve", [a_, cb_, cw], [cb_], lambda e: e.scalar_tensor_tensor(
                    out=cv, in0=av[:, :, 1:1 + TL], scalar=cw[:, 1, c:c + 1], in1=cv, op0=ALU.mult, op1=ALU.add))
                k.op("dve", [a_, cb_, cw], [cb_], lambda e: e.scalar_tensor_tensor(
                    out=cv, in0=av[:, :, 0:TL], scalar=cw[:, 0, c:c + 1], in1=cv, op0=ALU.mult, op1=ALU.add))
                if not sample:
                    k.op("pool", [a_], [halo], lambda e: e.tensor_copy(out=halo[:, c, :], in_=a_[:, T:T + 2]))
                k.op("act", [cb_], [cb_], lambda e: e.activation(out=cb_[:, 0:T], in_=cb_[:, 0:T], func=AF.Gelu))
                k.op("dve", [cb_, ps_u], [actT_c[c]], lambda e: e.tensor_tensor(out=actT[:, c, 0:T], in0=cb_[:, 0:T], in1=ps_u[:, 0:T], op=ALU.mult))
            pf_tr(0)
            pf_tr(1)
            pf_dma(2)
            pf_dma(3)
            for tt in range(ntt):
                if tt == 1:
                    pf_tr(2)
                    pf_tr(3)
                rb = res[cntb[1] % 2]
                hb = rb
                yb = rb
                sb2 = stat2[cntb[1] % 2]
                cntb[1] += 1
                r0 = tok0 + tt * TT
                k.dma("sp", hb[0:TT, :], h_scr[r0:r0 + TT, :], [hscr], [hb])
                for half in range(2):
                    ps = nps()
                    for c in range(NFF):
                        k.op("pe", [actT_c[c], wd], [ps], lambda e, c=c: e.matmul(
                            ps[0:TT, :], lhsT=actT[:, c, tt * TT:(tt + 1) * TT], rhs=wd[:, c, half * 512:(half + 1) * 512],
                            start=(c == 0), stop=(c == NFF - 1)))
                    k.op("dve", [hb, ps], [rb], lambda e, half=half: e.scalar_tensor_tensor(
                        out=rb[0:TT, half * 512:(half + 1) * 512], in0=hb[0:TT, half * 512:(half + 1) * 512], scalar=ALPHA,
                        in1=ps[0:TT, :], op0=ALU.mult, op1=ALU.add))
                layer_norm(es, rb, TT, lng, lnbb, sb2, yb)
                dst = ys[r0 - SEQ:r0 - SEQ + TT, :] if sample else yp[r0:r0 + TT, :]
                k.dma("pool", dst, yb[0:TT, :], [yb], [])
            if sample or tok0 + T == SEQ:
                M = 64 if sample else 2
                c0 = 0 if sample else T - 2
                for n0 in range(0, DFF, 512):
                    nn = min(512, DFF - n0)
                    ps = nps()
                    for kk_ in range(8):
                        k.op("pe", [hT] + wg_g, [ps], lambda e, kk_=kk_: e.matmul(
                            ps[0:M, 0:nn], lhsT=hT[:, kk_, c0:c0 + M], rhs=wg[:, kk_, n0:n0 + nn], start=(kk_ == 0), stop=(kk_ == 7)))
                    k.op("act", [ps], [ctm], lambda e: e.copy(out=ctm[0:M, 0:nn], in_=ps[0:M, 0:nn]))
                    if sample:
                        k.dma("sp", [nconv_s[:, r, n0:n0 + nn] for r in range(2)],
                              [ctm[2 + r:64:4, 0:nn] for r in range(2)], [ctm], [])
                    else:
                        k.dma("sp", nconv_p[:, n0:n0 + nn], ctm[0:2, 0:nn], [ctm], [])

        tilesb = [(i_ * TB, TB, False) for i_ in range(SEQ // TB)] + [(SEQ, NS * ST, True)]
        for st in range(SEQ // TB):
            b_supertile(st, *tilesb[st], tilesb[st + 1])
            ckpt("b st%d" % st)
        for p0 in range(0, DFF, 1024):
            pn_ = min(1024, DFF - p0)
            stg = res[(p0 // 1024) % 2]
            k.dma("sp", stg[0:32, 0:pn_], sconv[:, p0:p0 + pn_], [], [stg])
            for c in range(p0 // 128, (p0 + pn_) // 128):
                ps = nps()
                k.op("pe", [stg, ident32], [ps], lambda e, c=c: e.transpose(
                    out=ps[:, 0:32], in_=stg[0:32, c * 128 - p0:(c + 1) * 128 - p0], identity=ident32[0:32, 0:32]))
                k.op("dve", [ps], [sab], lambda e, c=c: e.tensor_copy(out=sab[:, c, :], in_=ps[:, 0:32]))
        b_supertile(SEQ // TB, *tilesb[-1], None)
        ckpt("b sample")
        k.barrier()


def _consts():
    c = {}
    c["c_ident"] = np.eye(128, dtype=np.float32)
    c["c_ones"] = np.full((128, 128), 1.0 / 128.0, np.float32)
    rp = np.ones((128, TA), np.float32)
    rp[:, ::64] = 0.0
    c["c_rmask_p"] = rp
    rs = np.ones((128, 64), np.float32)
    rs[:, ::ST] = 0.0
    c["c_rmask_s"] = rs
    s = np.arange(128)[:, None]
    t = np.arange(128)[None, :]
    c["c_cmask_p"] = ((s // 64 == t // 64) & (s <= t)).astype(np.float32)
    s = np.arange(64)[:, None]
    t = np.arange(64)[None, :]
    c["c_cmask_s"] = ((s // ST == t // ST) & (s <= t)).astype(np.float32)
    ic = np.zeros((128, 2, 16), np.float32)
    for c2 in range(2):
        for half in range(2):
            w = 2 ** (2 * c2 + half + 1)
            ic[half * 64:(half + 1) * 64, c2, :] = 1.0 / np.minimum(np.arange(16) + 1, w)
    c["c_invcnt"] = ic
    tt = np.arange(64)
    c["c_seqm_tm"] = (tt[:, None] // ST == np.arange(16)[None, :]).astype(np.float32)
    c["c_seqm_fm"] = np.broadcast_to((np.arange(16)[:, None] == tt[None, :] // ST).astype(np.float32), (128, 16, 64)).copy()
    return c


_NC_CACHE = {}


def kernel(x_prompt, x_sample, state_pool, state_hgrn, state_ffn_conv, cache_mem_k, cache_mem_v, mem_prompt,
           lb_logits, w_in, w_pool_grp, pool_scale, hg_norm_g, w_mem_k, w_mem_v, w_br_pool, w_br_hg, w_br_mem,
           w_out, ln1_g, ln1_b, w_gate, w_up, conv_w, conv_b, w_down, ln2_g, ln2_b):
    f = lambda a: np.ascontiguousarray(np.asarray(a, dtype=np.float32))
    if "nc" not in _NC_CACHE:
        _NC_CACHE["nc"] = _build()
    nc = _NC_CACHE["nc"]
    shared = {
        "lbl": f(lb_logits), "w_in": f(w_in)[0], "w_grp": f(w_pool_grp)[0], "pool_scale": f(pool_scale)[0],
        "hg_g": f(hg_norm_g)[0], "w_mem_k": f(w_mem_k)[0], "w_mem_v": f(w_mem_v)[0], "w_br_pool": f(w_br_pool)[0],
        "w_br_hg": f(w_br_hg)[0], "w_br_mem": f(w_br_mem)[0], "w_out": f(w_out)[0], "ln1_g": f(ln1_g)[0],
        "ln1_b": f(ln1_b)[0], "w_gate": f(w_gate)[0], "w_up": f(w_up)[0], "conv_w": f(conv_w)[0], "conv_b": f(conv_b)[0],
        "w_down": f(w_down)[0], "ln2_g": f(ln2_g)[0], "ln2_b": f(ln2_b)[0],
    }
    shared.update(_consts())
    xp_, xs_ = f(x_prompt), f(x_sample)
    sp_, sh_, sc_ = f(state_pool)[0], f(state_hgrn)[0], f(state_ffn_conv)[0]
    ck_, cv_, mp_ = f(cache_mem_k)[0], f(cache_mem_v)[0], f(mem_prompt)
    in_maps = []
    for c in range(NCORES):
        b = slice(c * NS, (c + 1) * NS)
        m = dict(shared)
        m["xp"] = xp_[c]
        m["xs"] = xs_[b].reshape(NS * ST, D)
        m["spool"] = sp_[b]
        m["shg"] = sh_[b]
        m["sconv"] = sc_[b].reshape(NS * 2, DFF)
        m["cmk"] = ck_[b].reshape(NS, 256, 256)
        m["cmv"] = cv_[b].reshape(NS, 256, 256)
        m["memp"] = mp_[c]
        in_maps.append(m)
    res = run_bass_kernel_spmd(nc, in_maps, core_ids=list(range(NCORES)))
    R = res.results
    g = lambda n: np.stack([np.asarray(R[c][n], dtype=np.float32) for c in range(NCORES)])
    y_p = g("yp")
    y_s = g("ys").reshape(NCORES * NS, ST, D)
    o_pool_p = g("npool_p")[None]
    o_hg_p = g("nhg_p")[None]
    o_conv_p = g("nconv_p")[None]
    o_mk_p = g("nmk_p").reshape(1, NCORES, 256, 4, 64)
    o_mv_p = g("nmv_p").reshape(1, NCORES, 256, 4, 64)
    o_pool_s = g("npool_s").reshape(1, NCORES * NS, 15, 256)
    o_hg_s = g("nhg_s").reshape(1, NCORES * NS, 4, 128, 128)
    o_conv_s = g("nconv_s").reshape(1, NCORES * NS, 2, DFF)
    return (y_p, y_s, o_pool_p, o_hg_p, o_conv_p, o_mk_p, o_mv_p, o_pool_s, o_hg_s, o_conv_s)
```

```python
import numpy as np
from contextlib import ExitStack
import concourse.bass as bass
import concourse.mybir as mybir
from concourse.bass_utils import run_bass_kernel_spmd

F32 = mybir.dt.float32
BF16 = mybir.dt.bfloat16
ALU = mybir.AluOpType
AF = mybir.ActivationFunctionType
AX = mybir.AxisListType

NCORES = 8
D = 1024
SEQ = 2048
NS = 16
ST = 4
NTOK = SEQ + NS * ST
DFF = 2816
NFF = DFF // 128
ALPHA = float(2.0 ** 0.25)
LN_EPS = 1e-5
RMS_EPS = 1e-6
TA = 256
TB = 512


class Buf:
    def __init__(self, k, t, name):
        self.k = k
        self.t = t
        self.name = name
        self.w = None
        self.r = []
        self.dsem = None
        self.dcnt = 0
        self.excl = False

    def __getitem__(self, idx):
        return self.t[idx]


class Eng:
    def __init__(self, name, eng, sem):
        self.name = name
        self.eng = eng
        self.sem = sem
        self.cnt = 0
        self.waited = {}


class RR:
    def __init__(self):
        import threading
        self.th = threading
        self.cv = threading.Condition()
        self.turn = 0
        self.alive = []
        self.idx = {}

    def _advance(self):
        n = len(self.alive)
        for d in range(1, n + 1):
            j = (self.turn + d) % n
            if self.alive[j]:
                self.turn = j
                return
        self.turn = -1

    def switch(self):
        i = self.idx.get(self.th.get_ident())
        if i is None:
            return
        self.cnt[i] += 1
        if self.cnt[i] % self.wts[i]:
            return
        with self.cv:
            self._advance()
            self.cv.notify_all()
            while self.turn != i:
                self.cv.wait()

    def run(self, fns, wts=None):
        errs = []
        self.wts = list(wts) if wts else [1] * len(fns)
        self.cnt = [0] * len(fns)
        self.alive = [True] * len(fns)
        self.turn = 0
        self.idx = {}

        def worker(i, fn):
            self.idx[self.th.get_ident()] = i
            with self.cv:
                while self.turn != i:
                    self.cv.wait()
            try:
                fn()
            except BaseException as ex:
                errs.append(ex)
            finally:
                with self.cv:
                    self.alive[i] = False
                    if self.turn == i:
                        self._advance()
                    self.cv.notify_all()
        ths = [self.th.Thread(target=worker, args=(i, f)) for i, f in enumerate(fns)]
        for t in ths:
            t.start()
        for t in ths:
            t.join()
        self.idx = {}
        if errs:
            raise errs[0]


class K:
    def __init__(self, nc, es):
        self.nc = nc
        self.es = es
        self.E = {}
        for name, eng in (("pe", nc.tensor), ("act", nc.scalar), ("dve", nc.vector),
                          ("pool", nc.gpsimd), ("sp", nc.sync)):
            self.E[name] = Eng(name, eng, es.enter_context(nc.semaphore("sem_" + name)))
        self.dma_bufs = []
        self.nsem = 5
        self.psn = 0
        self.stopped = False
        self.rr = RR()
        import threading
        self.tls = threading.local()

    def sb(self, es, name, shape, dt):
        return Buf(self, es.enter_context(self.nc.sbuf_tensor(name, list(shape), dt)), name)

    def alias(self, b, name):
        return Buf(self, b.t, name)

    def dsem_of(self, b):
        if b.dsem is None:
            b.dsem = self.es.enter_context(self.nc.semaphore("ds_" + b.name))
            self.nsem += 1
            self.dma_bufs.append(b)
        return b.dsem

    def _wait(self, e, tok):
        if tok is None:
            return
        sem, val = tok
        key = id(sem)
        if e.waited.get(key, 0) >= val:
            return
        import os as _os
        if sem is e.sem and _os.environ.get("NOSELF"):
            return
        if sem is e.sem and e.name in ("pe", "act"):
            return
        e.eng.wait_ge(sem, val)
        e.waited[key] = val

    def _deps(self, e, reads, writes):
        toks = []
        for b in reads:
            toks.append(b.w)
            if b.excl:
                toks += [t for t in b.r if t[0] is not e.sem]
        for b in writes:
            toks.append(b.w)
            toks += b.r
        best = {}
        for t in toks:
            if t is not None and (id(t[0]) not in best or best[id(t[0])][1] < t[1]):
                best[id(t[0])] = t
        for t in best.values():
            self._wait(e, t)

    def op(self, en, reads, writes, fn):
        if self.stopped:
            return None
        e = self.E[en]
        self._deps(e, reads, writes)
        ins = fn(e.eng)
        e.cnt += 1
        ins.then_inc(e.sem, 1)
        tok = (e.sem, e.cnt)
        for b in reads:
            b.r.append(tok)
        for b in writes:
            b.w = tok
            b.r = []
        self.rr.switch()
        return ins

    def dma(self, qn, out, in_, reads, writes, n=1, **kw):
        if self.stopped:
            return
        e = self.E[qn]
        self._deps(e, reads, writes)
        tb = writes[0] if writes else reads[0]
        sem = self.dsem_of(tb)
        outs = out if isinstance(out, list) else [out]
        ins = in_ if isinstance(in_, list) else [in_]
        for o, i in zip(outs, ins):
            e.eng.dma_start(out=o, in_=i, **kw).then_inc(sem, 16)
            tb.dcnt += 16
        tok = (sem, tb.dcnt)
        for b in reads:
            b.r.append(tok)
        for b in writes:
            b.w = tok
            b.r = []
        self.rr.switch()

    def barrier(self, bufs=()):
        if self.stopped:
            return
        toks = [(e.sem, e.cnt) for e in self.E.values() if e.cnt > 0]
        for b in self.dma_bufs:
            if b.dcnt:
                toks.append((b.dsem, b.dcnt))
        for e in self.E.values():
            for t in toks:
                if t[0] is e.sem:
                    continue
                self._wait(e, t)


def _build(dbg=False):
    nc = bass.Bass("TRN2", target_bir_lowering=False)
    outer = ExitStack()
    with outer:
        _emit(nc, outer)
    return nc


class _Stop(Exception):
    pass


def _emit(nc, outer):
    import os
    k = K(nc, outer)
    kstop = int(os.environ.get("KSTOP", "0"))
    stage = [0]

    def ckpt(name):
        stage[0] += 1
        if kstop and stage[0] >= kstop and not k.stopped:
            print("STOP at stage", stage[0], name)
            k.stopped = True
    _emit2(nc, outer, k, ckpt)
    k.stopped = False
    k.barrier()
    print("instr counts:", {n: e.cnt for n, e in k.E.items()}, "sems:", k.nsem)


def _emit2(nc, outer, k, ckpt):

    def din(name, shape):
        return nc.dram_tensor(name, list(shape), F32, kind="ExternalInput").ap()

    def dout(name, shape):
        return nc.dram_tensor(name, list(shape), F32, kind="ExternalOutput").ap()

    xp = din("xp", [SEQ, D])
    xs = din("xs", [NS * ST, D])
    spool = din("spool", [NS, 15, 256])
    shg = din("shg", [NS, 4, 128, 128])
    sconv = din("sconv", [NS * 2, DFF])
    cmk = din("cmk", [NS, 256, 256])
    cmv = din("cmv", [NS, 256, 256])
    memp = din("memp", [256, D])
    lbl = din("lbl", [2, 512])
    w_in = din("w_in", [D, 5632])
    w_grp = din("w_grp", [4, 64, 64])
    pool_scale = din("pool_scale", [256])
    hg_g = din("hg_g", [512])
    w_mem_k = din("w_mem_k", [D, 256])
    w_mem_v = din("w_mem_v", [D, 256])
    w_br_pool = din("w_br_pool", [256, D])
    w_br_hg = din("w_br_hg", [512, D])
    w_br_mem = din("w_br_mem", [256, D])
    w_out = din("w_out", [D, D])
    ln1_g = din("ln1_g", [D])
    ln1_b = din("ln1_b", [D])
    w_gate = din("w_gate", [D, DFF])
    w_up = din("w_up", [D, DFF])
    conv_w = din("conv_w", [3, DFF])
    conv_b = din("conv_b", [DFF])
    w_down = din("w_down", [DFF, D])
    ln2_g = din("ln2_g", [D])
    ln2_b = din("ln2_b", [D])
    c_ident = din("c_ident", [128, 128])
    c_ones = din("c_ones", [128, 128])
    c_rmask_p = din("c_rmask_p", [128, TA])
    c_rmask_s = din("c_rmask_s", [128, 64])
    c_cmask_p = din("c_cmask_p", [128, 128])
    c_cmask_s = din("c_cmask_s", [64, 64])
    c_invcnt = din("c_invcnt", [128, 2, 16])
    c_seqm_tm = din("c_seqm_tm", [64, 16])
    c_seqm_fm = din("c_seqm_fm", [128, 16, 64])

    yp = dout("yp", [SEQ, D])
    ys = dout("ys", [NS * ST, D])
    npool_p = dout("npool_p", [15, 256])
    nhg_p = dout("nhg_p", [4, 128, 128])
    nconv_p = dout("nconv_p", [2, DFF])
    nmk_p = dout("nmk_p", [256, 256])
    nmv_p = dout("nmv_p", [256, 256])
    npool_s = dout("npool_s", [NS, 15, 256])
    nhg_s = dout("nhg_s", [NS, 4, 128, 128])
    nconv_s = dout("nconv_s", [NS, 2, DFF])
    h_scr_t = nc.dram_tensor("h_scr", [NTOK, D], F32)
    h_scr = h_scr_t.ap()

    es0 = outer
    ident = k.sb(es0, "ident", [128, 128], BF16)
    ident32 = k.sb(es0, "ident32", [128, 128], F32)
    ones32 = k.sb(es0, "ones32", [128, 128], F32)
    epsln = k.sb(es0, "epsln", [128, 1], F32)
    epsrms = k.sb(es0, "epsrms", [128, 1], F32)
    hscr = Buf(k, h_scr_t, "hscr")
    esA = ExitStack()
    esA.__enter__()
    xT_scr_t = nc.dram_tensor("xT_scr", [128, 8, NTOK], BF16)
    yc_scr_t = nc.dram_tensor("yc_scr", [128, 8, NTOK], BF16)
    xT_scr, yc_scr = xT_scr_t.ap(), yc_scr_t.ap()
    xscr = Buf(k, xT_scr_t, "xscr")
    yscr = Buf(k, yc_scr_t, "yscr")

    PS = []
    PSB = []
    psbn = [0]

    def set_psum(es, nf, nb, tag):
        PS.clear()
        for i in range(8):
            PS.append(Buf(k, es.enter_context(nc.psum_tensor("ps%s%d" % (tag, i), [128, 512], F32)), "ps%s%d" % (tag, i)))
            PS[-1].excl = True
        ldT_cache.clear()

    class Chain:
        def __init__(self, banks):
            self.banks = banks
            self.n = 0

    def use_chain(banks):
        k.tls.chain = Chain(banks)

    def cur_banks():
        ch = getattr(k.tls, "chain", None)
        if ch is None:
            ch = k.tls.chain = Chain(list(PS))
        return ch

    def nps():
        ch = cur_banks()
        b = ch.banks[ch.n % len(ch.banks)]
        ch.n += 1
        return b

    npsb = nps

    def bfv(ps):
        return ps.t[:].bitcast(BF16)

    ldT_cache = {}

    k.dma("pool", ident[:], c_ident, [], [ident])
    k.dma("sp", ident32[:], c_ident, [], [ident32])
    k.dma("sp", ones32[:], c_ones, [], [ones32])
    k.op("dve", [], [epsln], lambda e: e.memset(epsln[:], LN_EPS))
    k.op("dve", [], [epsrms], lambda e: e.memset(epsrms[:], RMS_EPS))

    ldT_cache = {}

    ckpt("consts")
    import os as _os3
    KDBG = bool(_os3.environ.get("KDBG"))

    def tap(name, buf, ap):
        if not KDBG:
            return
        d = nc.dram_tensor("dbg_" + name, list(ap.shape), F32, kind="ExternalOutput").ap()
        k.dma("pool", d, ap, [buf], [])

    def load_T_dma(es, src_rows, ntok, tag, qn="pool"):
        if tag not in ldT_cache:
            ldT_cache[tag] = k.sb(es, "ldT_" + tag, [128, D], BF16)
        tb = ldT_cache[tag]
        k.dma(qn, tb[0:ntok, :], src_rows, [], [tb])

    def load_T(es, src_rows, dst, col0, ntok, tag, qn="pool", dma=True):
        if dma:
            load_T_dma(es, src_rows, ntok, tag, qn)
        tb = ldT_cache[tag]
        ps = npsb()
        pv = bfv(ps)
        for c in range(8):
            k.op("pe", [tb, ident], [ps], lambda e, c=c: e.transpose(
                out=pv[:, c * 128:c * 128 + ntok], in_=tb[0:ntok, c * 128:(c + 1) * 128],
                identity=ident[0:ntok, 0:ntok]))
        k.op("act", [ps], [dst], lambda e: e.copy(
            out=dst[:, :, col0:col0 + ntok],
            in_=pv.rearrange("p (c t) -> p c t", c=8)[:, :, 0:ntok]))

    with ExitStack() as es:
        set_psum(es, 6, 2, "a")
        use_chain(list(PS))
        w_inA = k.sb(es, "w_inA", [128, 8, 2560], BF16)
        wmk = k.sb(es, "wmk", [128, 8, 256], BF16)
        wmv = k.sb(es, "wmv", [128, 8, 256], BF16)
        wgrp = k.sb(es, "wgrp", [128, 2, 128], BF16)
        lbt = k.sb(es, "lbt", [128, 2, 4], F32)
        lbv = k.sb(es, "lbv", [128, 4], F32)
        omlv = k.sb(es, "omlv", [128, 4], F32)
        nomlv = k.sb(es, "nomlv", [128, 4], F32)
        pscale = k.sb(es, "pscale", [128, 2], F32)
        hgam = k.sb(es, "hgam", [128, 4], F32)
        rmask_p = k.sb(es, "rmask_p", [128, TA], F32)
        rmask_s = k.sb(es, "rmask_s", [128, 64], F32)
        cmask_p = k.sb(es, "cmask_p", [128, 128], F32)
        cmask_s = k.sb(es, "cmask_s", [64, 64], F32)
        invcnt = k.sb(es, "invcnt", [128, 2, 16], F32)
        seqm_tm = k.sb(es, "seqm_tm", [64, 16], F32)
        seqm_fm = k.sb(es, "seqm_fm", [128, 16, 64], BF16)

        w_inA_g = [k.alias(w_inA, "w_inA_g%d" % g) for g in range(5)]
        k.dma("pool", wmk[:], w_mem_k.rearrange("(c p) n -> p c n", p=128), [], [wmk])
        k.dma("pool", wmv[:], w_mem_v.rearrange("(c p) n -> p c n", p=128), [], [wmv])
        k.op("dve", [], [wgrp], lambda e: e.memset(wgrp[:], 0.0))
        k.dma("pool", [wgrp[(g % 2) * 64:(g % 2) * 64 + 64, g // 2, (g % 2) * 64:(g % 2) * 64 + 64] for g in range(4)],
              [w_grp[g] for g in range(4)], [], [wgrp])
        with nc.allow_non_contiguous_dma(reason="tiny per-partition scalar tables"):
            k.dma("sp", lbt[:], lbl.rearrange("r (h p) -> p r h", p=128), [], [lbt])
            k.dma("sp", pscale[:], pool_scale.rearrange("(c p) -> p c", p=128), [], [pscale])
            k.dma("sp", hgam[:], hg_g.rearrange("(c p) -> p c", p=128), [], [hgam])
        k.dma("sp", rmask_p[:], c_rmask_p, [], [rmask_p])
        k.dma("sp", rmask_s[:], c_rmask_s, [], [rmask_s])
        k.dma("sp", cmask_p[:], c_cmask_p, [], [cmask_p])
        k.dma("sp", cmask_s[:], c_cmask_s, [], [cmask_s])
        k.dma("sp", invcnt[:], c_invcnt, [], [invcnt])
        k.dma("sp", seqm_tm[:], c_seqm_tm, [], [seqm_tm])
        k.dma("pool", seqm_fm[:], c_seqm_fm, [], [seqm_fm])
        k.op("dve", [lbt], [lbv], lambda e: e.tensor_tensor(out=lbv[:], in0=lbt[:, 0, :], in1=lbt[:, 1, :], op=ALU.subtract))
        k.op("act", [lbv], [lbv], lambda e: e.activation(out=lbv[:], in_=lbv[:], func=AF.Sigmoid))
        k.op("dve", [lbv], [omlv], lambda e: e.tensor_scalar(out=omlv[:], in0=lbv[:], scalar1=-1.0, scalar2=1.0, op0=ALU.mult, op1=ALU.add))
        k.op("dve", [omlv], [nomlv], lambda e: e.tensor_scalar(out=nomlv[:], in0=omlv[:], scalar1=-1.0, scalar2=None, op0=ALU.mult))

        ckpt("a1 weights")
        memT = k.sb(es, "memT", [128, 8, 256], BF16)
        ktp = k.sb(es, "ktp", [128, 4, 256], BF16)
        vpd = k.sb(es, "vpd", [128, 2, 4, 128], BF16)
        mko = k.sb(es, "mko", [128, 2, 512], F32)
        import os as _os
        for mc in range(int(_os.environ.get("LTN", "2"))):
            load_T(es, memp[mc * 128:(mc + 1) * 128, :], memT, mc * 128, 128, "mem%d" % mc)
        ckpt("memT")
        k.op("dve", [], [ktp], lambda e: e.memset(ktp[:], 0.0))
        k.op("dve", [], [vpd], lambda e: e.memset(vpd[:], 0.0))
        STG4 = int(_os.environ.get("STG4", "9"))
        for mc in range(2):
            ps = nps()
            if STG4 < 2:
                continue
            for c in range(8):
                k.op("pe", [memT, wmk], [ps], lambda e, c=c: e.matmul(
                    ps[:, 0:256], lhsT=memT[:, c, mc * 128:(mc + 1) * 128], rhs=wmk[:, c, :], start=(c == 0), stop=(c == 7)))
            for c in range(8):
                k.op("pe", [memT, wmv], [ps], lambda e, c=c: e.matmul(
                    ps[:, 256:512], lhsT=memT[:, c, mc * 128:(mc + 1) * 128], rhs=wmv[:, c, :], start=(c == 0), stop=(c == 7)))
            if STG4 < 3:
                continue
            k.op("act", [ps], [mko], lambda e: e.copy(out=mko[:, mc, :], in_=ps[:, :]))
            if STG4 < 4:
                continue
            for h in range(4):
                k.op("dve", [ps] + ([mko] if _os.environ.get("SERIAL") else []), [vpd], lambda e, h=h: e.tensor_copy(
                    out=vpd[:, mc, h, (h % 2) * 64:(h % 2) * 64 + 64], in_=ps[:, 256 + h * 64:256 + (h + 1) * 64]))
        ckpt("mem TM")
        k.dma("sp", nmk_p.rearrange("(c p) n -> p c n", p=128), mko[:, :, 0:256], [mko], [])
        k.dma("sp", nmv_p.rearrange("(c p) n -> p c n", p=128), mko[:, :, 256:512], [mko], [])
        ckpt("mem out")
        for j in range(2):
            ps = nps()
            for c in range(8):
                k.op("pe", [memT, wmk], [ps], lambda e, c=c: e.matmul(
                    ps[:, 0:256], lhsT=wmk[:, c, j * 128:(j + 1) * 128], rhs=memT[:, c, :], start=(c == 0), stop=(c == 7)))
            for hh in range(2):
                h = 2 * j + hh
                k.op("dve", [ps], [ktp], lambda e, h=h, hh=hh: e.tensor_copy(
                    out=ktp[hh * 64:(hh + 1) * 64, h, :], in_=ps[hh * 64:(hh + 1) * 64, 0:256]))

        ckpt("memkv")
        ub = [k.sb(es, "ub%d" % i, [128, 2, 15 + TA], F32) for i in range(2)]
        sw = [k.sb(es, "sw%d" % i, [128, 2, 15 + TA], F32) for i in range(2)]
        dif = k.sb(es, "dif", [128, 2, TA], BF16)
        class NS_:
            pass
        BB = [NS_(), NS_()]
        for i_, b_ in enumerate(BB):
            b_.xt = k.sb(es, "xTs%d" % i_, [128, 8, TA], BF16)
            b_.yc = k.sb(es, "ycs%d" % i_, [128, 8, TA], BF16)
            b_.yc_a = k.alias(b_.yc, "ycs%d_a" % i_)
            b_.yc_b = k.alias(b_.yc, "ycs%d_b" % i_)
            b_.yc_c = k.alias(b_.yc, "ycs%d_c" % i_)
            b_.sgm = k.sb(es, "sgm%d" % i_, [128, 4, TA], F32)
            b_.qs = k.sb(es, "qs%d" % i_, [128, 4, TA], F32)
            b_.gs = k.sb(es, "gs%d" % i_, [128, 4, TA], F32)
            b_.qcT = k.sb(es, "qcT%d" % i_, [128, 2, TA], BF16)
            b_.sgm_h = [k.alias(b_.sgm, "sgm%d_h%d" % (i_, h_)) for h_ in range(4)]
            b_.qs_h = [k.alias(b_.qs, "qs%d_h%d" % (i_, h_)) for h_ in range(4)]
            b_.gs_h = [k.alias(b_.gs, "gs%d_h%d" % (i_, h_)) for h_ in range(4)]
            b_.vtm = [k.sb(es, "vtm%d_%d" % (i_, t_), [128, 512], BF16) for t_ in range(2)]
        NPF = 8
        sst = [k.sb(es, "sst%d" % i_, [128, 4, 128], F32) for i_ in range(NPF)]
        kst = [k.sb(es, "kst%d" % i_, [128, 2, 256], BF16) for i_ in range(NPF)]
        vst = [k.sb(es, "vst%d" % i_, [128, 2, 256], BF16) for i_ in range(NPF)]
        khm2 = [k.sb(es, "khm%d" % i_, [64, 512], BF16) for i_ in range(2)]
        qms2 = [k.sb(es, "qms%d" % i_, [128, 2, 64], BF16) for i_ in range(2)]
        eb = k.sb(es, "eb", [128, 4, TA], F32)
        eb_h = [k.alias(eb, "eb_h%d" % h_) for h_ in range(4)]
        lf = [k.sb(es, "lf%d" % i, [128, TA], F32) for i in range(2)]
        kk = [k.sb(es, "kk%d" % i, [128, TA], F32) for i in range(2)]
        enb = [k.sb(es, "enb%d" % i, [128, TA], F32) for i in range(2)]
        qtil = k.sb(es, "qtil", [128, 4, TA], BF16)
        ktil = k.sb(es, "ktil", [128, 4, TA], BF16)
        khat = k.sb(es, "khat", [128, 4, TA], BF16)
        khtm = [k.sb(es, "khtm%d" % i, [128, 512], BF16) for i in range(2)]
        scm = [k.sb(es, "scm%d" % i, [128, 4, 128], BF16) for i in range(2)]
        oT = k.sb(es, "oT", [128, 4, TA], F32)
        osq = k.sb(es, "osq", [128, 4, TA], F32)
        rstd = osq
        S32 = k.sb(es, "S32", [128, 4, 128], F32)
        Sbfs = [k.sb(es, "Sbf%d" % i, [128, 4, 128], BF16) for i in range(2)]
        sbi = [0]
        p32 = k.sb(es, "p32", [128, 4, 256], F32)
        pn = k.sb(es, "pn", [128, 4, 256], BF16)
        pT = k.sb(es, "pT", [128, 8, 128], BF16)
        stat = k.sb(es, "stat", [128, 16], F32)
        utm = k.sb(es, "utm", [128, 256], F32)

        def a1_supertile(st_i, tok0, T, sample):
            TT = 64 if sample else 128
            ntt = T // TT
            C = ST if sample else 64
            B = BB[st_i % 2]
            u = ub[st_i % 2]
            up = ub[(st_i + 1) % 2]

            def proj(cc):
                ps = nps()
                for c in range(8):
                    k.op("pe", [B.xt, w_inA_g[cc // 4]], [ps], lambda e, c=c: e.matmul(
                        ps[:, 0:T], lhsT=w_inA[:, c, cc * 128:(cc + 1) * 128], rhs=B.xt[:, c, 0:T], start=(c == 0), stop=(c == 7)))
                return ps

            def c_X():
                cur[0] = B
                src = xs if sample else xp
                for tt in range(ntt):
                    r0 = (tok0 - (SEQ if sample else 0)) + tt * TT
                    load_T(es, src[r0:r0 + TT, :], B.xt, tt * TT, TT, "x%d" % (tt % 2), dma=(st_i != 0))
                k.dma("sp", xT_scr[:, :, tok0:tok0 + T], B.xt[:, :, 0:T], [B.xt], [xscr])
                yield
                if not sample:
                    if tok0 == 0:
                        k.op("pool", [], [u], lambda e: e.memset(u[:, :, 0:15], 0.0))
                    else:
                        k.op("pool", [up], [u], lambda e: e.tensor_copy(out=u[:, :, 0:15], in_=up[:, :, TA:TA + 15]))
                    for c2 in range(2):
                        ps = proj(c2)
                        k.op("act", [ps], [u], lambda e, c2=c2: e.copy(out=u[:, c2, 15:15 + T], in_=ps[:, 0:T]))
                    yield
                for h in range(4):
                    psq = proj(2 + h)
                    k.op("act", [psq], [B.qs_h[h]], lambda e, h=h: e.activation(out=B.qs[:, h, 0:T], in_=psq[:, 0:T], func=AF.Sigmoid))
                    k.op("dve", [psq, B.qs_h[h]], [B.qs_h[h]], lambda e, h=h: e.tensor_tensor(out=B.qs[:, h, 0:T], in0=B.qs[:, h, 0:T], in1=psq[:, 0:T], op=ALU.mult))
                    psf = proj(6 + h)
                    k.op("act", [psf], [B.sgm_h[h]], lambda e, h=h: e.activation(out=B.sgm[:, h, 0:T], in_=psf[:, 0:T], func=AF.Sigmoid))
                    psg = proj(14 + h)
                    k.op("act", [psg], [B.gs_h[h]], lambda e, h=h: e.activation(out=B.gs[:, h, 0:T], in_=psg[:, 0:T], func=AF.Sigmoid))
                    k.op("dve", [psg, B.gs_h[h]], [B.gs_h[h]], lambda e, h=h: e.tensor_tensor(out=B.gs[:, h, 0:T], in0=B.gs[:, h, 0:T], in1=psg[:, 0:T], op=ALU.mult))
                yield
                for tt in range(ntt):
                    ps = nps()
                    for c in range(8):
                        k.op("pe", [B.xt, w_inA_g[2], w_inA_g[3]], [ps], lambda e, c=c, tt=tt: e.matmul(
                            ps[0:TT, :], lhsT=B.xt[:, c, tt * TT:(tt + 1) * TT], rhs=w_inA[:, c, 1280:1792], start=(c == 0), stop=(c == 7)))
                    k.op("act", [ps], [B.vtm[tt]], lambda e, tt=tt: e.copy(out=B.vtm[tt][0:TT, :], in_=ps[0:TT, :]))
                yield
                for j in range(2):
                    ps = proj(18 + j)
                    k.op("dve", [ps], [B.qcT], lambda e, j=j: e.tensor_copy(out=B.qcT[:, j, 0:T], in_=ps[:, 0:T]))
                yield
                return

            sgm, qs, gs, qcT, vtm = B.sgm, B.qs, B.gs, B.qcT, B.vtm
            def c_pool_attn():
                cur[0] = B
                if not sample:
                    L = 15 + T
                    s2, s4 = sw[0], sw[1]
                    k.op("dve", [u], [s2], lambda e: e.tensor_tensor(out=s2[:, :, 0:L - 1], in0=u[:, :, 0:L - 1], in1=u[:, :, 1:L], op=ALU.add))
                    k.op("dve", [s2], [s4], lambda e: e.tensor_tensor(out=s4[:, :, 0:L - 3], in0=s2[:, :, 0:L - 3], in1=s2[:, :, 2:L - 1], op=ALU.add))
                    k.op("dve", [s2, u], [dif], lambda e: e.scalar_tensor_tensor(
                        out=dif[0:64, 0, 0:T], in0=s2[0:64, 0, 14:14 + T], scalar=0.5, in1=u[0:64, 0, 15:15 + T], op0=ALU.mult, op1=ALU.subtract))
                    k.op("dve", [s4, u], [dif], lambda e: e.scalar_tensor_tensor(
                        out=dif[64:128, 0, 0:T], in0=s4[64:128, 0, 12:12 + T], scalar=0.25, in1=u[64:128, 0, 15:15 + T], op0=ALU.mult, op1=ALU.subtract))
                    k.op("dve", [s4, s2], [s2], lambda e: e.tensor_tensor(out=s2[:, 1, 0:L - 7], in0=s4[:, 1, 0:L - 7], in1=s4[:, 1, 4:L - 3], op=ALU.add))
                    k.op("dve", [s2, u], [dif], lambda e: e.scalar_tensor_tensor(
                        out=dif[0:64, 1, 0:T], in0=s2[0:64, 1, 8:8 + T], scalar=0.125, in1=u[0:64, 1, 15:15 + T], op0=ALU.mult, op1=ALU.subtract))
                    k.op("dve", [s2, s4], [s4], lambda e: e.tensor_tensor(out=s4[:, 1, 0:L - 15], in0=s2[:, 1, 0:L - 15], in1=s2[:, 1, 8:L - 7], op=ALU.add))
                    k.op("dve", [s4, u], [dif], lambda e: e.scalar_tensor_tensor(
                        out=dif[64:128, 1, 0:T], in0=s4[64:128, 1, 0:T], scalar=0.0625, in1=u[64:128, 1, 15:15 + T], op0=ALU.mult, op1=ALU.subtract))
                    if tok0 == 0:
                        for (p0, c2, sbuf, off) in ((0, 0, s2, 14), (64, 0, s4, 12), (0, 1, s2, 8), (64, 1, s4, 0)):
                            k.op("dve", [sbuf, invcnt], [sbuf], lambda e, p0=p0, c2=c2, sbuf=sbuf, off=off: e.tensor_tensor(
                                out=sbuf[p0:p0 + 64, c2, off:off + 16], in0=sbuf[p0:p0 + 64, c2, off:off + 16],
                                in1=invcnt[p0:p0 + 64, c2, :], op=ALU.mult))
                            k.op("dve", [sbuf, u], [dif], lambda e, p0=p0, c2=c2, sbuf=sbuf, off=off: e.tensor_tensor(
                                out=dif[p0:p0 + 64, c2, 0:16], in0=sbuf[p0:p0 + 64, c2, off:off + 16],
                                in1=u[p0:p0 + 64, c2, 15:31], op=ALU.subtract))
                    for c2 in range(2):
                        ps = nps()
                        k.op("pe", [dif, wgrp], [ps], lambda e, c2=c2: e.matmul(ps[:, 0:T], lhsT=wgrp[:, c2, :], rhs=dif[:, c2, 0:T], start=True, stop=True))
                        k.op("act", [ps, pscale], [B.yc_a], lambda e, c2=c2: e.activation(
                            out=B.yc[:, c2, 0:T], in_=ps[:, 0:T], func=AF.Copy, scale=pscale[:, c2:c2 + 1]))
                    if tok0 + T == SEQ:
                        ps = nps()
                        for c in range(8):
                            k.op("pe", [B.xt, w_inA_g[0]], [ps], lambda e, c=c: e.matmul(
                                ps[0:15, 0:256], lhsT=B.xt[:, c, T - 15:T], rhs=w_inA[:, c, 0:256], start=(c == 0), stop=(c == 7)))
                        k.op("act", [ps], [utm], lambda e: e.copy(out=utm[0:15, :], in_=ps[0:15, 0:256]))
                        k.dma("sp", npool_p, utm[0:15, :], [utm], [])

                if sample:
                    sample_pool(proj)
                if not sample:
                    for tt in range(ntt):
                        tc0 = tt * TT
                        pss = [nps(), nps()]
                        for h in range(4):
                            k.op("pe", [qcT, ktp], [pss[h // 2]], lambda e, h=h: e.matmul(
                                pss[h // 2][0:TT, (h % 2) * 256:(h % 2) * 256 + 256], lhsT=qcT[:, h // 2, tc0:tc0 + TT], rhs=ktp[:, h, :],
                                start=True, stop=True))
                        for j in range(2):
                            k.op("dve", [pss[j]], [stat], lambda e, j=j: e.tensor_reduce(
                                out=stat[0:TT, 2 * j:2 * j + 2], in_=pss[j][0:TT, :].rearrange("p (h m) -> p h m", h=2), axis=AX.X, op=ALU.max))
                        k.op("dve", [stat], [stat], lambda e: e.tensor_scalar(
                            out=stat[0:TT, 4:8], in0=stat[0:TT, 0:4], scalar1=-0.125, scalar2=None, op0=ALU.mult))
                        for h in range(4):
                            k.op("act", [pss[h // 2], stat], [p32, stat], lambda e, h=h: e.activation(
                                out=p32[0:TT, h, :], in_=pss[h // 2][0:TT, (h % 2) * 256:(h % 2) * 256 + 256], func=AF.Exp,
                                scale=0.125, bias=stat[0:TT, 4 + h:5 + h], accum_out=stat[0:TT, 8 + h:9 + h]))
                        k.op("dve", [stat], [stat], lambda e: e.reciprocal(out=stat[0:TT, 12:16], in_=stat[0:TT, 8:12]))
                        k.op("dve", [p32, stat], [pn], lambda e: e.tensor_tensor(
                            out=pn[0:TT, :, :], in0=p32[0:TT, :, :], in1=stat[0:TT, 12:16].unsqueeze(2).to_broadcast([TT, 4, 256]), op=ALU.mult))
                        ps = npsb()
                        pv = bfv(ps)
                        for h in range(4):
                            for mc in range(2):
                                i = h * 2 + mc
                                k.op("pe", [pn, ident], [ps], lambda e, h=h, mc=mc, i=i: e.transpose(
                                    out=pv[:, i * 128:i * 128 + TT], in_=pn[0:TT, h, mc * 128:(mc + 1) * 128], identity=ident[0:TT, 0:TT]))
                        k.op("act", [ps], [pT], lambda e: e.copy(out=pT[:, :, 0:TT], in_=pv.rearrange("p (i t) -> p i t", i=8)[:, :, 0:TT]))
                        ps_y = nps()
                        for j in range(2):
                            n = 0
                            for hh in range(2):
                                h = 2 * j + hh
                                for mc in range(2):
                                    k.op("pe", [vpd, pT], [ps_y], lambda e, h=h, mc=mc, j=j, n=n: e.matmul(
                                        ps_y[:, j * 128:j * 128 + TT], lhsT=vpd[:, mc, h, :], rhs=pT[:, h * 2 + mc, 0:TT],
                                        start=(n == 0), stop=(n == 3)))
                                    n += 1
                        k.op("dve", [ps_y], [B.yc_c], lambda e: e.tensor_copy(
                            out=B.yc[:, 6:8, tc0:tc0 + TT], in_=ps_y[:, 0:256].rearrange("p (j t) -> p j t", j=2)[:, :, 0:TT]))
                        yield

                if sample:
                    sample_attn()
                yield
            def c_hgrn():
                cur[0] = B
                rmask = rmask_s if sample else rmask_p
                yield
                nch = T // C
                for h in range(4):
                    if h == 2:
                        yield
                    l, kq, en = lf[h % 2], kk[h % 2], enb[h % 2]
                    k.op("act", [B.sgm_h[h], omlv, lbv], [l], lambda e, h=h, l=l: e.activation(
                        out=l[:, 0:T], in_=sgm[:, h, 0:T], func=AF.Ln, scale=omlv[:, h:h + 1], bias=lbv[:, h:h + 1]))
                    k.op("dve", [B.sgm_h[h], omlv, nomlv], [kq], lambda e, h=h, kq=kq: e.tensor_scalar(
                        out=kq[:, 0:T], in0=sgm[:, h, 0:T], scalar1=nomlv[:, h:h + 1], scalar2=omlv[:, h:h + 1], op0=ALU.mult, op1=ALU.add))
                    k.op("dve", [l, rmask], [l], lambda e, l=l: e.tensor_tensor_scan(
                        out=l[:, 0:T], data0=rmask[:, 0:T], data1=l[:, 0:T], initial=0.0, op0=ALU.mult, op1=ALU.add))
                    k.op("act", [l], [eb_h[h]], lambda e, h=h, l=l: e.activation(out=eb[:, h, 0:T], in_=l[:, 0:T], func=AF.Exp))
                    k.op("act", [l], [en], lambda e, l=l, en=en: e.activation(out=en[:, 0:T], in_=l[:, 0:T], func=AF.Exp, scale=-1.0))
                    k.op("pool", [B.qs_h[h], eb_h[h]], [qtil], lambda e, h=h: e.tensor_tensor(out=qtil[:, h, 0:T], in0=qs[:, h, 0:T], in1=eb[:, h, 0:T], op=ALU.mult))
                    k.op("dve", [kq, en], [kq], lambda e, kq=kq, en=en: e.tensor_tensor(out=kq[:, 0:T], in0=kq[:, 0:T], in1=en[:, 0:T], op=ALU.mult))
                    k.op("act", [kq], [ktil], lambda e, h=h, kq=kq: e.copy(out=ktil[:, h, 0:T], in_=kq[:, 0:T]))
                    k.op("dve", [kq, eb_h[h]], [khat], lambda e, h=h, kq=kq: e.tensor_tensor(
                        out=khat[:, h, 0:T].rearrange("p (c s) -> p c s", s=C),
                        in0=kq[:, 0:T].rearrange("p (c s) -> p c s", s=C),
                        in1=eb[:, h, C - 1:T:C].unsqueeze(2).to_broadcast([128, nch, C]), op=ALU.mult))
                yield
                for tt in range(ntt):
                    ps = npsb()
                    pv = bfv(ps)
                    for h in range(4):
                        k.op("pe", [khat, ident], [ps], lambda e, h=h, tt=tt: e.transpose(
                            out=pv[0:TT, h * 128:(h + 1) * 128], in_=khat[:, h, tt * TT:(tt + 1) * TT], identity=ident[:, :]))
                    k.op("act", [ps], [khtm[tt]], lambda e, tt=tt: e.copy(out=khtm[tt][0:TT, :], in_=pv[0:TT, 0:512]))
                yield
                cmask = cmask_s if sample else cmask_p
                for tt in range(ntt):
                    if tt:
                        yield
                    tc0 = tt * TT
                    ps_sc = nps()
                    for h in range(4):
                        k.op("pe", [ktil, qtil], [ps_sc], lambda e, h=h: e.matmul(
                            ps_sc[0:TT, h * 128:h * 128 + TT], lhsT=ktil[:, h, tc0:tc0 + TT], rhs=qtil[:, h, tc0:tc0 + TT], start=True, stop=True))
                    sm = scm[tt % 2]
                    k.op("dve", [ps_sc, cmask], [sm], lambda e, sm=sm: e.tensor_tensor(
                        out=sm[0:TT, :, 0:TT], in0=ps_sc[0:TT, :].rearrange("p (h t) -> p h t", h=4)[:, :, 0:TT],
                        in1=cmask[0:TT, 0:TT].unsqueeze(1).to_broadcast([TT, 4, TT]), op=ALU.mult))
                    ps_o = nps()
                    if not sample:
                        first = (tok0 == 0 and tt == 0)
                        ps_us = [nps(), nps()]
                        for c in range(2):
                            for h in range(4):
                                k.op("pe", [khtm[tt], vtm[tt]], [ps_us[c]], lambda e, h=h, c=c: e.matmul(
                                    ps_us[c][:, h * 128:(h + 1) * 128], lhsT=khtm[tt][c * 64:(c + 1) * 64, h * 128:(h + 1) * 128],
                                    rhs=vtm[tt][c * 64:(c + 1) * 64, h * 128:(h + 1) * 128], start=True, stop=True))
                        for c in range(2):
                            has_inter = not (first and c == 0)
                            Scur = Sbfs[sbi[0] % 2]
                            Snxt = Sbfs[(sbi[0] + 1) % 2]
                            sbi[0] += 1
                            for h in range(4):
                                k.op("pe", [vtm[tt], sm], [ps_o], lambda e, h=h, c=c: e.matmul(
                                    ps_o[:, h * 128 + c * 64:h * 128 + (c + 1) * 64], lhsT=vtm[tt][0:TT, h * 128:(h + 1) * 128],
                                    rhs=sm[0:TT, h, c * 64:(c + 1) * 64], start=True, stop=(not has_inter)))
                                if has_inter:
                                    k.op("pe", [Scur, qtil], [ps_o], lambda e, h=h, c=c: e.matmul(
                                        ps_o[:, h * 128 + c * 64:h * 128 + (c + 1) * 64], lhsT=Scur[:, h, :],
                                        rhs=qtil[:, h, tc0 + c * 64:tc0 + (c + 1) * 64], start=False, stop=True))
                            ps_u = ps_us[c]
                            if first and c == 0:
                                k.op("dve", [ps_u], [S32], lambda e: e.tensor_copy(out=S32[:], in_=ps_u[:, :].rearrange("p (h e) -> p h e", h=4)))
                            else:
                                cl = tc0 + c * 64 + 63
                                k.op("dve", [S32] + eb_h, [S32], lambda e, cl=cl: e.tensor_tensor(
                                    out=S32[:], in0=S32[:], in1=eb[:, :, cl:cl + 1].to_broadcast([128, 4, 128]), op=ALU.mult))
                                k.op("dve", [S32, ps_u], [S32], lambda e: e.tensor_tensor(
                                    out=S32[:], in0=S32[:], in1=ps_u[:, :].rearrange("p (h e) -> p h e", h=4), op=ALU.add))
                            k.op("dve", [S32], [Snxt], lambda e: e.tensor_copy(out=Snxt[:], in_=S32[:]))
                    if sample:
                        sample_hgrn(ps_o, sm)
                    k.op("act", [ps_o], [oT], lambda e: e.copy(
                        out=oT[:, :, tc0:tc0 + TT], in_=ps_o[:, :].rearrange("p (h t) -> p h t", h=4)[:, :, 0:TT]))
                if (not sample) and tok0 + T == SEQ:
                    k.dma("sp", nhg_p.rearrange("h d e -> d h e"), S32[:], [S32], [])
                if tok0 == 0:
                    tap("oT", oT, oT[:, :, 0:T])
                    tap("qtil", qtil, qtil[:, :, 0:T])
                    tap("scm", scm[1], scm[1][:, :, :])
                    tap("gs", gs, gs[:, :, 0:T])
                yield
                k.op("act", [oT], [osq], lambda e: e.activation(out=osq[:, :, 0:T], in_=oT[:, :, 0:T], func=AF.Square))
                hp = min(4, 512 // T)
                for h0 in range(0, 4, hp):
                    ps = nps()
                    for hh in range(hp):
                        h = h0 + hh
                        k.op("pe", [osq, ones32], [ps], lambda e, h=h, hh=hh: e.matmul(
                            ps[:, hh * T:(hh + 1) * T], lhsT=ones32[:, :], rhs=osq[:, h, 0:T], start=True, stop=True))
                    k.op("act", [ps, epsrms], [rstd], lambda e, h0=h0: e.activation(
                        out=rstd[:, h0:h0 + hp, 0:T], in_=ps[:, 0:hp * T].rearrange("p (h t) -> p h t", h=hp), func=AF.Ln, bias=epsrms[:, 0:1]))
                k.op("act", [rstd], [rstd], lambda e: e.activation(out=rstd[:, :, 0:T], in_=rstd[:, :, 0:T], func=AF.Exp, scale=-0.5))
                k.op("dve", [oT, rstd], [oT], lambda e: e.tensor_tensor(out=oT[:, :, 0:T], in0=oT[:, :, 0:T], in1=rstd[:, :, 0:T], op=ALU.mult))
                for h in range(4):
                    k.op("dve", [oT, B.gs_h[h], hgam], [B.yc_b], lambda e, h=h: e.scalar_tensor_tensor(
                        out=B.yc[:, 2 + h, 0:T], in0=oT[:, h, 0:T], scalar=hgam[:, h:h + 1], in1=gs[:, h, 0:T], op0=ALU.mult, op1=ALU.mult))

                yield
                yield
            def c_fin():
                k.dma("sp", yc_scr[:, :, tok0:tok0 + T], B.yc[:, :, 0:T], [B.yc, B.yc_a, B.yc_b, B.yc_c], [yscr])
            return c_X, c_pool_attn, c_hgrn, c_fin

        class _Cur:
            def __getitem__(self, i):
                return k.tls.curB

            def __setitem__(self, i, v):
                k.tls.curB = v
        cur = _Cur()
        def sample_pool(proj):
            T = NS * ST
            L = 15 + ST
            us = p32[:].rearrange("p a b -> p (a b)")[:, 0:2 * NS * L].rearrange("p (c b l) -> p c b l", c=2, b=NS)
            s2 = mko[:].rearrange("p a b -> p (a b)")[:, 0:2 * NS * L].rearrange("p (c b l) -> p c b l", c=2, b=NS)
            s4 = sw[0][:].rearrange("p a b -> p (a b)")[:, 0:2 * NS * 16].rearrange("p (c b l) -> p c b l", c=2, b=NS)
            for half in range(2):
                k.dma("sp", utm[0:120, :], spool[half * 8:(half + 1) * 8].rearrange("b r c -> (b r) c"), [], [utm])
                for c2 in range(2):
                    ps = nps()
                    k.op("pe", [utm, ident32], [ps], lambda e, c2=c2: e.transpose(
                        out=ps[:, 0:120], in_=utm[0:120, c2 * 128:(c2 + 1) * 128], identity=ident32[0:120, 0:120]))
                    k.op("dve", [ps], [p32], lambda e, c2=c2, half=half: e.tensor_copy(
                        out=us[:, c2, half * 8:(half + 1) * 8, 0:15], in_=ps[:, 0:120].rearrange("p (b r) -> p b r", r=15)))
            for c2 in range(2):
                ps = proj(c2)
                k.op("act", [ps], [p32], lambda e, c2=c2: e.copy(out=us[:, c2, :, 15:L], in_=ps[:, 0:T].rearrange("p (b t) -> p b t", t=ST)))
            k.op("dve", [p32], [mko], lambda e: e.tensor_tensor(out=s2[:, :, :, 0:L - 1], in0=us[:, :, :, 0:L - 1], in1=us[:, :, :, 1:L], op=ALU.add))
            k.op("dve", [mko], [sw[0]], lambda e: e.tensor_tensor(out=s4[:, :, :, 0:L - 3], in0=s2[:, :, :, 0:L - 3], in1=s2[:, :, :, 2:L - 1], op=ALU.add))
            dv = dif[:, :, 0:T].rearrange("p c (b t) -> p c b t", t=ST)
            k.op("dve", [mko, p32], [dif], lambda e: e.scalar_tensor_tensor(
                out=dv[0:64, 0], in0=s2[0:64, 0, :, 14:14 + ST], scalar=0.5, in1=us[0:64, 0, :, 15:L], op0=ALU.mult, op1=ALU.subtract))
            k.op("dve", [sw[0], p32], [dif], lambda e: e.scalar_tensor_tensor(
                out=dv[64:128, 0], in0=s4[64:128, 0, :, 12:12 + ST], scalar=0.25, in1=us[64:128, 0, :, 15:L], op0=ALU.mult, op1=ALU.subtract))
            k.op("dve", [sw[0], mko], [mko], lambda e: e.tensor_tensor(out=s2[:, 1, :, 0:L - 7], in0=s4[:, 1, :, 0:L - 7], in1=s4[:, 1, :, 4:L - 3], op=ALU.add))
            k.op("dve", [mko, p32], [dif], lambda e: e.scalar_tensor_tensor(
                out=dv[0:64, 1], in0=s2[0:64, 1, :, 8:8 + ST], scalar=0.125, in1=us[0:64, 1, :, 15:L], op0=ALU.mult, op1=ALU.subtract))
            k.op("dve", [mko, sw[0]], [sw[0]], lambda e: e.tensor_tensor(out=s4[:, 1, :, 0:L - 15], in0=s2[:, 1, :, 0:L - 15], in1=s2[:, 1, :, 8:L - 7], op=ALU.add))
            k.op("dve", [sw[0], p32], [dif], lambda e: e.scalar_tensor_tensor(
                out=dv[64:128, 1], in0=s4[64:128, 1, :, 0:ST], scalar=0.0625, in1=us[64:128, 1, :, 15:L], op0=ALU.mult, op1=ALU.subtract))
            for c2 in range(2):
                ps = nps()
                k.op("pe", [dif, wgrp], [ps], lambda e, c2=c2: e.matmul(ps[:, 0:T], lhsT=wgrp[:, c2, :], rhs=dif[:, c2, 0:T], start=True, stop=True))
                k.op("act", [ps, pscale], [cur[0].yc_a], lambda e, c2=c2: e.activation(
                    out=cur[0].yc[:, c2, 0:T], in_=ps[:, 0:T], func=AF.Copy, scale=pscale[:, c2:c2 + 1]))
            k.dma("sp", npool_s[:, 0:11, :], spool[:, 4:15, :], [], [utm])
            ps = nps()
            for c in range(8):
                k.op("pe", [cur[0].xt, w_inA_g[0]], [ps], lambda e, c=c: e.matmul(
                    ps[0:T, 0:256], lhsT=cur[0].xt[:, c, 0:T], rhs=w_inA[:, c, 0:256], start=(c == 0), stop=(c == 7)))
            k.op("act", [ps], [utm], lambda e: e.copy(out=utm[0:T, :], in_=ps[0:T, 0:256]))
            k.dma("sp", [npool_s[:, 11 + t, :] for t in range(ST)], [utm[t:T:ST, :] for t in range(ST)], [utm], [])

        def sample_hgrn(ps_o, sm):
            T = NS * ST
            vtm = cur[0].vtm
            k.op("pool", [], [sm], lambda e: e.memset(sm[64:128, :, :], 0.0))
            for j in range(NS):
                Sb = sst[j % NPF]
                Sv = Sb[:]
                Sbf_j = Sbfs[j % 2]
                km = khm2[j % 2]
                k.op("act", [Sb], [Sbf_j], lambda e: e.copy(out=Sbf_j[:], in_=Sv))
                for h in range(4):
                    k.op("pe", [vtm[0], sm], [ps_o], lambda e, h=h: e.matmul(
                        ps_o[:, h * 128 + ST * j:h * 128 + ST * (j + 1)], lhsT=vtm[0][:, h * 128:(h + 1) * 128],
                        rhs=sm[:, h, ST * j:ST * (j + 1)], start=True, stop=False))
                    k.op("pe", [Sbf_j, qtil], [ps_o], lambda e, h=h: e.matmul(
                        ps_o[:, h * 128 + ST * j:h * 128 + ST * (j + 1)], lhsT=Sbf_j[:, h, :],
                        rhs=qtil[:, h, ST * j:ST * (j + 1)], start=False, stop=True))
                k.op("dve", [khtm[0], seqm_tm], [km], lambda e: e.tensor_scalar(
                    out=km[0:T, :], in0=khtm[0][0:T, :], scalar1=seqm_tm[:, j:j + 1], scalar2=None, op0=ALU.mult))
                cands_ = [b_ for b_ in cur_banks().banks if b_ is not ps_o]
                ps_u = cands_[j % len(cands_)]
                for h in range(4):
                    k.op("pe", [km, vtm[0]], [ps_u], lambda e, h=h: e.matmul(
                        ps_u[:, h * 128:(h + 1) * 128], lhsT=km[0:T, h * 128:(h + 1) * 128], rhs=vtm[0][0:T, h * 128:(h + 1) * 128],
                        start=True, stop=True))
                cl = ST * j + ST - 1
                k.op("dve", [Sb] + eb_h, [Sb], lambda e: e.tensor_tensor(
                    out=Sv, in0=Sv, in1=eb[:, :, cl:cl + 1].to_broadcast([128, 4, 128]), op=ALU.mult))
                k.op("dve", [Sb, ps_u], [Sb], lambda e: e.tensor_tensor(
                    out=Sv, in0=Sv, in1=ps_u[:, :].rearrange("p (h e) -> p h e", h=4), op=ALU.add))
                k.dma("sp", nhg_s[j].rearrange("h d e -> d h e"), Sv, [Sb], [])
                if j + NPF < NS:
                    k.dma("sp", Sv, shg[j + NPF].rearrange("h d e -> d h e"), [], [Sb])

        def sample_attn():
            T = NS * ST
            qcT = cur[0].qcT
            ktps = [ktp, pT]
            ktv = [ktp[:], pT[:].rearrange("p (h a) t -> p h (a t)", h=4)]
            vps = [vpd[:], memT[:, 0:4, :].rearrange("p a b -> p (a b)").rearrange("p (m h c) -> p m h c", m=2, h=4)]
            vpb = [vpd, memT]
            k.op("pool", [], [pT], lambda e: e.memset(pT[:], 0.0))
            k.op("pool", [], [memT], lambda e: e.memset(memT[:], 0.0))
            bk_ = cur_banks().banks
            pss = bk_[0:4]
            for j in range(NS):
                sgb = kst[j % NPF]
                sg_ = sgb[:]
                psb = bk_[4]
                pv = bfv(psb)
                for jj in range(2):
                    for mc in range(2):
                        k.op("pe", [sgb, ident], [psb], lambda e, jj=jj, mc=mc: e.transpose(
                            out=pv[:, (jj * 2 + mc) * 128:(jj * 2 + mc + 1) * 128], in_=sg_[:, mc, jj * 128:(jj + 1) * 128], identity=ident[:, :]))
                if j + NPF < NS:
                    k.dma("pool", sg_, cmk[j + NPF].rearrange("(c p) n -> p c n", p=128), [], [sgb])
                kt = ktv[j % 2]
                for h in range(4):
                    hh = h % 2
                    k.op("act", [psb], [ktps[j % 2]], lambda e, h=h, hh=hh: e.copy(
                        out=kt[hh * 64:(hh + 1) * 64, h, :], in_=pv[hh * 64:(hh + 1) * 64, (h // 2) * 256:(h // 2) * 256 + 256]))
                qmb = qms2[j % 2]
                qm = qmb[:]
                k.op("dve", [qcT, seqm_fm], [qmb], lambda e: e.tensor_tensor(
                    out=qm, in0=qcT[:, :, 0:T], in1=seqm_fm[:, j, :].unsqueeze(1).to_broadcast([128, 2, T]), op=ALU.mult))
                for h in range(4):
                    k.op("pe", [qmb, ktps[j % 2]], [pss[h]], lambda e, h=h: e.matmul(
                        pss[h][0:T, 0:256], lhsT=qm[:, h // 2, :], rhs=kt[:, h, :], start=(j == 0), stop=(j == NS - 1)))
            for h in range(4):
                k.op("dve", [pss[h]], [stat], lambda e, h=h: e.tensor_reduce(
                    out=stat[0:T, h:h + 1], in_=pss[h][0:T, 0:256], axis=AX.X, op=ALU.max))
            k.op("dve", [stat], [stat], lambda e: e.tensor_scalar(
                out=stat[0:T, 4:8], in0=stat[0:T, 0:4], scalar1=-0.125, scalar2=None, op0=ALU.mult))
            for h in range(4):
                k.op("act", [pss[h], stat], [p32, stat], lambda e, h=h: e.activation(
                    out=p32[0:T, h, :], in_=pss[h][0:T, 0:256], func=AF.Exp,
                    scale=0.125, bias=stat[0:T, 4 + h:5 + h], accum_out=stat[0:T, 8 + h:9 + h]))
            k.op("dve", [stat], [stat], lambda e: e.reciprocal(out=stat[0:T, 12:16], in_=stat[0:T, 8:12]))
            k.op("dve", [p32, stat], [pn], lambda e: e.tensor_tensor(
                out=pn[0:T, :, :], in0=p32[0:T, :, :], in1=stat[0:T, 12:16].unsqueeze(2).to_broadcast([T, 4, 256]), op=ALU.mult))
            psb = bk_[4]
            pv = bfv(psb)
            for h in range(4):
                for mc in range(2):
                    i = h * 2 + mc
                    k.op("pe", [pn, ident], [psb], lambda e, h=h, mc=mc, i=i: e.transpose(
                        out=pv[:, i * 128:i * 128 + T], in_=pn[0:T, h, mc * 128:(mc + 1) * 128], identity=ident[0:T, 0:T]))
            k.op("act", [psb], [pT], lambda e: e.copy(out=pT[:, :, 0:T], in_=pv.rearrange("p (i t) -> p i t", i=8)[:, :, 0:T]))
            ps_y = bk_[0]
            for j in range(NS):
                sgb = vst[j % NPF]
                sg_ = sgb[:]
                vp = vps[j % 2]
                for mc in range(2):
                    k.op("dve", [sgb], [vpb[j % 2]], lambda e, mc=mc: e.tensor_copy(
                        out=vp[:, mc, 0:4:2, 0:64], in_=sg_[:, mc, :].rearrange("p (h d) -> p h d", h=4)[:, 0:4:2, :]))
                    k.op("act", [sgb], [vpb[j % 2]], lambda e, mc=mc: e.copy(
                        out=vp[:, mc, 1:4:2, 64:128], in_=sg_[:, mc, :].rearrange("p (h d) -> p h d", h=4)[:, 1:4:2, :]))
                if j + NPF < NS:
                    k.dma("pool", sg_, cmv[j + NPF].rearrange("(c p) n -> p c n", p=128), [], [sgb])
                for jj in range(2):
                    n = 0
                    for hh in range(2):
                        h = 2 * jj + hh
                        for mc in range(2):
                            k.op("pe", [vpb[j % 2], pT], [ps_y], lambda e, h=h, mc=mc, jj=jj, n=n: e.matmul(
                                ps_y[:, jj * 128 + ST * j:jj * 128 + ST * (j + 1)], lhsT=vp[:, mc, h, :],
                                rhs=pT[:, h * 2 + mc, ST * j:ST * (j + 1)], start=(n == 0), stop=(n == 3)))
                            n += 1
            k.op("dve", [ps_y], [cur[0].yc_c], lambda e: e.tensor_copy(
                out=cur[0].yc[:, 6:8, 0:T], in_=ps_y[:, 0:256].rearrange("p (j t) -> p j t", j=2)[:, :, 0:T]))

        tiles = [(i_, i_ * TA, TA, False) for i_ in range(SEQ // TA)] + [(SEQ // TA, SEQ, NS * ST, True)]
        chains = [a1_supertile(*tl) for tl in tiles]

        def runner(gen_fn, banks):
            def f():
                use_chain(banks)
                for _ in gen_fn():
                    pass
            return f
        BX, BP, BH = [PS[0], PS[1]], [PS[2], PS[3], PS[4]], [PS[5], PS[6], PS[7]]
        for tt_ in range(2):
            load_T_dma(es, xp[tt_ * 128:(tt_ + 1) * 128, :], 128, "x%d" % tt_)
        for g in (0, 1, 3, 2, 4):
            k.dma("pool", w_inA[:, :, g * 512:(g + 1) * 512], w_in[:, g * 512:(g + 1) * 512].rearrange("(c p) n -> p c n", p=128), [], [w_inA_g[g]])
        k.rr.run([runner(chains[0][0], BX)])
        for j_ in range(NPF):
            k.dma("sp", sst[j_][:], shg[j_].rearrange("h d e -> d h e"), [], [sst[j_]])
            k.dma("pool", kst[j_][:], cmk[j_].rearrange("(c p) n -> p c n", p=128), [], [kst[j_]])
            k.dma("pool", vst[j_][:], cmv[j_].rearrange("(c p) n -> p c n", p=128), [], [vst[j_]])
        for i_, tl in enumerate(tiles):
            last = (i_ + 1 == len(tiles))
            fns = [runner(chains[i_][2], BH), runner(chains[i_][1], (BX + BP) if last else BP)]
            if not last:
                fns.append(runner(chains[i_ + 1][0], BX))
            k.rr.run(fns, [2, 1, 1][:len(fns)])
            chains[i_][3]()
            ckpt("a1 st%d" % i_)
        k.barrier()

    def layer_norm(es_, res, TT, lng, lnbb, stat, outb):
        k.op("dve", [res], [stat], lambda e: e.bn_stats(out=stat[0:TT, 0:6], in_=res[0:TT, 0:512]))
        k.op("dve", [res], [stat], lambda e: e.bn_stats(out=stat[0:TT, 6:12], in_=res[0:TT, 512:1024]))
        k.op("dve", [stat], [stat], lambda e: e.bn_aggr(out=stat[0:TT, 12:14], in_=stat[0:TT, 0:12]))
        k.op("act", [stat, epsln], [stat], lambda e: e.activation(
            out=stat[0:TT, 14:15], in_=stat[0:TT, 13:14], func=AF.Sqrt, bias=epsln[0:TT, 0:1]))
        k.op("dve", [stat], [stat], lambda e: e.reciprocal(out=stat[0:TT, 15:16], in_=stat[0:TT, 14:15]))
        k.op("dve", [res, stat, lng], [res], lambda e: e.scalar_tensor_tensor(
            out=res[0:TT, :], in0=res[0:TT, :], scalar=stat[0:TT, 12:13], in1=lng[0:TT, :], op0=ALU.subtract, op1=ALU.mult))
        k.op("dve", [res, stat, lnbb], [outb], lambda e: e.scalar_tensor_tensor(
            out=outb[0:TT, :], in0=res[0:TT, :], scalar=stat[0:TT, 15:16], in1=lnbb[0:TT, :], op0=ALU.mult, op1=ALU.add))

    with ExitStack() as es:
        set_psum(es, 8, 0, "b")
        use_chain(list(PS))
        w_inG = k.sb(es, "w_inG", [128, 8, 3072], BF16)
        w_br = k.sb(es, "w_br", [128, 8, D], BF16)
        w_o = k.sb(es, "w_o", [128, 8, D], BF16)
        lng = k.sb(es, "lng", [128, D], F32)
        lnbb = k.sb(es, "lnbb", [128, D], F32)
        NG = 4
        w_inG_g = [k.alias(w_inG, "w_inG_g%d" % g) for g in range(NG)]
        w_br_g = [k.alias(w_br, "w_br_g%d" % g) for g in range(NG)]
        for g in range(NG):
            cs = slice(g * 256, (g + 1) * 256)
            k.dma("pool", [w_br[:, 0:2, cs], w_br[:, 2:6, cs], w_br[:, 6:8, cs]],
                  [w_br_pool[:, cs].rearrange("(c p) n -> p c n", p=128), w_br_hg[:, cs].rearrange("(c p) n -> p c n", p=128),
                   w_br_mem[:, cs].rearrange("(c p) n -> p c n", p=128)], [], [w_br_g[g]])
            k.dma("pool", [w_inG[:, :, b * 1024 + g * 256:b * 1024 + (g + 1) * 256] for b in range(3)],
                  [w_in[:, 2560 + b * 1024 + g * 256:2560 + b * 1024 + (g + 1) * 256].rearrange("(c p) n -> p c n", p=128) for b in range(3)],
                  [], [w_inG_g[g]])
        k.dma("pool", w_o[:], w_out.rearrange("(c p) n -> p c n", p=128), [], [w_o])
        xt2 = [k.sb(es, "xt2_%d" % i, [128, 8, TB], BF16) for i in range(2)]
        yc2 = [k.sb(es, "yc2_%d" % i, [128, 8, TB], BF16) for i in range(2)]
        k.dma("sp", lng[:], ln1_g.partition_broadcast(128), [], [lng])
        k.dma("sp", lnbb[:], ln1_b.partition_broadcast(128), [], [lnbb])
        x32 = [k.sb(es, "x32_%d" % i, [128, D], F32) for i in range(2)]
        gsg = [k.sb(es, "gsg%d" % i, [128, TB], F32) for i in range(6)]
        t1 = [k.sb(es, "t1_%d" % i, [128, TB], F32) for i in range(2)]
        t2 = [k.sb(es, "t2_%d" % i, [128, TB], F32) for i in range(2)]
        mTs = [k.sb(es, "mT%d" % i, [128, 8, TB], BF16) for i in range(2)]
        mTs_c = [[k.alias(mTs[i], "mT%d_c%d" % (i, c_)) for c_ in range(8)] for i in range(2)]
        res = [k.sb(es, "res%d" % i, [128, D], F32) for i in range(2)]
        hout = [k.sb(es, "hout%d" % i, [128, D], F32) for i in range(2)]
        stat2 = [k.sb(es, "stat2_%d" % i, [128, 16], F32) for i in range(2)]
        cnt = [0, 0]
        BRC = ((0, 2), (2, 6), (6, 8))

        def a2_load(idx, tok0, T):
            k.dma("sp", xt2[idx % 2][:, :, 0:T], xT_scr[:, :, tok0:tok0 + T], [xscr], [xt2[idx % 2]])
            k.dma("sp", yc2[idx % 2][:, :, 0:T], yc_scr[:, :, tok0:tok0 + T], [yscr], [yc2[idx % 2]])

        def a2_jloop(idx, tok0, T, sample):
            xt_, yc_ = xt2[idx % 2], yc2[idx % 2]
            mT = mTs[idx % 2]
            for j in range(8):
                g3 = [gsg[(cnt[0] % 2) * 3 + b] for b in range(3)]
                ta, tb2 = t1[cnt[0] % 2], t2[cnt[0] % 2]
                cnt[0] += 1
                for b in range(3):
                    ps = nps()
                    for c in range(8):
                        k.op("pe", [xt_, w_inG_g[j // 2]], [ps], lambda e, c=c: e.matmul(
                            ps[:, 0:T], lhsT=w_inG[:, c, b * 1024 + j * 128:b * 1024 + (j + 1) * 128], rhs=xt_[:, c, 0:T],
                            start=(c == 0), stop=(c == 7)))
                    k.op("act", [ps], [g3[b]], lambda e, b=b: e.activation(out=g3[b][:, 0:T], in_=ps[:, 0:T], func=AF.Sigmoid))
                psb3 = []
                for b, (c0, c1) in enumerate(BRC):
                    ps = nps()
                    for c in range(c0, c1):
                        k.op("pe", [yc_, w_br_g[j // 2]], [ps], lambda e, c=c: e.matmul(
                            ps[:, 0:T], lhsT=w_br[:, c, j * 128:(j + 1) * 128], rhs=yc_[:, c, 0:T], start=(c == c0), stop=(c == c1 - 1)))
                    psb3.append(ps)
                k.op("dve", [g3[0], psb3[0]], [ta], lambda e: e.tensor_tensor(out=ta[:, 0:T], in0=g3[0][:, 0:T], in1=psb3[0][:, 0:T], op=ALU.mult))
                k.op("dve", [g3[1], psb3[1]], [tb2], lambda e: e.tensor_tensor(out=tb2[:, 0:T], in0=g3[1][:, 0:T], in1=psb3[1][:, 0:T], op=ALU.mult))
                k.op("pool", [ta, tb2], [ta], lambda e: e.tensor_tensor(out=ta[:, 0:T], in0=ta[:, 0:T], in1=tb2[:, 0:T], op=ALU.add))
                k.op("dve", [g3[2], psb3[2]], [tb2], lambda e: e.tensor_tensor(out=tb2[:, 0:T], in0=g3[2][:, 0:T], in1=psb3[2][:, 0:T], op=ALU.mult))
                k.op("pool", [ta, tb2], [mTs_c[idx % 2][j]], lambda e: e.tensor_tensor(out=mT[:, j, 0:T], in0=ta[:, 0:T], in1=tb2[:, 0:T], op=ALU.add))

        def a2_tail(idx, tok0, T, sample):
            TT = 64 if sample else 128
            ntt = T // TT
            mT = mTs[idx % 2]
            for tt in range(ntt):
                xb = x32[cnt[1] % 2]
                rb = res[cnt[1] % 2]
                hb = hout[cnt[1] % 2]
                sb2 = stat2[cnt[1] % 2]
                cnt[1] += 1
                r0 = tok0 + tt * TT
                src = xs[r0 - SEQ:r0 - SEQ + TT, :] if sample else xp[r0:r0 + TT, :]
                k.dma("sp", xb[0:TT, :], src, [], [xb])
                for half in range(2):
                    ps = nps()
                    for c in range(8):
                        k.op("pe", [mTs_c[idx % 2][c], w_o], [ps], lambda e, c=c: e.matmul(
                            ps[0:TT, :], lhsT=mT[:, c, tt * TT:(tt + 1) * TT], rhs=w_o[:, c, half * 512:(half + 1) * 512],
                            start=(c == 0), stop=(c == 7)))
                    k.op("dve", [xb, ps], [rb], lambda e, half=half: e.scalar_tensor_tensor(
                        out=rb[0:TT, half * 512:(half + 1) * 512], in0=xb[0:TT, half * 512:(half + 1) * 512], scalar=ALPHA,
                        in1=ps[0:TT, :], op0=ALU.mult, op1=ALU.add))
                layer_norm(es, rb, TT, lng, lnbb, sb2, hb)
                k.dma("pool", h_scr[r0:r0 + TT, :], hb[0:TT, :], [hb], [hscr])

        tiles2 = [(i_, i_ * TB, TB, False) for i_ in range(SEQ // TB)] + [(SEQ // TB, SEQ, NS * ST, True)]
        a2_load(0, tiles2[0][1], tiles2[0][2])
        a2_load(1, tiles2[1][1], tiles2[1][2])

        def chain(fn, banks, *a):
            def f():
                use_chain(banks)
                fn(*a)
            return f
        BJ, BT = list(PS[0:6]), list(PS[6:8])
        k.rr.run([chain(a2_jloop, BJ, *tiles2[0])])
        for i_, tl in enumerate(tiles2):
            fns = [chain(a2_tail, BT, *tl)]
            if i_ + 1 < len(tiles2):
                fns.append(chain(a2_jloop, BJ, *tiles2[i_ + 1]))
            k.rr.run(fns, [1, 3][:len(fns)])
            if i_ + 2 < len(tiles2):
                a2_load(i_ + 2, tiles2[i_ + 2][1], tiles2[i_ + 2][2])
            ckpt("a2 st%d" % i_)
        k.barrier()
    esA.close()

    with ExitStack() as es:
        set_psum(es, 7, 1, "c")
        use_chain(list(PS))
        wg = k.sb(es, "wg", [128, 8, DFF], BF16)
        wu = k.sb(es, "wu", [128, 8, DFF], BF16)
        wd = k.sb(es, "wd", [128, NFF, D], BF16)
        lng = k.sb(es, "lng2", [128, D], F32)
        lnbb = k.sb(es, "lnbb2", [128, D], F32)
        cw = k.sb(es, "cw", [128, 3, NFF], F32)
        cb = k.sb(es, "cb", [128, NFF], F32)
        hTs = [k.sb(es, "hT%d" % i, [128, 8, TB], BF16) for i in range(2)]
        for tt_ in range(TB // 128):
            load_T(es, h_scr[tt_ * 128:(tt_ + 1) * 128, :], hTs[0], tt_ * 128, 128, "h%d" % (tt_ % 2))
        FG = [(0, 2), (2, 6), (6, 14), (14, NFF)]
        wg_g = [k.alias(wg, "wg_g%d" % g) for g in range(len(FG))]
        wu_g = [k.alias(wu, "wu_g%d" % g) for g in range(len(FG))]
        ffg = {}
        for g, (a_, b_) in enumerate(FG):
            for c_ in range(a_, b_):
                ffg[c_] = g
            k.dma("pool", wg[:, :, a_ * 128:b_ * 128], w_gate[:, a_ * 128:b_ * 128].rearrange("(c p) n -> p c n", p=128), [], [wg_g[g]])
            k.dma("pool", wu[:, :, a_ * 128:b_ * 128], w_up[:, a_ * 128:b_ * 128].rearrange("(c p) n -> p c n", p=128), [], [wu_g[g]])
        k.dma("pool", wd[:], w_down.rearrange("(c p) n -> p c n", p=128), [], [wd])
        k.dma("sp", lng[:], ln2_g.partition_broadcast(128), [], [lng])
        k.dma("sp", lnbb[:], ln2_b.partition_broadcast(128), [], [lnbb])
        with nc.allow_non_contiguous_dma(reason="per-partition conv taps"):
            k.dma("sp", cw[:], conv_w.rearrange("j (c p) -> p j c", p=128), [], [cw])
            k.dma("sp", cb[:], conv_b.rearrange("(c p) -> p c", p=128), [], [cb])
        ab = [k.sb(es, "ab%d" % i, [128, 2 + TB], F32) for i in range(2)]
        cbuf = [k.sb(es, "cbuf%d" % i, [128, TB], F32) for i in range(2)]
        halo = k.sb(es, "halo", [128, NFF, 2], F32)
        actT = k.sb(es, "actT", [128, NFF, TB], BF16)
        actT_c = [k.alias(actT, "actT_c%d" % c_) for c_ in range(NFF)]
        res = [k.sb(es, "resb%d" % i, [128, D], F32) for i in range(2)]
        stat2 = [k.sb(es, "stat3_%d" % i, [128, 16], F32) for i in range(2)]
        ctm = k.sb(es, "ctm", [64, 512], F32)
        sab = k.sb(es, "sab", [128, NFF, NS * 2], F32)
        cntb = [0, 0]
        k.op("pool", [], [halo], lambda e: e.memset(halo[:], 0.0))

        def b_supertile(idx, tok0, T, sample, nxt):
            TT = 64 if sample else 128
            ntt = T // TT
            hT = hTs[idx % 2]
            if sample:
                NB, TL = NS, ST
            else:
                NB, TL = 1, T
            L = TL + 2
            pend = []
            if nxt is not None:
                n_tok0, n_T, n_sample = nxt
                n_TT = 64 if n_sample else 128
                pend = [(n_tok0 + t_ * n_TT, t_, n_TT) for t_ in range(n_T // n_TT)]

            def pf_dma(i):
                if i < len(pend):
                    r0_, t_, n_ = pend[i]
                    load_T_dma(es, h_scr[r0_:r0_ + n_, :], n_, "h%d" % (t_ % 2))

            def pf_tr(i):
                if i < len(pend):
                    r0_, t_, n_ = pend[i]
                    load_T(es, h_scr[r0_:r0_ + n_, :], hTs[(idx + 1) % 2], t_ * n_, n_, "h%d" % (t_ % 2), dma=False)
            pf_dma(0)
            pf_dma(1)
            for c in range(NFF):
                a_ = ab[cntb[0] % 2]
                cb_ = cbuf[cntb[0] % 2]
                cntb[0] += 1
                ps_a = nps()
                for kk_ in range(8):
                    k.op("pe", [hT, wg_g[ffg[c]]], [ps_a], lambda e, kk_=kk_: e.matmul(
                        ps_a[:, 0:T], lhsT=wg[:, kk_, c * 128:(c + 1) * 128], rhs=hT[:, kk_, 0:T], start=(kk_ == 0), stop=(kk_ == 7)))
                ps_u = nps()
                for kk_ in range(8):
                    k.op("pe", [hT, wu_g[ffg[c]]], [ps_u], lambda e, kk_=kk_: e.matmul(
                        ps_u[:, 0:T], lhsT=wu[:, kk_, c * 128:(c + 1) * 128], rhs=hT[:, kk_, 0:T], start=(kk_ == 0), stop=(kk_ == 7)))
                av = a_[:, 0:NB * L].rearrange("p (b l) -> p b l", b=NB)
                cv = cb_[:, 0:T].rearrange("p (b l) -> p b l", b=NB)
                pav = ps_a[:, 0:T].rearrange("p (b l) -> p b l", b=NB)
                if sample:
                    k.op("pool", [sab], [a_], lambda e: e.tensor_copy(
                        out=av[:, :, 0:2], in_=sab[:, c, :].rearrange("p (b r) -> p b r", r=2)))
                else:
                    k.op("pool", [halo], [a_], lambda e: e.tensor_copy(out=a_[:, 0:2], in_=halo[:, c, :]))
                k.op("act", [ps_a], [a_], lambda e: e.copy(out=av[:, :, 2:L], in_=pav))
                k.op("act", [ps_a, cw, cb], [cb_], lambda e: e.activation(
                    out=cb_[:, 0:T], in_=ps_a[:, 0:T], func=AF.Identity, scale=cw[:, 2, c:c + 1], bias=cb[:, c:c + 1]))
                k.op("dve", [a_, cb_, cw], [cb_], lambda e: e.scalar_tensor_tensor(
                    out=cv, in0=av[:, :, 1:1 + TL], scalar=cw[:, 1, c:c + 1], in1=cv, op0=ALU.mult, op1=ALU.add))
                k.op("dve", [a_, cb_, cw], [cb_], lambda e: e.scalar_tensor_tensor(
                    out=cv, in0=av[:, :, 0:TL], scalar=cw[:, 0, c:c + 1], in1=cv, op0=ALU.mult, op1=ALU.add))
                if not sample:
                    k.op("pool", [a_], [halo], lambda e: e.tensor_copy(out=halo[:, c, :], in_=a_[:, T:T + 2]))
                k.op("act", [cb_], [cb_], lambda e: e.activation(out=cb_[:, 0:T], in_=cb_[:, 0:T], func=AF.Gelu))
                k.op("dve", [cb_, ps_u], [actT_c[c]], lambda e: e.tensor_tensor(out=actT[:, c, 0:T], in0=cb_[:, 0:T], in1=ps_u[:, 0:T], op=ALU.mult))
            pf_tr(0)
            pf_tr(1)
            pf_dma(2)
            pf_dma(3)
            for tt in range(ntt):
                if tt == 1:
                    pf_tr(2)
                    pf_tr(3)
                rb = res[cntb[1] % 2]
                hb = rb
                yb = rb
                sb2 = stat2[cntb[1] % 2]
                cntb[1] += 1
                r0 = tok0 + tt * TT
                k.dma("sp", hb[0:TT, :], h_scr[r0:r0 + TT, :], [hscr], [hb])
                for half in range(2):
                    ps = nps()
                    for c in range(NFF):
                        k.op("pe", [actT_c[c], wd], [ps], lambda e, c=c: e.matmul(
                            ps[0:TT, :], lhsT=actT[:, c, tt * TT:(tt + 1) * TT], rhs=wd[:, c, half * 512:(half + 1) * 512],
                            start=(c == 0), stop=(c == NFF - 1)))
                    k.op("dve", [hb, ps], [rb], lambda e, half=half: e.scalar_tensor_tensor(
                        out=rb[0:TT, half * 512:(half + 1) * 512], in0=hb[0:TT, half * 512:(half + 1) * 512], scalar=ALPHA,
                        in1=ps[0:TT, :], op0=ALU.mult, op1=ALU.add))
                layer_norm(es, rb, TT, lng, lnbb, sb2, yb)
                dst = ys[r0 - SEQ:r0 - SEQ + TT, :] if sample else yp[r0:r0 + TT, :]
                k.dma("pool", dst, yb[0:TT, :], [yb], [])
            if sample or tok0 + T == SEQ:
                M = 64 if sample else 2
                c0 = 0 if sample else T - 2
                for n0 in range(0, DFF, 512):
                    nn = min(512, DFF - n0)
                    ps = nps()
                    for kk_ in range(8):
                        k.op("pe", [hT] + wg_g, [ps], lambda e, kk_=kk_: e.matmul(
                            ps[0:M, 0:nn], lhsT=hT[:, kk_, c0:c0 + M], rhs=wg[:, kk_, n0:n0 + nn], start=(kk_ == 0), stop=(kk_ == 7)))
                    k.op("act", [ps], [ctm], lambda e: e.copy(out=ctm[0:M, 0:nn], in_=ps[0:M, 0:nn]))
                    if sample:
                        k.dma("sp", [nconv_s[:, r, n0:n0 + nn] for r in range(2)],
                              [ctm[2 + r:64:4, 0:nn] for r in range(2)], [ctm], [])
                    else:
                        k.dma("sp", nconv_p[:, n0:n0 + nn], ctm[0:2, 0:nn], [ctm], [])

        tilesb = [(i_ * TB, TB, False) for i_ in range(SEQ // TB)] + [(SEQ, NS * ST, True)]
        for st in range(SEQ // TB):
            b_supertile(st, *tilesb[st], tilesb[st + 1])
            ckpt("b st%d" % st)
        for p0 in range(0, DFF, 1024):
            pn_ = min(1024, DFF - p0)
            stg = res[(p0 // 1024) % 2]
            k.dma("sp", stg[0:32, 0:pn_], sconv[:, p0:p0 + pn_], [], [stg])
            for c in range(p0 // 128, (p0 + pn_) // 128):
                ps = nps()
                k.op("pe", [stg, ident32], [ps], lambda e, c=c: e.transpose(
                    out=ps[:, 0:32], in_=stg[0:32, c * 128 - p0:(c + 1) * 128 - p0], identity=ident32[0:32, 0:32]))
                k.op("dve", [ps], [sab], lambda e, c=c: e.tensor_copy(out=sab[:, c, :], in_=ps[:, 0:32]))
        b_supertile(SEQ // TB, *tilesb[-1], None)
        ckpt("b sample")
        k.barrier()


def _consts():
    c = {}
    c["c_ident"] = np.eye(128, dtype=np.float32)
    c["c_ones"] = np.full((128, 128), 1.0 / 128.0, np.float32)
    rp = np.ones((128, TA), np.float32)
    rp[:, ::64] = 0.0
    c["c_rmask_p"] = rp
    rs = np.ones((128, 64), np.float32)
    rs[:, ::ST] = 0.0
    c["c_rmask_s"] = rs
    s = np.arange(128)[:, None]
    t = np.arange(128)[None, :]
    c["c_cmask_p"] = ((s // 64 == t // 64) & (s <= t)).astype(np.float32)
    s = np.arange(64)[:, None]
    t = np.arange(64)[None, :]
    c["c_cmask_s"] = ((s // ST == t // ST) & (s <= t)).astype(np.float32)
    ic = np.zeros((128, 2, 16), np.float32)
    for c2 in range(2):
        for half in range(2):
            w = 2 ** (2 * c2 + half + 1)
            ic[half * 64:(half + 1) * 64, c2, :] = 1.0 / np.minimum(np.arange(16) + 1, w)
    c["c_invcnt"] = ic
    tt = np.arange(64)
    c["c_seqm_tm"] = (tt[:, None] // ST == np.arange(16)[None, :]).astype(np.float32)
    c["c_seqm_fm"] = np.broadcast_to((np.arange(16)[:, None] == tt[None, :] // ST).astype(np.float32), (128, 16, 64)).copy()
    return c


_NC_CACHE = {}


def kernel(x_prompt, x_sample, state_pool, state_hgrn, state_ffn_conv, cache_mem_k, cache_mem_v, mem_prompt,
           lb_logits, w_in, w_pool_grp, pool_scale, hg_norm_g, w_mem_k, w_mem_v, w_br_pool, w_br_hg, w_br_mem,
           w_out, ln1_g, ln1_b, w_gate, w_up, conv_w, conv_b, w_down, ln2_g, ln2_b):
    f = lambda a: np.ascontiguousarray(np.asarray(a, dtype=np.float32))
    if "nc" not in _NC_CACHE:
        _NC_CACHE["nc"] = _build()
    nc = _NC_CACHE["nc"]
    shared = {
        "lbl": f(lb_logits), "w_in": f(w_in)[0], "w_grp": f(w_pool_grp)[0], "pool_scale": f(pool_scale)[0],
        "hg_g": f(hg_norm_g)[0], "w_mem_k": f(w_mem_k)[0], "w_mem_v": f(w_mem_v)[0], "w_br_pool": f(w_br_pool)[0],
        "w_br_hg": f(w_br_hg)[0], "w_br_mem": f(w_br_mem)[0], "w_out": f(w_out)[0], "ln1_g": f(ln1_g)[0],
        "ln1_b": f(ln1_b)[0], "w_gate": f(w_gate)[0], "w_up": f(w_up)[0], "conv_w": f(conv_w)[0], "conv_b": f(conv_b)[0],
        "w_down": f(w_down)[0], "ln2_g": f(ln2_g)[0], "ln2_b": f(ln2_b)[0],
    }
    shared.update(_consts())
    xp_, xs_ = f(x_prompt), f(x_sample)
    sp_, sh_, sc_ = f(state_pool)[0], f(state_hgrn)[0], f(state_ffn_conv)[0]
    ck_, cv_, mp_ = f(cache_mem_k)[0], f(cache_mem_v)[0], f(mem_prompt)
    in_maps = []
    for c in range(NCORES):
        b = slice(c * NS, (c + 1) * NS)
        m = dict(shared)
        m["xp"] = xp_[c]
        m["xs"] = xs_[b].reshape(NS * ST, D)
        m["spool"] = sp_[b]
        m["shg"] = sh_[b]
        m["sconv"] = sc_[b].reshape(NS * 2, DFF)
        m["cmk"] = ck_[b].reshape(NS, 256, 256)
        m["cmv"] = cv_[b].reshape(NS, 256, 256)
        m["memp"] = mp_[c]
        in_maps.append(m)
    res = run_bass_kernel_spmd(nc, in_maps, core_ids=list(range(NCORES)))
    R = res.results
    g = lambda n: np.stack([np.asarray(R[c][n], dtype=np.float32) for c in range(NCORES)])
    y_p = g("yp")
    y_s = g("ys").reshape(NCORES * NS, ST, D)
    o_pool_p = g("npool_p")[None]
    o_hg_p = g("nhg_p")[None]
    o_conv_p = g("nconv_p")[None]
    o_mk_p = g("nmk_p").reshape(1, NCORES, 256, 4, 64)
    o_mv_p = g("nmv_p").reshape(1, NCORES, 256, 4, 64)
    o_pool_s = g("npool_s").reshape(1, NCORES * NS, 15, 256)
    o_hg_s = g("nhg_s").reshape(1, NCORES * NS, 4, 128, 128)
    o_conv_s = g("nconv_s").reshape(1, NCORES * NS, 2, DFF)
    return (y_p, y_s, o_pool_p, o_hg_p, o_conv_p, o_mk_p, o_mv_p, o_pool_s, o_hg_s, o_conv_s)
```

```python
import numpy as np
from contextlib import ExitStack
import concourse.bass as bass
import concourse.mybir as mybir
from concourse.bass_utils import run_bass_kernel_spmd

F32 = mybir.dt.float32
BF16 = mybir.dt.bfloat16
ALU = mybir.AluOpType
AF = mybir.ActivationFunctionType
AX = mybir.AxisListType

NCORES = 8
D = 1024
SEQ = 2048
NS = 16
ST = 4
NTOK = SEQ + NS * ST
DFF = 2816
NFF = DFF // 128
ALPHA = float(2.0 ** 0.25)
LN_EPS = 1e-5
RMS_EPS = 1e-6
TA = 256
TB = 512


class Buf:
    def __init__(self, k, t, name):
        self.k = k
        self.t = t
        self.name = name
        self.w = None
        self.r = []
        self.dsem = None
        self.dcnt = 0
        self.excl = False

    def __getitem__(self, idx):
        return self.t[idx]


class Eng:
    def __init__(self, name, eng, sem):
        self.name = name
        self.eng = eng
        self.sem = sem
        self.cnt = 0
        self.waited = {}


class RR:
    def __init__(self):
        import threading
        self.th = threading
        self.cv = threading.Condition()
        self.turn = 0
        self.alive = []
        self.idx = {}

    def _advance(self):
        n = len(self.alive)
        for d in range(1, n + 1):
            j = (self.turn + d) % n
            if self.alive[j]:
                self.turn = j
                return
        self.turn = -1

    def switch(self):
        i = self.idx.get(self.th.get_ident())
        if i is None:
            return
        self.cnt[i] += 1
        if self.cnt[i] % self.wts[i]:
            return
        with self.cv:
            self._advance()
            self.cv.notify_all()
            while self.turn != i:
                self.cv.wait()

    def run(self, fns, wts=None):
        errs = []
        self.wts = list(wts) if wts else [1] * len(fns)
        self.cnt = [0] * len(fns)
        self.alive = [True] * len(fns)
        self.turn = 0
        self.idx = {}

        def worker(i, fn):
            self.idx[self.th.get_ident()] = i
            with self.cv:
                while self.turn != i:
                    self.cv.wait()
            try:
                fn()
            except BaseException as ex:
                errs.append(ex)
            finally:
                with self.cv:
                    self.alive[i] = False
                    if self.turn == i:
                        self._advance()
                    self.cv.notify_all()
        ths = [self.th.Thread(target=worker, args=(i, f)) for i, f in enumerate(fns)]
        for t in ths:
            t.start()
        for t in ths:
            t.join()
        self.idx = {}
        if errs:
            raise errs[0]


class K:
    def __init__(self, nc, es):
        self.nc = nc
        self.es = es
        self.E = {}
        for name, eng in (("pe", nc.tensor), ("act", nc.scalar), ("dve", nc.vector),
                          ("pool", nc.gpsimd), ("sp", nc.sync)):
            self.E[name] = Eng(name, eng, es.enter_context(nc.semaphore("sem_" + name)))
        self.dma_bufs = []
        self.nsem = 5
        self.psn = 0
        self.stopped = False
        self.rr = RR()
        import threading
        self.tls = threading.local()

    def sb(self, es, name, shape, dt):
        return Buf(self, es.enter_context(self.nc.sbuf_tensor(name, list(shape), dt)), name)

    def alias(self, b, name):
        return Buf(self, b.t, name)

    def dsem_of(self, b):
        if b.dsem is None:
            b.dsem = self.es.enter_context(self.nc.semaphore("ds_" + b.name))
            self.nsem += 1
            self.dma_bufs.append(b)
        return b.dsem

    def _wait(self, e, tok):
        if tok is None:
            return
        sem, val = tok
        key = id(sem)
        if e.waited.get(key, 0) >= val:
            return
        import os as _os
        if sem is e.sem and _os.environ.get("NOSELF"):
            return
        if sem is e.sem and e.name in ("pe", "act"):
            return
        e.eng.wait_ge(sem, val)
        e.waited[key] = val

    def _deps(self, e, reads, writes):
        toks = []
        for b in reads:
            toks.append(b.w)
            if b.excl:
                toks += [t for t in b.r if t[0] is not e.sem]
        for b in writes:
            toks.append(b.w)
            toks += b.r
        best = {}
        for t in toks:
            if t is not None and (id(t[0]) not in best or best[id(t[0])][1] < t[1]):
                best[id(t[0])] = t
        for t in best.values():
            self._wait(e, t)

    def op(self, en, reads, writes, fn):
        if self.stopped:
            return None
        e = self.E[en]
        self._deps(e, reads, writes)
        ins = fn(e.eng)
        e.cnt += 1
        ins.then_inc(e.sem, 1)
        tok = (e.sem, e.cnt)
        for b in reads:
            b.r.append(tok)
        for b in writes:
            b.w = tok
            b.r = []
        self.rr.switch()
        return ins

    def dma(self, qn, out, in_, reads, writes, n=1, **kw):
        if self.stopped:
            return
        e = self.E[qn]
        self._deps(e, reads, writes)
        tb = writes[0] if writes else reads[0]
        sem = self.dsem_of(tb)
        outs = out if isinstance(out, list) else [out]
        ins = in_ if isinstance(in_, list) else [in_]
        for o, i in zip(outs, ins):
            e.eng.dma_start(out=o, in_=i, **kw).then_inc(sem, 16)
            tb.dcnt += 16
        tok = (sem, tb.dcnt)
        for b in reads:
            b.r.append(tok)
        for b in writes:
            b.w = tok
            b.r = []
        self.rr.switch()

    def barrier(self, bufs=()):
        if self.stopped:
            return
        toks = [(e.sem, e.cnt) for e in self.E.values() if e.cnt > 0]
        for b in self.dma_bufs:
            if b.dcnt:
                toks.append((b.dsem, b.dcnt))
        for e in self.E.values():
            for t in toks:
                if t[0] is e.sem:
                    continue
                self._wait(e, t)


def _build(dbg=False):
    nc = bass.Bass("TRN2", target_bir_lowering=False)
    outer = ExitStack()
    with outer:
        _emit(nc, outer)
    return nc


class _Stop(Exception):
    pass


def _emit(nc, outer):
    import os
    k = K(nc, outer)
    kstop = int(os.environ.get("KSTOP", "0"))
    stage = [0]

    def ckpt(name):
        stage[0] += 1
        if kstop and stage[0] >= kstop and not k.stopped:
            print("STOP at stage", stage[0], name)
            k.stopped = True
    _emit2(nc, outer, k, ckpt)
    k.stopped = False
    k.barrier()
    print("instr counts:", {n: e.cnt for n, e in k.E.items()}, "sems:", k.nsem)


def _emit2(nc, outer, k, ckpt):

    def din(name, shape):
        return nc.dram_tensor(name, list(shape), F32, kind="ExternalInput").ap()

    def dout(name, shape):
        return nc.dram_tensor(name, list(shape), F32, kind="ExternalOutput").ap()

    xp = din("xp", [SEQ, D])
    xs = din("xs", [NS * ST, D])
    spool = din("spool", [NS, 15, 256])
    shg = din("shg", [NS, 4, 128, 128])
    sconv = din("sconv", [NS * 2, DFF])
    cmk = din("cmk", [NS, 256, 256])
    cmv = din("cmv", [NS, 256, 256])
    memp = din("memp", [256, D])
    lbl = din("lbl", [2, 512])
    w_in = din("w_in", [D, 5632])
    w_grp = din("w_grp", [4, 64, 64])
    pool_scale = din("pool_scale", [256])
    hg_g = din("hg_g", [512])
    w_mem_k = din("w_mem_k", [D, 256])
    w_mem_v = din("w_mem_v", [D, 256])
    w_br_pool = din("w_br_pool", [256, D])
    w_br_hg = din("w_br_hg", [512, D])
    w_br_mem = din("w_br_mem", [256, D])
    w_out = din("w_out", [D, D])
    ln1_g = din("ln1_g", [D])
    ln1_b = din("ln1_b", [D])
    w_gate = din("w_gate", [D, DFF])
    w_up = din("w_up", [D, DFF])
    conv_w = din("conv_w", [3, DFF])
    conv_b = din("conv_b", [DFF])
    w_down = din("w_down", [DFF, D])
    ln2_g = din("ln2_g", [D])
    ln2_b = din("ln2_b", [D])
    c_ident = din("c_ident", [128, 128])
    c_ones = din("c_ones", [128, 128])
    c_rmask_p = din("c_rmask_p", [128, TA])
    c_rmask_s = din("c_rmask_s", [128, 64])
    c_cmask_p = din("c_cmask_p", [128, 128])
    c_cmask_s = din("c_cmask_s", [64, 64])
    c_invcnt = din("c_invcnt", [128, 2, 16])
    c_seqm_tm = din("c_seqm_tm", [64, 16])
    c_seqm_fm = din("c_seqm_fm", [128, 16, 64])

    yp = dout("yp", [SEQ, D])
    ys = dout("ys", [NS * ST, D])
    npool_p = dout("npool_p", [15, 256])
    nhg_p = dout("nhg_p", [4, 128, 128])
    nconv_p = dout("nconv_p", [2, DFF])
    nmk_p = dout("nmk_p", [256, 256])
    nmv_p = dout("nmv_p", [256, 256])
    npool_s = dout("npool_s", [NS, 15, 256])
    nhg_s = dout("nhg_s", [NS, 4, 128, 128])
    nconv_s = dout("nconv_s", [NS, 2, DFF])
    h_scr_t = nc.dram_tensor("h_scr", [NTOK, D], F32)
    h_scr = h_scr_t.ap()

    es0 = outer
    ident = k.sb(es0, "ident", [128, 128], BF16)
    ident32 = k.sb(es0, "ident32", [128, 128], F32)
    ones32 = k.sb(es0, "ones32", [128, 128], F32)
    epsln = k.sb(es0, "epsln", [128, 1], F32)
    epsrms = k.sb(es0, "epsrms", [128, 1], F32)
    hscr = Buf(k, h_scr_t, "hscr")
    esA = ExitStack()
    esA.__enter__()
    xT_scr_t = nc.dram_tensor("xT_scr", [128, 8, NTOK], BF16)
    yc_scr_t = nc.dram_tensor("yc_scr", [128, 8, NTOK], BF16)
    xT_scr, yc_scr = xT_scr_t.ap(), yc_scr_t.ap()
    xscr = Buf(k, xT_scr_t, "xscr")
    yscr = Buf(k, yc_scr_t, "yscr")

    PS = []
    PSB = []
    psbn = [0]

    def set_psum(es, nf, nb, tag):
        PS.clear()
        for i in range(8):
            PS.append(Buf(k, es.enter_context(nc.psum_tensor("ps%s%d" % (tag, i), [128, 512], F32)), "ps%s%d" % (tag, i)))
            PS[-1].excl = True
        ldT_cache.clear()

    class Chain:
        def __init__(self, banks):
            self.banks = banks
            self.n = 0

    def use_chain(banks):
        k.tls.chain = Chain(banks)

    def cur_banks():
        ch = getattr(k.tls, "chain", None)
        if ch is None:
            ch = k.tls.chain = Chain(list(PS))
        return ch

    def nps():
        ch = cur_banks()
        b = ch.banks[ch.n % len(ch.banks)]
        ch.n += 1
        return b

    npsb = nps

    def bfv(ps):
        return ps.t[:].bitcast(BF16)

    ldT_cache = {}

    k.dma("pool", ident[:], c_ident, [], [ident])
    k.dma("sp", ident32[:], c_ident, [], [ident32])
    k.dma("sp", ones32[:], c_ones, [], [ones32])
    k.op("dve", [], [epsln], lambda e: e.memset(epsln[:], LN_EPS))
    k.op("dve", [], [epsrms], lambda e: e.memset(epsrms[:], RMS_EPS))

    ldT_cache = {}

    ckpt("consts")
    import os as _os3
    KDBG = bool(_os3.environ.get("KDBG"))

    def tap(name, buf, ap):
        if not KDBG:
            return
        d = nc.dram_tensor("dbg_" + name, list(ap.shape), F32, kind="ExternalOutput").ap()
        k.dma("pool", d, ap, [buf], [])

    def load_T_dma(es, src_rows, ntok, tag, qn="pool"):
        if tag not in ldT_cache:
            ldT_cache[tag] = k.sb(es, "ldT_" + tag, [128, D], BF16)
        tb = ldT_cache[tag]
        k.dma(qn, tb[0:ntok, :], src_rows, [], [tb])

    def load_T(es, src_rows, dst, col0, ntok, tag, qn="pool", dma=True):
        if dma:
            load_T_dma(es, src_rows, ntok, tag, qn)
        tb = ldT_cache[tag]
        ps = npsb()
        pv = bfv(ps)
        for c in range(8):
            k.op("pe", [tb, ident], [ps], lambda e, c=c: e.transpose(
                out=pv[:, c * 128:c * 128 + ntok], in_=tb[0:ntok, c * 128:(c + 1) * 128],
                identity=ident[0:ntok, 0:ntok]))
        k.op("act", [ps], [dst], lambda e: e.copy(
            out=dst[:, :, col0:col0 + ntok],
            in_=pv.rearrange("p (c t) -> p c t", c=8)[:, :, 0:ntok]))

    with ExitStack() as es:
        set_psum(es, 6, 2, "a")
        use_chain(list(PS))
        w_inA = k.sb(es, "w_inA", [128, 8, 2560], BF16)
        wmk = k.sb(es, "wmk", [128, 8, 256], BF16)
        wmv = k.sb(es, "wmv", [128, 8, 256], BF16)
        wgrp = k.sb(es, "wgrp", [128, 2, 128], BF16)
        lbt = k.sb(es, "lbt", [128, 2, 4], F32)
        lbv = k.sb(es, "lbv", [128, 4], F32)
        omlv = k.sb(es, "omlv", [128, 4], F32)
        nomlv = k.sb(es, "nomlv", [128, 4], F32)
        pscale = k.sb(es, "pscale", [128, 2], F32)
        hgam = k.sb(es, "hgam", [128, 4], F32)
        rmask_p = k.sb(es, "rmask_p", [128, TA], F32)
        rmask_s = k.sb(es, "rmask_s", [128, 64], F32)
        cmask_p = k.sb(es, "cmask_p", [128, 128], F32)
        cmask_s = k.sb(es, "cmask_s", [64, 64], F32)
        invcnt = k.sb(es, "invcnt", [128, 2, 16], F32)
        seqm_tm = k.sb(es, "seqm_tm", [64, 16], F32)
        seqm_fm = k.sb(es, "seqm_fm", [128, 16, 64], BF16)

        w_inA_g = [k.alias(w_inA, "w_inA_g%d" % g) for g in range(5)]
        k.dma("pool", wmk[:], w_mem_k.rearrange("(c p) n -> p c n", p=128), [], [wmk])
        k.dma("pool", wmv[:], w_mem_v.rearrange("(c p) n -> p c n", p=128), [], [wmv])
        k.op("dve", [], [wgrp], lambda e: e.memset(wgrp[:], 0.0))
        k.dma("pool", [wgrp[(g % 2) * 64:(g % 2) * 64 + 64, g // 2, (g % 2) * 64:(g % 2) * 64 + 64] for g in range(4)],
              [w_grp[g] for g in range(4)], [], [wgrp])
        with nc.allow_non_contiguous_dma(reason="tiny per-partition scalar tables"):
            k.dma("sp", lbt[:], lbl.rearrange("r (h p) -> p r h", p=128), [], [lbt])
            k.dma("sp", pscale[:], pool_scale.rearrange("(c p) -> p c", p=128), [], [pscale])
            k.dma("sp", hgam[:], hg_g.rearrange("(c p) -> p c", p=128), [], [hgam])
        k.dma("sp", rmask_p[:], c_rmask_p, [], [rmask_p])
        k.dma("sp", rmask_s[:], c_rmask_s, [], [rmask_s])
        k.dma("sp", cmask_p[:], c_cmask_p, [], [cmask_p])
        k.dma("sp", cmask_s[:], c_cmask_s, [], [cmask_s])
        k.dma("sp", invcnt[:], c_invcnt, [], [invcnt])
        k.dma("sp", seqm_tm[:], c_seqm_tm, [], [seqm_tm])
        k.dma("pool", seqm_fm[:], c_seqm_fm, [], [seqm_fm])
        k.op("dve", [lbt], [lbv], lambda e: e.tensor_tensor(out=lbv[:], in0=lbt[:, 0, :], in1=lbt[:, 1, :], op=ALU.subtract))
        k.op("act", [lbv], [lbv], lambda e: e.activation(out=lbv[:], in_=lbv[:], func=AF.Sigmoid))
        k.op("dve", [lbv], [omlv], lambda e: e.tensor_scalar(out=omlv[:], in0=lbv[:], scalar1=-1.0, scalar2=1.0, op0=ALU.mult, op1=ALU.add))
        k.op("dve", [omlv], [nomlv], lambda e: e.tensor_scalar(out=nomlv[:], in0=omlv[:], scalar1=-1.0, scalar2=None, op0=ALU.mult))

        ckpt("a1 weights")
        memT = k.sb(es, "memT", [128, 8, 256], BF16)
        ktp = k.sb(es, "ktp", [128, 4, 256], BF16)
        vpd = k.sb(es, "vpd", [128, 2, 4, 128], BF16)
        mko = k.sb(es, "mko", [128, 2, 512], F32)
        import os as _os
        for mc in range(int(_os.environ.get("LTN", "2"))):
            load_T(es, memp[mc * 128:(mc + 1) * 128, :], memT, mc * 128, 128, "mem%d" % mc)
        ckpt("memT")
        k.op("dve", [], [ktp], lambda e: e.memset(ktp[:], 0.0))
        k.op("dve", [], [vpd], lambda e: e.memset(vpd[:], 0.0))
        STG4 = int(_os.environ.get("STG4", "9"))
        for mc in range(2):
            ps = nps()
            if STG4 < 2:
                continue
            for c in range(8):
                k.op("pe", [memT, wmk], [ps], lambda e, c=c: e.matmul(
                    ps[:, 0:256], lhsT=memT[:, c, mc * 128:(mc + 1) * 128], rhs=wmk[:, c, :], start=(c == 0), stop=(c == 7)))
            for c in range(8):
                k.op("pe", [memT, wmv], [ps], lambda e, c=c: e.matmul(
                    ps[:, 256:512], lhsT=memT[:, c, mc * 128:(mc + 1) * 128], rhs=wmv[:, c, :], start=(c == 0), stop=(c == 7)))
            if STG4 < 3:
                continue
            k.op("act", [ps], [mko], lambda e: e.copy(out=mko[:, mc, :], in_=ps[:, :]))
            if STG4 < 4:
                continue
            for h in range(4):
                k.op("dve", [ps] + ([mko] if _os.environ.get("SERIAL") else []), [vpd], lambda e, h=h: e.tensor_copy(
                    out=vpd[:, mc, h, (h % 2) * 64:(h % 2) * 64 + 64], in_=ps[:, 256 + h * 64:256 + (h + 1) * 64]))
        ckpt("mem TM")
        k.dma("sp", nmk_p.rearrange("(c p) n -> p c n", p=128), mko[:, :, 0:256], [mko], [])
        k.dma("sp", nmv_p.rearrange("(c p) n -> p c n", p=128), mko[:, :, 256:512], [mko], [])
        ckpt("mem out")
        for j in range(2):
            ps = nps()
            for c in range(8):
                k.op("pe", [memT, wmk], [ps], lambda e, c=c: e.matmul(
                    ps[:, 0:256], lhsT=wmk[:, c, j * 128:(j + 1) * 128], rhs=memT[:, c, :], start=(c == 0), stop=(c == 7)))
            for hh in range(2):
                h = 2 * j + hh
                k.op("dve", [ps], [ktp], lambda e, h=h, hh=hh: e.tensor_copy(
                    out=ktp[hh * 64:(hh + 1) * 64, h, :], in_=ps[hh * 64:(hh + 1) * 64, 0:256]))

        ckpt("memkv")
        ub = [k.sb(es, "ub%d" % i, [128, 2, 15 + TA], F32) for i in range(2)]
        sw = [k.sb(es, "sw%d" % i, [128, 2, 15 + TA], F32) for i in range(2)]
        dif = k.sb(es, "dif", [128, 2, TA], BF16)
        class NS_:
            pass
        BB = [NS_(), NS_()]
        for i_, b_ in enumerate(BB):
            b_.xt = k.sb(es, "xTs%d" % i_, [128, 8, TA], BF16)
            b_.yc = k.sb(es, "ycs%d" % i_, [128, 8, TA], BF16)
            b_.yc_a = k.alias(b_.yc, "ycs%d_a" % i_)
            b_.yc_b = k.alias(b_.yc, "ycs%d_b" % i_)
            b_.yc_c = k.alias(b_.yc, "ycs%d_c" % i_)
            b_.sgm = k.sb(es, "sgm%d" % i_, [128, 4, TA], F32)
            b_.qs = k.sb(es, "qs%d" % i_, [128, 4, TA], F32)
            b_.gs = k.sb(es, "gs%d" % i_, [128, 4, TA], F32)
            b_.qcT = k.sb(es, "qcT%d" % i_, [128, 2, TA], BF16)
            b_.sgm_h = [k.alias(b_.sgm, "sgm%d_h%d" % (i_, h_)) for h_ in range(4)]
            b_.qs_h = [k.alias(b_.qs, "qs%d_h%d" % (i_, h_)) for h_ in range(4)]
            b_.gs_h = [k.alias(b_.gs, "gs%d_h%d" % (i_, h_)) for h_ in range(4)]
            b_.vtm = [k.sb(es, "vtm%d_%d" % (i_, t_), [128, 512], BF16) for t_ in range(2)]
        NPF = 8
        sst = [k.sb(es, "sst%d" % i_, [128, 4, 128], F32) for i_ in range(NPF)]
        kst = [k.sb(es, "kst%d" % i_, [128, 2, 256], BF16) for i_ in range(NPF)]
        vst = [k.sb(es, "vst%d" % i_, [128, 2, 256], BF16) for i_ in range(NPF)]
        khm2 = [k.sb(es, "khm%d" % i_, [64, 512], BF16) for i_ in range(2)]
        qms2 = [k.sb(es, "qms%d" % i_, [128, 2, 64], BF16) for i_ in range(2)]
        eb = k.sb(es, "eb", [128, 4, TA], F32)
        eb_h = [k.alias(eb, "eb_h%d" % h_) for h_ in range(4)]
        lf = [k.sb(es, "lf%d" % i, [128, TA], F32) for i in range(2)]
        kk = [k.sb(es, "kk%d" % i, [128, TA], F32) for i in range(2)]
        enb = [k.sb(es, "enb%d" % i, [128, TA], F32) for i in range(2)]
        qtil = k.sb(es, "qtil", [128, 4, TA], BF16)
        ktil = k.sb(es, "ktil", [128, 4, TA], BF16)
        khat = k.sb(es, "khat", [128, 4, TA], BF16)
        khtm = [k.sb(es, "khtm%d" % i, [128, 512], BF16) for i in range(2)]
        scm = [k.sb(es, "scm%d" % i, [128, 4, 128], BF16) for i in range(2)]
        oT = k.sb(es, "oT", [128, 4, TA], F32)
        osq = k.sb(es, "osq", [128, 4, TA], F32)
        rstd = osq
        S32 = k.sb(es, "S32", [128, 4, 128], F32)
        Sbfs = [k.sb(es, "Sbf%d" % i, [128, 4, 128], BF16) for i in range(2)]
        sbi = [0]
        p32 = k.sb(es, "p32", [128, 4, 256], F32)
        pn = k.sb(es, "pn", [128, 4, 256], BF16)
        pT = k.sb(es, "pT", [128, 8, 128], BF16)
        stat = k.sb(es, "stat", [128, 16], F32)
        utm = k.sb(es, "utm", [128, 256], F32)

        def a1_supertile(st_i, tok0, T, sample):
            TT = 64 if sample else 128
            ntt = T // TT
            C = ST if sample else 64
            B = BB[st_i % 2]
            u = ub[st_i % 2]
            up = ub[(st_i + 1) % 2]

            def proj(cc):
                ps = nps()
                for c in range(8):
                    k.op("pe", [B.xt, w_inA_g[cc // 4]], [ps], lambda e, c=c: e.matmul(
                        ps[:, 0:T], lhsT=w_inA[:, c, cc * 128:(cc + 1) * 128], rhs=B.xt[:, c, 0:T], start=(c == 0), stop=(c == 7)))
                return ps

            def c_X():
                cur[0] = B
                src = xs if sample else xp
                for tt in range(ntt):
                    r0 = (tok0 - (SEQ if sample else 0)) + tt * TT
                    load_T(es, src[r0:r0 + TT, :], B.xt, tt * TT, TT, "x%d" % (tt % 2), dma=(st_i != 0))
                k.dma("sp", xT_scr[:, :, tok0:tok0 + T], B.xt[:, :, 0:T], [B.xt], [xscr])
                yield
                if not sample:
                    if tok0 == 0:
                        k.op("pool", [], [u], lambda e: e.memset(u[:, :, 0:15], 0.0))
                    else:
                        k.op("pool", [up], [u], lambda e: e.tensor_copy(out=u[:, :, 0:15], in_=up[:, :, TA:TA + 15]))
                    for c2 in range(2):
                        ps = proj(c2)
                        k.op("act", [ps], [u], lambda e, c2=c2: e.copy(out=u[:, c2, 15:15 + T], in_=ps[:, 0:T]))
                    yield
                for h in range(4):
                    psq = proj(2 + h)
                    k.op("act", [psq], [B.qs_h[h]], lambda e, h=h: e.activation(out=B.qs[:, h, 0:T], in_=psq[:, 0:T], func=AF.Sigmoid))
                    k.op("dve", [psq, B.qs_h[h]], [B.qs_h[h]], lambda e, h=h: e.tensor_tensor(out=B.qs[:, h, 0:T], in0=B.qs[:, h, 0:T], in1=psq[:, 0:T], op=ALU.mult))
                    psf = proj(6 + h)
                    k.op("act", [psf], [B.sgm_h[h]], lambda e, h=h: e.activation(out=B.sgm[:, h, 0:T], in_=psf[:, 0:T], func=AF.Sigmoid))
                    psg = proj(14 + h)
                    k.op("act", [psg], [B.gs_h[h]], lambda e, h=h: e.activation(out=B.gs[:, h, 0:T], in_=psg[:, 0:T], func=AF.Sigmoid))
                    k.op("dve", [psg, B.gs_h[h]], [B.gs_h[h]], lambda e, h=h: e.tensor_tensor(out=B.gs[:, h, 0:T], in0=B.gs[:, h, 0:T], in1=psg[:, 0:T], op=ALU.mult))
                yield
                for tt in range(ntt):
                    ps = nps()
                    for c in range(8):
                        k.op("pe", [B.xt, w_inA_g[2], w_inA_g[3]], [ps], lambda e, c=c, tt=tt: e.matmul(
                            ps[0:TT, :], lhsT=B.xt[:, c, tt * TT:(tt + 1) * TT], rhs=w_inA[:, c, 1280:1792], start=(c == 0), stop=(c == 7)))
                    k.op("act", [ps], [B.vtm[tt]], lambda e, tt=tt: e.copy(out=B.vtm[tt][0:TT, :], in_=ps[0:TT, :]))
                yield
                for j in range(2):
                    ps = proj(18 + j)
                    k.op("dve", [ps], [B.qcT], lambda e, j=j: e.tensor_copy(out=B.qcT[:, j, 0:T], in_=ps[:, 0:T]))
                yield
                return

            sgm, qs, gs, qcT, vtm = B.sgm, B.qs, B.gs, B.qcT, B.vtm
            def c_pool_attn():
                cur[0] = B
                if not sample:
                    L = 15 + T
                    s2, s4 = sw[0], sw[1]
                    k.op("dve", [u], [s2], lambda e: e.tensor_tensor(out=s2[:, :, 0:L - 1], in0=u[:, :, 0:L - 1], in1=u[:, :, 1:L], op=ALU.add))
                    k.op("dve", [s2], [s4], lambda e: e.tensor_tensor(out=s4[:, :, 0:L - 3], in0=s2[:, :, 0:L - 3], in1=s2[:, :, 2:L - 1], op=ALU.add))
                    k.op("dve", [s2, u], [dif], lambda e: e.scalar_tensor_tensor(
                        out=dif[0:64, 0, 0:T], in0=s2[0:64, 0, 14:14 + T], scalar=0.5, in1=u[0:64, 0, 15:15 + T], op0=ALU.mult, op1=ALU.subtract))
                    k.op("dve", [s4, u], [dif], lambda e: e.scalar_tensor_tensor(
                        out=dif[64:128, 0, 0:T], in0=s4[64:128, 0, 12:12 + T], scalar=0.25, in1=u[64:128, 0, 15:15 + T], op0=ALU.mult, op1=ALU.subtract))
                    k.op("dve", [s4, s2], [s2], lambda e: e.tensor_tensor(out=s2[:, 1, 0:L - 7], in0=s4[:, 1, 0:L - 7], in1=s4[:, 1, 4:L - 3], op=ALU.add))
                    k.op("dve", [s2, u], [dif], lambda e: e.scalar_tensor_tensor(
                        out=dif[0:64, 1, 0:T], in0=s2[0:64, 1, 8:8 + T], scalar=0.125, in1=u[0:64, 1, 15:15 + T], op0=ALU.mult, op1=ALU.subtract))
                    k.op("dve", [s2, s4], [s4], lambda e: e.tensor_tensor(out=s4[:, 1, 0:L - 15], in0=s2[:, 1, 0:L - 15], in1=s2[:, 1, 8:L - 7], op=ALU.add))
                    k.op("dve", [s4, u], [dif], lambda e: e.scalar_tensor_tensor(
                        out=dif[64:128, 1, 0:T], in0=s4[64:128, 1, 0:T], scalar=0.0625, in1=u[64:128, 1, 15:15 + T], op0=ALU.mult, op1=ALU.subtract))
                    if tok0 == 0:
                        for (p0, c2, sbuf, off) in ((0, 0, s2, 14), (64, 0, s4, 12), (0, 1, s2, 8), (64, 1, s4, 0)):
                            k.op("dve", [sbuf, invcnt], [sbuf], lambda e, p0=p0, c2=c2, sbuf=sbuf, off=off: e.tensor_tensor(
                                out=sbuf[p0:p0 + 64, c2, off:off + 16], in0=sbuf[p0:p0 + 64, c2, off:off + 16],
                                in1=invcnt[p0:p0 + 64, c2, :], op=ALU.mult))
                            k.op("dve", [sbuf, u], [dif], lambda e, p0=p0, c2=c2, sbuf=sbuf, off=off: e.tensor_tensor(
                                out=dif[p0:p0 + 64, c2, 0:16], in0=sbuf[p0:p0 + 64, c2, off:off + 16],
                                in1=u[p0:p0 + 64, c2, 15:31], op=ALU.subtract))
                    for c2 in range(2):
                        ps = nps()
                        k.op("pe", [dif, wgrp], [ps], lambda e, c2=c2: e.matmul(ps[:, 0:T], lhsT=wgrp[:, c2, :], rhs=dif[:, c2, 0:T], start=True, stop=True))
                        k.op("act", [ps, pscale], [B.yc_a], lambda e, c2=c2: e.activation(
                            out=B.yc[:, c2, 0:T], in_=ps[:, 0:T], func=AF.Copy, scale=pscale[:, c2:c2 + 1]))
                    if tok0 + T == SEQ:
                        ps = nps()
                        for c in range(8):
                            k.op("pe", [B.xt, w_inA_g[0]], [ps], lambda e, c=c: e.matmul(
                                ps[0:15, 0:256], lhsT=B.xt[:, c, T - 15:T], rhs=w_inA[:, c, 0:256], start=(c == 0), stop=(c == 7)))
                        k.op("act", [ps], [utm], lambda e: e.copy(out=utm[0:15, :], in_=ps[0:15, 0:256]))
                        k.dma("sp", npool_p, utm[0:15, :], [utm], [])

                if sample:
                    sample_pool(proj)
                if not sample:
                    for tt in range(ntt):
                        tc0 = tt * TT
                        pss = [nps(), nps()]
                        for h in range(4):
                            k.op("pe", [qcT, ktp], [pss[h // 2]], lambda e, h=h: e.matmul(
                                pss[h // 2][0:TT, (h % 2) * 256:(h % 2) * 256 + 256], lhsT=qcT[:, h // 2, tc0:tc0 + TT], rhs=ktp[:, h, :],
                                start=True, stop=True))
                        for j in range(2):
                            k.op("dve", [pss[j]], [stat], lambda e, j=j: e.tensor_reduce(
                                out=stat[0:TT, 2 * j:2 * j + 2], in_=pss[j][0:TT, :].rearrange("p (h m) -> p h m", h=2), axis=AX.X, op=ALU.max))
                        k.op("dve", [stat], [stat], lambda e: e.tensor_scalar(
                            out=stat[0:TT, 4:8], in0=stat[0:TT, 0:4], scalar1=-0.125, scalar2=None, op0=ALU.mult))
                        for h in range(4):
                            k.op("act", [pss[h // 2], stat], [p32, stat], lambda e, h=h: e.activation(
                                out=p32[0:TT, h, :], in_=pss[h // 2][0:TT, (h % 2) * 256:(h % 2) * 256 + 256], func=AF.Exp,
                                scale=0.125, bias=stat[0:TT, 4 + h:5 + h], accum_out=stat[0:TT, 8 + h:9 + h]))
                        k.op("dve", [stat], [stat], lambda e: e.reciprocal(out=stat[0:TT, 12:16], in_=stat[0:TT, 8:12]))
                        k.op("dve", [p32, stat], [pn], lambda e: e.tensor_tensor(
                            out=pn[0:TT, :, :], in0=p32[0:TT, :, :], in1=stat[0:TT, 12:16].unsqueeze(2).to_broadcast([TT, 4, 256]), op=ALU.mult))
                        ps = npsb()
                        pv = bfv(ps)
                        for h in range(4):
                            for mc in range(2):
                                i = h * 2 + mc
                                k.op("pe", [pn, ident], [ps], lambda e, h=h, mc=mc, i=i: e.transpose(
                                    out=pv[:, i * 128:i * 128 + TT], in_=pn[0:TT, h, mc * 128:(mc + 1) * 128], identity=ident[0:TT, 0:TT]))
                        k.op("act", [ps], [pT], lambda e: e.copy(out=pT[:, :, 0:TT], in_=pv.rearrange("p (i t) -> p i t", i=8)[:, :, 0:TT]))
                        ps_y = nps()
                        for j in range(2):
                            n = 0
                            for hh in range(2):
                                h = 2 * j + hh
                                for mc in range(2):
                                    k.op("pe", [vpd, pT], [ps_y], lambda e, h=h, mc=mc, j=j, n=n: e.matmul(
                                        ps_y[:, j * 128:j * 128 + TT], lhsT=vpd[:, mc, h, :], rhs=pT[:, h * 2 + mc, 0:TT],
                                        start=(n == 0), stop=(n == 3)))
                                    n += 1
                        k.op("dve", [ps_y], [B.yc_c], lambda e: e.tensor_copy(
                            out=B.yc[:, 6:8, tc0:tc0 + TT], in_=ps_y[:, 0:256].rearrange("p (j t) -> p j t", j=2)[:, :, 0:TT]))
                        yield

                if sample:
                    sample_attn()
                yield
            def c_hgrn():
                cur[0] = B
                rmask = rmask_s if sample else rmask_p
                yield
                nch = T // C
                for h in range(4):
                    if h == 2:
                        yield
                    l, kq, en = lf[h % 2], kk[h % 2], enb[h % 2]
                    k.op("act", [B.sgm_h[h], omlv, lbv], [l], lambda e, h=h, l=l: e.activation(
                        out=l[:, 0:T], in_=sgm[:, h, 0:T], func=AF.Ln, scale=omlv[:, h:h + 1], bias=lbv[:, h:h + 1]))
                    k.op("dve", [B.sgm_h[h], omlv, nomlv], [kq], lambda e, h=h, kq=kq: e.tensor_scalar(
                        out=kq[:, 0:T], in0=sgm[:, h, 0:T], scalar1=nomlv[:, h:h + 1], scalar2=omlv[:, h:h + 1], op0=ALU.mult, op1=ALU.add))
                    k.op("dve", [l, rmask], [l], lambda e, l=l: e.tensor_tensor_scan(
                        out=l[:, 0:T], data0=rmask[:, 0:T], data1=l[:, 0:T], initial=0.0, op0=ALU.mult, op1=ALU.add))
                    k.op("act", [l], [eb_h[h]], lambda e, h=h, l=l: e.activation(out=eb[:, h, 0:T], in_=l[:, 0:T], func=AF.Exp))
                    k.op("act", [l], [en], lambda e, l=l, en=en: e.activation(out=en[:, 0:T], in_=l[:, 0:T], func=AF.Exp, scale=-1.0))
                    k.op("pool", [B.qs_h[h], eb_h[h]], [qtil], lambda e, h=h: e.tensor_tensor(out=qtil[:, h, 0:T], in0=qs[:, h, 0:T], in1=eb[:, h, 0:T], op=ALU.mult))
                    k.op("dve", [kq, en], [kq], lambda e, kq=kq, en=en: e.tensor_tensor(out=kq[:, 0:T], in0=kq[:, 0:T], in1=en[:, 0:T], op=ALU.mult))
                    k.op("act", [kq], [ktil], lambda e, h=h, kq=kq: e.copy(out=ktil[:, h, 0:T], in_=kq[:, 0:T]))
                    k.op("dve", [kq, eb_h[h]], [khat], lambda e, h=h, kq=kq: e.tensor_tensor(
                        out=khat[:, h, 0:T].rearrange("p (c s) -> p c s", s=C),
                        in0=kq[:, 0:T].rearrange("p (c s) -> p c s", s=C),
                        in1=eb[:, h, C - 1:T:C].unsqueeze(2).to_broadcast([128, nch, C]), op=ALU.mult))
                yield
                for tt in range(ntt):
                    ps = npsb()
                    pv = bfv(ps)
                    for h in range(4):
                        k.op("pe", [khat, ident], [ps], lambda e, h=h, tt=tt: e.transpose(
                            out=pv[0:TT, h * 128:(h + 1) * 128], in_=khat[:, h, tt * TT:(tt + 1) * TT], identity=ident[:, :]))
                    k.op("dve", [ps], [khtm[tt]], lambda e, tt=tt: e.tensor_copy(out=khtm[tt][0:TT, :], in_=pv[0:TT, 0:512]))
                yield
                cmask = cmask_s if sample else cmask_p
                for tt in range(ntt):
                    if tt:
                        yield
                    tc0 = tt * TT
                    ps_sc = nps()
                    for h in range(4):
                        k.op("pe", [ktil, qtil], [ps_sc], lambda e, h=h: e.matmul(
                            ps_sc[0:TT, h * 128:h * 128 + TT], lhsT=ktil[:, h, tc0:tc0 + TT], rhs=qtil[:, h, tc0:tc0 + TT], start=True, stop=True))
                    sm = scm[tt % 2]
                    k.op("dve", [ps_sc, cmask], [sm], lambda e, sm=sm: e.tensor_tensor(
                        out=sm[0:TT, :, 0:TT], in0=ps_sc[0:TT, :].rearrange("p (h t) -> p h t", h=4)[:, :, 0:TT],
                        in1=cmask[0:TT, 0:TT].unsqueeze(1).to_broadcast([TT, 4, TT]), op=ALU.mult))
                    ps_o = nps()
                    if not sample:
                        first = (tok0 == 0 and tt == 0)
                        ps_us = [nps(), nps()]
                        for c in range(2):
                            for h in range(4):
                                k.op("pe", [khtm[tt], vtm[tt]], [ps_us[c]], lambda e, h=h, c=c: e.matmul(
                                    ps_us[c][:, h * 128:(h + 1) * 128], lhsT=khtm[tt][c * 64:(c + 1) * 64, h * 128:(h + 1) * 128],
                                    rhs=vtm[tt][c * 64:(c + 1) * 64, h * 128:(h + 1) * 128], start=True, stop=True))
                        for c in range(2):
                            has_inter = not (first and c == 0)
                            Scur = Sbfs[sbi[0] % 2]
                            Snxt = Sbfs[(sbi[0] + 1) % 2]
                            sbi[0] += 1
                            for h in range(4):
                                k.op("pe", [vtm[tt], sm], [ps_o], lambda e, h=h, c=c: e.matmul(
                                    ps_o[:, h * 128 + c * 64:h * 128 + (c + 1) * 64], lhsT=vtm[tt][0:TT, h * 128:(h + 1) * 128],
                                    rhs=sm[0:TT, h, c * 64:(c + 1) * 64], start=True, stop=(not has_inter)))
                                if has_inter:
                                    k.op("pe", [Scur, qtil], [ps_o], lambda e, h=h, c=c: e.matmul(
                                        ps_o[:, h * 128 + c * 64:h * 128 + (c + 1) * 64], lhsT=Scur[:, h, :],
                                        rhs=qtil[:, h, tc0 + c * 64:tc0 + (c + 1) * 64], start=False, stop=True))
                            ps_u = ps_us[c]
                            if first and c == 0:
                                k.op("dve", [ps_u], [S32], lambda e: e.tensor_copy(out=S32[:], in_=ps_u[:, :].rearrange("p (h e) -> p h e", h=4)))
                            else:
                                cl = tc0 + c * 64 + 63
                                k.op("dve", [S32] + eb_h, [S32], lambda e, cl=cl: e.tensor_tensor(
                                    out=S32[:], in0=S32[:], in1=eb[:, :, cl:cl + 1].to_broadcast([128, 4, 128]), op=ALU.mult))
                                k.op("dve", [S32, ps_u], [S32], lambda e: e.tensor_tensor(
                                    out=S32[:], in0=S32[:], in1=ps_u[:, :].rearrange("p (h e) -> p h e", h=4), op=ALU.add))
                            k.op("dve", [S32], [Snxt], lambda e: e.tensor_copy(out=Snxt[:], in_=S32[:]))
                    if sample:
                        sample_hgrn(ps_o, sm)
                    k.op("act", [ps_o], [oT], lambda e: e.copy(
                        out=oT[:, :, tc0:tc0 + TT], in_=ps_o[:, :].rearrange("p (h t) -> p h t", h=4)[:, :, 0:TT]))
                if (not sample) and tok0 + T == SEQ:
                    k.dma("sp", nhg_p.rearrange("h d e -> d h e"), S32[:], [S32], [])
                if tok0 == 0:
                    tap("oT", oT, oT[:, :, 0:T])
                    tap("qtil", qtil, qtil[:, :, 0:T])
                    tap("scm", scm[1], scm[1][:, :, :])
                    tap("gs", gs, gs[:, :, 0:T])
                yield
                k.op("act", [oT], [osq], lambda e: e.activation(out=osq[:, :, 0:T], in_=oT[:, :, 0:T], func=AF.Square))
                hp = min(4, 512 // T)
                for h0 in range(0, 4, hp):
                    ps = nps()
                    for hh in range(hp):
                        h = h0 + hh
                        k.op("pe", [osq, ones32], [ps], lambda e, h=h, hh=hh: e.matmul(
                            ps[:, hh * T:(hh + 1) * T], lhsT=ones32[:, :], rhs=osq[:, h, 0:T], start=True, stop=True))
                    k.op("act", [ps, epsrms], [rstd], lambda e, h0=h0: e.activation(
                        out=rstd[:, h0:h0 + hp, 0:T], in_=ps[:, 0:hp * T].rearrange("p (h t) -> p h t", h=hp), func=AF.Ln, bias=epsrms[:, 0:1]))
                k.op("act", [rstd], [rstd], lambda e: e.activation(out=rstd[:, :, 0:T], in_=rstd[:, :, 0:T], func=AF.Exp, scale=-0.5))
                k.op("dve", [oT, rstd], [oT], lambda e: e.tensor_tensor(out=oT[:, :, 0:T], in0=oT[:, :, 0:T], in1=rstd[:, :, 0:T], op=ALU.mult))
                for h in range(4):
                    k.op("dve", [oT, B.gs_h[h], hgam], [B.yc_b], lambda e, h=h: e.scalar_tensor_tensor(
                        out=B.yc[:, 2 + h, 0:T], in0=oT[:, h, 0:T], scalar=hgam[:, h:h + 1], in1=gs[:, h, 0:T], op0=ALU.mult, op1=ALU.mult))

                yield
                yield
            def c_fin():
                k.dma("sp", yc_scr[:, :, tok0:tok0 + T], B.yc[:, :, 0:T], [B.yc, B.yc_a, B.yc_b, B.yc_c], [yscr])
            return c_X, c_pool_attn, c_hgrn, c_fin

        class _Cur:
            def __getitem__(self, i):
                return k.tls.curB

            def __setitem__(self, i, v):
                k.tls.curB = v
        cur = _Cur()
        def sample_pool(proj):
            T = NS * ST
            L = 15 + ST
            us = p32[:].rearrange("p a b -> p (a b)")[:, 0:2 * NS * L].rearrange("p (c b l) -> p c b l", c=2, b=NS)
            s2 = mko[:].rearrange("p a b -> p (a b)")[:, 0:2 * NS * L].rearrange("p (c b l) -> p c b l", c=2, b=NS)
            s4 = sw[0][:].rearrange("p a b -> p (a b)")[:, 0:2 * NS * 16].rearrange("p (c b l) -> p c b l", c=2, b=NS)
            for half in range(2):
                k.dma("sp", utm[0:120, :], spool[half * 8:(half + 1) * 8].rearrange("b r c -> (b r) c"), [], [utm])
                for c2 in range(2):
                    ps = nps()
                    k.op("pe", [utm, ident32], [ps], lambda e, c2=c2: e.transpose(
                        out=ps[:, 0:120], in_=utm[0:120, c2 * 128:(c2 + 1) * 128], identity=ident32[0:120, 0:120]))
                    k.op("dve", [ps], [p32], lambda e, c2=c2, half=half: e.tensor_copy(
                        out=us[:, c2, half * 8:(half + 1) * 8, 0:15], in_=ps[:, 0:120].rearrange("p (b r) -> p b r", r=15)))
            for c2 in range(2):
                ps = proj(c2)
                k.op("act", [ps], [p32], lambda e, c2=c2: e.copy(out=us[:, c2, :, 15:L], in_=ps[:, 0:T].rearrange("p (b t) -> p b t", t=ST)))
            k.op("dve", [p32], [mko], lambda e: e.tensor_tensor(out=s2[:, :, :, 0:L - 1], in0=us[:, :, :, 0:L - 1], in1=us[:, :, :, 1:L], op=ALU.add))
            k.op("dve", [mko], [sw[0]], lambda e: e.tensor_tensor(out=s4[:, :, :, 0:L - 3], in0=s2[:, :, :, 0:L - 3], in1=s2[:, :, :, 2:L - 1], op=ALU.add))
            dv = dif[:, :, 0:T].rearrange("p c (b t) -> p c b t", t=ST)
            k.op("dve", [mko, p32], [dif], lambda e: e.scalar_tensor_tensor(
                out=dv[0:64, 0], in0=s2[0:64, 0, :, 14:14 + ST], scalar=0.5, in1=us[0:64, 0, :, 15:L], op0=ALU.mult, op1=ALU.subtract))
            k.op("dve", [sw[0], p32], [dif], lambda e: e.scalar_tensor_tensor(
                out=dv[64:128, 0], in0=s4[64:128, 0, :, 12:12 + ST], scalar=0.25, in1=us[64:128, 0, :, 15:L], op0=ALU.mult, op1=ALU.subtract))
            k.op("dve", [sw[0], mko], [mko], lambda e: e.tensor_tensor(out=s2[:, 1, :, 0:L - 7], in0=s4[:, 1, :, 0:L - 7], in1=s4[:, 1, :, 4:L - 3], op=ALU.add))
            k.op("dve", [mko, p32], [dif], lambda e: e.scalar_tensor_tensor(
                out=dv[0:64, 1], in0=s2[0:64, 1, :, 8:8 + ST], scalar=0.125, in1=us[0:64, 1, :, 15:L], op0=ALU.mult, op1=ALU.subtract))
            k.op("dve", [mko, sw[0]], [sw[0]], lambda e: e.tensor_tensor(out=s4[:, 1, :, 0:L - 15], in0=s2[:, 1, :, 0:L - 15], in1=s2[:, 1, :, 8:L - 7], op=ALU.add))
            k.op("dve", [sw[0], p32], [dif], lambda e: e.scalar_tensor_tensor(
                out=dv[64:128, 1], in0=s4[64:128, 1, :, 0:ST], scalar=0.0625, in1=us[64:128, 1, :, 15:L], op0=ALU.mult, op1=ALU.subtract))
            for c2 in range(2):
                ps = nps()
                k.op("pe", [dif, wgrp], [ps], lambda e, c2=c2: e.matmul(ps[:, 0:T], lhsT=wgrp[:, c2, :], rhs=dif[:, c2, 0:T], start=True, stop=True))
                k.op("act", [ps, pscale], [cur[0].yc_a], lambda e, c2=c2: e.activation(
                    out=cur[0].yc[:, c2, 0:T], in_=ps[:, 0:T], func=AF.Copy, scale=pscale[:, c2:c2 + 1]))
            k.dma("sp", npool_s[:, 0:11, :], spool[:, 4:15, :], [], [utm])
            ps = nps()
            for c in range(8):
                k.op("pe", [cur[0].xt, w_inA_g[0]], [ps], lambda e, c=c: e.matmul(
                    ps[0:T, 0:256], lhsT=cur[0].xt[:, c, 0:T], rhs=w_inA[:, c, 0:256], start=(c == 0), stop=(c == 7)))
            k.op("act", [ps], [utm], lambda e: e.copy(out=utm[0:T, :], in_=ps[0:T, 0:256]))
            k.dma("sp", [npool_s[:, 11 + t, :] for t in range(ST)], [utm[t:T:ST, :] for t in range(ST)], [utm], [])

        def sample_hgrn(ps_o, sm):
            T = NS * ST
            vtm = cur[0].vtm
            k.op("pool", [], [sm], lambda e: e.memset(sm[64:128, :, :], 0.0))
            for j in range(NS):
                Sb = sst[j % NPF]
                Sv = Sb[:]
                Sbf_j = Sbfs[j % 2]
                km = khm2[j % 2]
                k.op("act", [Sb], [Sbf_j], lambda e: e.copy(out=Sbf_j[:], in_=Sv))
                for h in range(4):
                    k.op("pe", [vtm[0], sm], [ps_o], lambda e, h=h: e.matmul(
                        ps_o[:, h * 128 + ST * j:h * 128 + ST * (j + 1)], lhsT=vtm[0][:, h * 128:(h + 1) * 128],
                        rhs=sm[:, h, ST * j:ST * (j + 1)], start=True, stop=False))
                    k.op("pe", [Sbf_j, qtil], [ps_o], lambda e, h=h: e.matmul(
                        ps_o[:, h * 128 + ST * j:h * 128 + ST * (j + 1)], lhsT=Sbf_j[:, h, :],
                        rhs=qtil[:, h, ST * j:ST * (j + 1)], start=False, stop=True))
                k.op("dve", [khtm[0], seqm_tm], [km], lambda e: e.tensor_scalar(
                    out=km[0:T, :], in0=khtm[0][0:T, :], scalar1=seqm_tm[:, j:j + 1], scalar2=None, op0=ALU.mult))
                cands_ = [b_ for b_ in cur_banks().banks if b_ is not ps_o]
                ps_u = cands_[j % len(cands_)]
                for h in range(4):
                    k.op("pe", [km, vtm[0]], [ps_u], lambda e, h=h: e.matmul(
                        ps_u[:, h * 128:(h + 1) * 128], lhsT=km[0:T, h * 128:(h + 1) * 128], rhs=vtm[0][0:T, h * 128:(h + 1) * 128],
                        start=True, stop=True))
                cl = ST * j + ST - 1
                k.op("dve", [Sb] + eb_h, [Sb], lambda e: e.tensor_tensor(
                    out=Sv, in0=Sv, in1=eb[:, :, cl:cl + 1].to_broadcast([128, 4, 128]), op=ALU.mult))
                k.op("dve", [Sb, ps_u], [Sb], lambda e: e.tensor_tensor(
                    out=Sv, in0=Sv, in1=ps_u[:, :].rearrange("p (h e) -> p h e", h=4), op=ALU.add))
                k.dma("sp", nhg_s[j].rearrange("h d e -> d h e"), Sv, [Sb], [])
                if j + NPF < NS:
                    k.dma("sp", Sv, shg[j + NPF].rearrange("h d e -> d h e"), [], [Sb])

        def sample_attn():
            T = NS * ST
            qcT = cur[0].qcT
            ktps = [ktp, pT]
            ktv = [ktp[:], pT[:].rearrange("p (h a) t -> p h (a t)", h=4)]
            vps = [vpd[:], memT[:, 0:4, :].rearrange("p a b -> p (a b)").rearrange("p (m h c) -> p m h c", m=2, h=4)]
            vpb = [vpd, memT]
            k.op("pool", [], [pT], lambda e: e.memset(pT[:], 0.0))
            k.op("pool", [], [memT], lambda e: e.memset(memT[:], 0.0))
            bk_ = cur_banks().banks
            pss = bk_[0:4]
            for j in range(NS):
                sgb = kst[j % NPF]
                sg_ = sgb[:]
                psb = bk_[4]
                pv = bfv(psb)
                for jj in range(2):
                    for mc in range(2):
                        k.op("pe", [sgb, ident], [psb], lambda e, jj=jj, mc=mc: e.transpose(
                            out=pv[:, (jj * 2 + mc) * 128:(jj * 2 + mc + 1) * 128], in_=sg_[:, mc, jj * 128:(jj + 1) * 128], identity=ident[:, :]))
                if j + NPF < NS:
                    k.dma("pool", sg_, cmk[j + NPF].rearrange("(c p) n -> p c n", p=128), [], [sgb])
                kt = ktv[j % 2]
                for h in range(4):
                    hh = h % 2
                    k.op("act", [psb], [ktps[j % 2]], lambda e, h=h, hh=hh: e.copy(
                        out=kt[hh * 64:(hh + 1) * 64, h, :], in_=pv[hh * 64:(hh + 1) * 64, (h // 2) * 256:(h // 2) * 256 + 256]))
                qmb = qms2[j % 2]
                qm = qmb[:]
                k.op("dve", [qcT, seqm_fm], [qmb], lambda e: e.tensor_tensor(
                    out=qm, in0=qcT[:, :, 0:T], in1=seqm_fm[:, j, :].unsqueeze(1).to_broadcast([128, 2, T]), op=ALU.mult))
                for h in range(4):
                    k.op("pe", [qmb, ktps[j % 2]], [pss[h]], lambda e, h=h: e.matmul(
                        pss[h][0:T, 0:256], lhsT=qm[:, h // 2, :], rhs=kt[:, h, :], start=(j == 0), stop=(j == NS - 1)))
            for h in range(4):
                k.op("dve", [pss[h]], [stat], lambda e, h=h: e.tensor_reduce(
                    out=stat[0:T, h:h + 1], in_=pss[h][0:T, 0:256], axis=AX.X, op=ALU.max))
            k.op("dve", [stat], [stat], lambda e: e.tensor_scalar(
                out=stat[0:T, 4:8], in0=stat[0:T, 0:4], scalar1=-0.125, scalar2=None, op0=ALU.mult))
            for h in range(4):
                k.op("act", [pss[h], stat], [p32, stat], lambda e, h=h: e.activation(
                    out=p32[0:T, h, :], in_=pss[h][0:T, 0:256], func=AF.Exp,
                    scale=0.125, bias=stat[0:T, 4 + h:5 + h], accum_out=stat[0:T, 8 + h:9 + h]))
            k.op("dve", [stat], [stat], lambda e: e.reciprocal(out=stat[0:T, 12:16], in_=stat[0:T, 8:12]))
            k.op("dve", [p32, stat], [pn], lambda e: e.tensor_tensor(
                out=pn[0:T, :, :], in0=p32[0:T, :, :], in1=stat[0:T, 12:16].unsqueeze(2).to_broadcast([T, 4, 256]), op=ALU.mult))
            psb = bk_[4]
            pv = bfv(psb)
            for h in range(4):
                for mc in range(2):
                    i = h * 2 + mc
                    k.op("pe", [pn, ident], [psb], lambda e, h=h, mc=mc, i=i: e.transpose(
                        out=pv[:, i * 128:i * 128 + T], in_=pn[0:T, h, mc * 128:(mc + 1) * 128], identity=ident[0:T, 0:T]))
            k.op("act", [psb], [pT], lambda e: e.copy(out=pT[:, :, 0:T], in_=pv.rearrange("p (i t) -> p i t", i=8)[:, :, 0:T]))
            ps_y = bk_[0]
            for j in range(NS):
                sgb = vst[j % NPF]
                sg_ = sgb[:]
                vp = vps[j % 2]
                for mc in range(2):
                    k.op("dve", [sgb], [vpb[j % 2]], lambda e, mc=mc: e.tensor_copy(
                        out=vp[:, mc, 0:4:2, 0:64], in_=sg_[:, mc, :].rearrange("p (h d) -> p h d", h=4)[:, 0:4:2, :]))
                    k.op("act", [sgb], [vpb[j % 2]], lambda e, mc=mc: e.copy(
                        out=vp[:, mc, 1:4:2, 64:128], in_=sg_[:, mc, :].rearrange("p (h d) -> p h d", h=4)[:, 1:4:2, :]))
                if j + NPF < NS:
                    k.dma("pool", sg_, cmv[j + NPF].rearrange("(c p) n -> p c n", p=128), [], [sgb])
                for jj in range(2):
                    n = 0
                    for hh in range(2):
                        h = 2 * jj + hh
                        for mc in range(2):
                            k.op("pe", [vpb[j % 2], pT], [ps_y], lambda e, h=h, mc=mc, jj=jj, n=n: e.matmul(
                                ps_y[:, jj * 128 + ST * j:jj * 128 + ST * (j + 1)], lhsT=vp[:, mc, h, :],
                                rhs=pT[:, h * 2 + mc, ST * j:ST * (j + 1)], start=(n == 0), stop=(n == 3)))
                            n += 1
            k.op("dve", [ps_y], [cur[0].yc_c], lambda e: e.tensor_copy(
                out=cur[0].yc[:, 6:8, 0:T], in_=ps_y[:, 0:256].rearrange("p (j t) -> p j t", j=2)[:, :, 0:T]))

        tiles = [(i_, i_ * TA, TA, False) for i_ in range(SEQ // TA)] + [(SEQ // TA, SEQ, NS * ST, True)]
        chains = [a1_supertile(*tl) for tl in tiles]

        def runner(gen_fn, banks):
            def f():
                use_chain(banks)
                for _ in gen_fn():
                    pass
            return f
        BX, BP, BH = [PS[0], PS[1]], [PS[2], PS[3], PS[4]], [PS[5], PS[6], PS[7]]
        for tt_ in range(2):
            load_T_dma(es, xp[tt_ * 128:(tt_ + 1) * 128, :], 128, "x%d" % tt_)
        for g in (0, 1, 3, 2, 4):
            k.dma("pool", w_inA[:, :, g * 512:(g + 1) * 512], w_in[:, g * 512:(g + 1) * 512].rearrange("(c p) n -> p c n", p=128), [], [w_inA_g[g]])
        k.rr.run([runner(chains[0][0], BX)])
        for j_ in range(NPF):
            k.dma("sp", sst[j_][:], shg[j_].rearrange("h d e -> d h e"), [], [sst[j_]])
            k.dma("pool", kst[j_][:], cmk[j_].rearrange("(c p) n -> p c n", p=128), [], [kst[j_]])
            k.dma("pool", vst[j_][:], cmv[j_].rearrange("(c p) n -> p c n", p=128), [], [vst[j_]])
        for i_, tl in enumerate(tiles):
            last = (i_ + 1 == len(tiles))
            fns = [runner(chains[i_][2], BH), runner(chains[i_][1], (BX + BP) if last else BP)]
            if not last:
                fns.append(runner(chains[i_ + 1][0], BX))
            k.rr.run(fns, [2, 1, 1][:len(fns)])
            chains[i_][3]()
            ckpt("a1 st%d" % i_)
        k.barrier()

    def layer_norm(es_, res, TT, lng, lnbb, stat, outb):
        k.op("dve", [res], [stat], lambda e: e.bn_stats(out=stat[0:TT, 0:6], in_=res[0:TT, 0:512]))
        k.op("dve", [res], [stat], lambda e: e.bn_stats(out=stat[0:TT, 6:12], in_=res[0:TT, 512:1024]))
        k.op("dve", [stat], [stat], lambda e: e.bn_aggr(out=stat[0:TT, 12:14], in_=stat[0:TT, 0:12]))
        k.op("act", [stat, epsln], [stat], lambda e: e.activation(
            out=stat[0:TT, 14:15], in_=stat[0:TT, 13:14], func=AF.Sqrt, bias=epsln[0:TT, 0:1]))
        k.op("dve", [stat], [stat], lambda e: e.reciprocal(out=stat[0:TT, 15:16], in_=stat[0:TT, 14:15]))
        k.op("dve", [res, stat, lng], [res], lambda e: e.scalar_tensor_tensor(
            out=res[0:TT, :], in0=res[0:TT, :], scalar=stat[0:TT, 12:13], in1=lng[0:TT, :], op0=ALU.subtract, op1=ALU.mult))
        k.op("dve", [res, stat, lnbb], [outb], lambda e: e.scalar_tensor_tensor(
            out=outb[0:TT, :], in0=res[0:TT, :], scalar=stat[0:TT, 15:16], in1=lnbb[0:TT, :], op0=ALU.mult, op1=ALU.add))

    with ExitStack() as es:
        set_psum(es, 8, 0, "b")
        use_chain(list(PS))
        w_inG = k.sb(es, "w_inG", [128, 8, 3072], BF16)
        w_br = k.sb(es, "w_br", [128, 8, D], BF16)
        w_o = k.sb(es, "w_o", [128, 8, D], BF16)
        lng = k.sb(es, "lng", [128, D], F32)
        lnbb = k.sb(es, "lnbb", [128, D], F32)
        NG = 4
        w_inG_g = [k.alias(w_inG, "w_inG_g%d" % g) for g in range(NG)]
        w_br_g = [k.alias(w_br, "w_br_g%d" % g) for g in range(NG)]
        for g in range(NG):
            cs = slice(g * 256, (g + 1) * 256)
            k.dma("pool", [w_br[:, 0:2, cs], w_br[:, 2:6, cs], w_br[:, 6:8, cs]],
                  [w_br_pool[:, cs].rearrange("(c p) n -> p c n", p=128), w_br_hg[:, cs].rearrange("(c p) n -> p c n", p=128),
                   w_br_mem[:, cs].rearrange("(c p) n -> p c n", p=128)], [], [w_br_g[g]])
            k.dma("pool", [w_inG[:, :, b * 1024 + g * 256:b * 1024 + (g + 1) * 256] for b in range(3)],
                  [w_in[:, 2560 + b * 1024 + g * 256:2560 + b * 1024 + (g + 1) * 256].rearrange("(c p) n -> p c n", p=128) for b in range(3)],
                  [], [w_inG_g[g]])
        k.dma("pool", w_o[:], w_out.rearrange("(c p) n -> p c n", p=128), [], [w_o])
        xt2 = [k.sb(es, "xt2_%d" % i, [128, 8, TB], BF16) for i in range(2)]
        yc2 = [k.sb(es, "yc2_%d" % i, [128, 8, TB], BF16) for i in range(2)]
        k.dma("sp", lng[:], ln1_g.partition_broadcast(128), [], [lng])
        k.dma("sp", lnbb[:], ln1_b.partition_broadcast(128), [], [lnbb])
        x32 = [k.sb(es, "x32_%d" % i, [128, D], F32) for i in range(2)]
        gsg = [k.sb(es, "gsg%d" % i, [128, TB], F32) for i in range(6)]
        t1 = [k.sb(es, "t1_%d" % i, [128, TB], F32) for i in range(2)]
        t2 = [k.sb(es, "t2_%d" % i, [128, TB], F32) for i in range(2)]
        mTs = [k.sb(es, "mT%d" % i, [128, 8, TB], BF16) for i in range(2)]
        mTs_c = [[k.alias(mTs[i], "mT%d_c%d" % (i, c_)) for c_ in range(8)] for i in range(2)]
        res = [k.sb(es, "res%d" % i, [128, D], F32) for i in range(2)]
        hout = [k.sb(es, "hout%d" % i, [128, D], F32) for i in range(2)]
        stat2 = [k.sb(es, "stat2_%d" % i, [128, 16], F32) for i in range(2)]
        cnt = [0, 0]
        BRC = ((0, 2), (2, 6), (6, 8))

        def a2_load(idx, tok0, T):
            k.dma("sp", xt2[idx % 2][:, :, 0:T], xT_scr[:, :, tok0:tok0 + T], [xscr], [xt2[idx % 2]])
            k.dma("sp", yc2[idx % 2][:, :, 0:T], yc_scr[:, :, tok0:tok0 + T], [yscr], [yc2[idx % 2]])

        def a2_jloop(idx, tok0, T, sample):
            xt_, yc_ = xt2[idx % 2], yc2[idx % 2]
            mT = mTs[idx % 2]
            for j in range(8):
                g3 = [gsg[(cnt[0] % 2) * 3 + b] for b in range(3)]
                ta, tb2 = t1[cnt[0] % 2], t2[cnt[0] % 2]
                cnt[0] += 1
                for b in range(3):
                    ps = nps()
                    for c in range(8):
                        k.op("pe", [xt_, w_inG_g[j // 2]], [ps], lambda e, c=c: e.matmul(
                            ps[:, 0:T], lhsT=w_inG[:, c, b * 1024 + j * 128:b * 1024 + (j + 1) * 128], rhs=xt_[:, c, 0:T],
                            start=(c == 0), stop=(c == 7)))
                    k.op("act", [ps], [g3[b]], lambda e, b=b: e.activation(out=g3[b][:, 0:T], in_=ps[:, 0:T], func=AF.Sigmoid))
                psb3 = []
                for b, (c0, c1) in enumerate(BRC):
                    ps = nps()
                    for c in range(c0, c1):
                        k.op("pe", [yc_, w_br_g[j // 2]], [ps], lambda e, c=c: e.matmul(
                            ps[:, 0:T], lhsT=w_br[:, c, j * 128:(j + 1) * 128], rhs=yc_[:, c, 0:T], start=(c == c0), stop=(c == c1 - 1)))
                    psb3.append(ps)
                k.op("dve", [g3[0], psb3[0]], [ta], lambda e: e.tensor_tensor(out=ta[:, 0:T], in0=g3[0][:, 0:T], in1=psb3[0][:, 0:T], op=ALU.mult))
                k.op("dve", [g3[1], psb3[1]], [tb2], lambda e: e.tensor_tensor(out=tb2[:, 0:T], in0=g3[1][:, 0:T], in1=psb3[1][:, 0:T], op=ALU.mult))
                k.op("pool", [ta, tb2], [ta], lambda e: e.tensor_tensor(out=ta[:, 0:T], in0=ta[:, 0:T], in1=tb2[:, 0:T], op=ALU.add))
                k.op("dve", [g3[2], psb3[2]], [tb2], lambda e: e.tensor_tensor(out=tb2[:, 0:T], in0=g3[2][:, 0:T], in1=psb3[2][:, 0:T], op=ALU.mult))
                k.op("pool", [ta, tb2], [mTs_c[idx % 2][j]], lambda e: e.tensor_tensor(out=mT[:, j, 0:T], in0=ta[:, 0:T], in1=tb2[:, 0:T], op=ALU.add))

        def a2_tail(idx, tok0, T, sample):
            TT = 64 if sample else 128
            ntt = T // TT
            mT = mTs[idx % 2]
            for tt in range(ntt):
                xb = x32[cnt[1] % 2]
                rb = res[cnt[1] % 2]
                hb = hout[cnt[1] % 2]
                sb2 = stat2[cnt[1] % 2]
                cnt[1] += 1
                r0 = tok0 + tt * TT
                src = xs[r0 - SEQ:r0 - SEQ + TT, :] if sample else xp[r0:r0 + TT, :]
                k.dma("sp", xb[0:TT, :], src, [], [xb])
                for half in range(2):
                    ps = nps()
                    for c in range(8):
                        k.op("pe", [mTs_c[idx % 2][c], w_o], [ps], lambda e, c=c: e.matmul(
                            ps[0:TT, :], lhsT=mT[:, c, tt * TT:(tt + 1) * TT], rhs=w_o[:, c, half * 512:(half + 1) * 512],
                            start=(c == 0), stop=(c == 7)))
                    k.op("dve", [xb, ps], [rb], lambda e, half=half: e.scalar_tensor_tensor(
                        out=rb[0:TT, half * 512:(half + 1) * 512], in0=xb[0:TT, half * 512:(half + 1) * 512], scalar=ALPHA,
                        in1=ps[0:TT, :], op0=ALU.mult, op1=ALU.add))
                layer_norm(es, rb, TT, lng, lnbb, sb2, hb)
                k.dma("pool", h_scr[r0:r0 + TT, :], hb[0:TT, :], [hb], [hscr])

        tiles2 = [(i_, i_ * TB, TB, False) for i_ in range(SEQ // TB)] + [(SEQ // TB, SEQ, NS * ST, True)]
        a2_load(0, tiles2[0][1], tiles2[0][2])
        a2_load(1, tiles2[1][1], tiles2[1][2])

        def chain(fn, banks, *a):
            def f():
                use_chain(banks)
                fn(*a)
            return f
        BJ, BT = list(PS[0:6]), list(PS[6:8])
        k.rr.run([chain(a2_jloop, BJ, *tiles2[0])])
        for i_, tl in enumerate(tiles2):
            fns = [chain(a2_tail, BT, *tl)]
            if i_ + 1 < len(tiles2):
                fns.append(chain(a2_jloop, BJ, *tiles2[i_ + 1]))
            k.rr.run(fns, [1, 3][:len(fns)])
            if i_ + 2 < len(tiles2):
                a2_load(i_ + 2, tiles2[i_ + 2][1], tiles2[i_ + 2][2])
            ckpt("a2 st%d" % i_)
        k.barrier()
    esA.close()

    with ExitStack() as es:
        set_psum(es, 7, 1, "c")
        use_chain(list(PS))
        wg = k.sb(es, "wg", [128, 8, DFF], BF16)
        wu = k.sb(es, "wu", [128, 8, DFF], BF16)
        wd = k.sb(es, "wd", [128, NFF, D], BF16)
        lng = k.sb(es, "lng2", [128, D], F32)
        lnbb = k.sb(es, "lnbb2", [128, D], F32)
        cw = k.sb(es, "cw", [128, 3, NFF], F32)
        cb = k.sb(es, "cb", [128, NFF], F32)
        hTs = [k.sb(es, "hT%d" % i, [128, 8, TB], BF16) for i in range(2)]
        for tt_ in range(TB // 128):
            load_T(es, h_scr[tt_ * 128:(tt_ + 1) * 128, :], hTs[0], tt_ * 128, 128, "h%d" % (tt_ % 2))
        FG = [(0, 2), (2, 6), (6, 14), (14, NFF)]
        wg_g = [k.alias(wg, "wg_g%d" % g) for g in range(len(FG))]
        wu_g = [k.alias(wu, "wu_g%d" % g) for g in range(len(FG))]
        ffg = {}
        for g, (a_, b_) in enumerate(FG):
            for c_ in range(a_, b_):
                ffg[c_] = g
            k.dma("pool", wg[:, :, a_ * 128:b_ * 128], w_gate[:, a_ * 128:b_ * 128].rearrange("(c p) n -> p c n", p=128), [], [wg_g[g]])
            k.dma("pool", wu[:, :, a_ * 128:b_ * 128], w_up[:, a_ * 128:b_ * 128].rearrange("(c p) n -> p c n", p=128), [], [wu_g[g]])
        k.dma("pool", wd[:], w_down.rearrange("(c p) n -> p c n", p=128), [], [wd])
        k.dma("sp", lng[:], ln2_g.partition_broadcast(128), [], [lng])
        k.dma("sp", lnbb[:], ln2_b.partition_broadcast(128), [], [lnbb])
        with nc.allow_non_contiguous_dma(reason="per-partition conv taps"):
            k.dma("sp", cw[:], conv_w.rearrange("j (c p) -> p j c", p=128), [], [cw])
            k.dma("sp", cb[:], conv_b.rearrange("(c p) -> p c", p=128), [], [cb])
        ab = [k.sb(es, "ab%d" % i, [128, 2 + TB], F32) for i in range(2)]
        cbuf = [k.sb(es, "cbuf%d" % i, [128, TB], F32) for i in range(2)]
        halo = k.sb(es, "halo", [128, NFF, 2], F32)
        actT = k.sb(es, "actT", [128, NFF, TB], BF16)
        actT_c = [k.alias(actT, "actT_c%d" % c_) for c_ in range(NFF)]
        res = [k.sb(es, "resb%d" % i, [128, D], F32) for i in range(2)]
        stat2 = [k.sb(es, "stat3_%d" % i, [128, 16], F32) for i in range(2)]
        ctm = k.sb(es, "ctm", [64, 512], F32)
        sab = k.sb(es, "sab", [128, NFF, NS * 2], F32)
        cntb = [0, 0]
        k.op("pool", [], [halo], lambda e: e.memset(halo[:], 0.0))

        def b_supertile(idx, tok0, T, sample, nxt):
            TT = 64 if sample else 128
            ntt = T // TT
            hT = hTs[idx % 2]
            if sample:
                NB, TL = NS, ST
            else:
                NB, TL = 1, T
            L = TL + 2
            pend = []
            if nxt is not None:
                n_tok0, n_T, n_sample = nxt
                n_TT = 64 if n_sample else 128
                pend = [(n_tok0 + t_ * n_TT, t_, n_TT) for t_ in range(n_T // n_TT)]

            def pf_dma(i):
                if i < len(pend):
                    r0_, t_, n_ = pend[i]
                    load_T_dma(es, h_scr[r0_:r0_ + n_, :], n_, "h%d" % (t_ % 2))

            def pf_tr(i):
                if i < len(pend):
                    r0_, t_, n_ = pend[i]
                    load_T(es, h_scr[r0_:r0_ + n_, :], hTs[(idx + 1) % 2], t_ * n_, n_, "h%d" % (t_ % 2), dma=False)
            pf_dma(0)
            pf_dma(1)
            for c in range(NFF):
                a_ = ab[cntb[0] % 2]
                cb_ = cbuf[cntb[0] % 2]
                cntb[0] += 1
                ps_a = nps()
                for kk_ in range(8):
                    k.op("pe", [hT, wg_g[ffg[c]]], [ps_a], lambda e, kk_=kk_: e.matmul(
                        ps_a[:, 0:T], lhsT=wg[:, kk_, c * 128:(c + 1) * 128], rhs=hT[:, kk_, 0:T], start=(kk_ == 0), stop=(kk_ == 7)))
                ps_u = nps()
                for kk_ in range(8):
                    k.op("pe", [hT, wu_g[ffg[c]]], [ps_u], lambda e, kk_=kk_: e.matmul(
                        ps_u[:, 0:T], lhsT=wu[:, kk_, c * 128:(c + 1) * 128], rhs=hT[:, kk_, 0:T], start=(kk_ == 0), stop=(kk_ == 7)))
                av = a_[:, 0:NB * L].rearrange("p (b l) -> p b l", b=NB)
                cv = cb_[:, 0:T].rearrange("p (b l) -> p b l", b=NB)
                pav = ps_a[:, 0:T].rearrange("p (b l) -> p b l", b=NB)
                if sample:
                    k.op("pool", [sab], [a_], lambda e: e.tensor_copy(
                        out=av[:, :, 0:2], in_=sab[:, c, :].rearrange("p (b r) -> p b r", r=2)))
                else:
                    k.op("pool", [halo], [a_], lambda e: e.tensor_copy(out=a_[:, 0:2], in_=halo[:, c, :]))
                k.op("act", [ps_a], [a_], lambda e: e.copy(out=av[:, :, 2:L], in_=pav))
                k.op("act", [ps_a, cw, cb], [cb_], lambda e: e.activation(
                    out=cb_[:, 0:T], in_=ps_a[:, 0:T], func=AF.Identity, scale=cw[:, 2, c:c + 1], bias=cb[:, c:c + 1]))
                k.op("dve", [a_, cb_, cw], [cb_], lambda e: e.scalar_tensor_tensor(
                    out=cv, in0=av[:, :, 1:1 + TL], scalar=cw[:, 1, c:c + 1], in1=cv, op0=ALU.mult, op1=ALU.add))
                k.op("dve", [a_, cb_, cw], [cb_], lambda e: e.scalar_tensor_tensor(
                    out=cv, in0=av[:, :, 0:TL], scalar=cw[:, 0, c:c + 1], in1=cv, op0=ALU.mult, op1=ALU.add))
                if not sample:
                    k.op("pool", [a_], [halo], lambda e: e.tensor_copy(out=halo[:, c, :], in_=a_[:, T:T + 2]))
                k.op("act", [cb_], [cb_], lambda e: e.activation(out=cb_[:, 0:T], in_=cb_[:, 0:T], func=AF.Gelu))
                k.op("dve", [cb_, ps_u], [actT_c[c]], lambda e: e.tensor_tensor(out=actT[:, c, 0:T], in0=cb_[:, 0:T], in1=ps_u[:, 0:T], op=ALU.mult))
            pf_tr(0)
            pf_tr(1)
            pf_dma(2)
            pf_dma(3)
            for tt in range(ntt):
                if tt == 1:
                    pf_tr(2)
                    pf_tr(3)
                rb = res[cntb[1] % 2]
                hb = rb
                yb = rb
                sb2 = stat2[cntb[1] % 2]
                cntb[1] += 1
                r0 = tok0 + tt * TT
                k.dma("sp", hb[0:TT, :], h_scr[r0:r0 + TT, :], [hscr], [hb])
                for half in range(2):
                    ps = nps()
                    for c in range(NFF):
                        k.op("pe", [actT_c[c], wd], [ps], lambda e, c=c: e.matmul(
                            ps[0:TT, :], lhsT=actT[:, c, tt * TT:(tt + 1) * TT], rhs=wd[:, c, half * 512:(half + 1) * 512],
                            start=(c == 0), stop=(c == NFF - 1)))
                    k.op("dve", [hb, ps], [rb], lambda e, half=half: e.scalar_tensor_tensor(
                        out=rb[0:TT, half * 512:(half + 1) * 512], in0=hb[0:TT, half * 512:(half + 1) * 512], scalar=ALPHA,
                        in1=ps[0:TT, :], op0=ALU.mult, op1=ALU.add))
                layer_norm(es, rb, TT, lng, lnbb, sb2, yb)
                dst = ys[r0 - SEQ:r0 - SEQ + TT, :] if sample else yp[r0:r0 + TT, :]
                k.dma("pool", dst, yb[0:TT, :], [yb], [])
            if sample or tok0 + T == SEQ:
                M = 64 if sample else 2
                c0 = 0 if sample else T - 2
                for n0 in range(0, DFF, 512):
                    nn = min(512, DFF - n0)
                    ps = nps()
                    for kk_ in range(8):
                        k.op("pe", [hT] + wg_g, [ps], lambda e, kk_=kk_: e.matmul(
                            ps[0:M, 0:nn], lhsT=hT[:, kk_, c0:c0 + M], rhs=wg[:, kk_, n0:n0 + nn], start=(kk_ == 0), stop=(kk_ == 7)))
                    k.op("act", [ps], [ctm], lambda e: e.copy(out=ctm[0:M, 0:nn], in_=ps[0:M, 0:nn]))
                    if sample:
                        k.dma("sp", [nconv_s[:, r, n0:n0 + nn] for r in range(2)],
                              [ctm[2 + r:64:4, 0:nn] for r in range(2)], [ctm], [])
                    else:
                        k.dma("sp", nconv_p[:, n0:n0 + nn], ctm[0:2, 0:nn], [ctm], [])

        tilesb = [(i_ * TB, TB, False) for i_ in range(SEQ // TB)] + [(SEQ, NS * ST, True)]
        for st in range(SEQ // TB):
            b_supertile(st, *tilesb[st], tilesb[st + 1])
            ckpt("b st%d" % st)
        for p0 in range(0, DFF, 1024):
            pn_ = min(1024, DFF - p0)
            stg = res[(p0 // 1024) % 2]
            k.dma("sp", stg[0:32, 0:pn_], sconv[:, p0:p0 + pn_], [], [stg])
            for c in range(p0 // 128, (p0 + pn_) // 128):
                ps = nps()
                k.op("pe", [stg, ident32], [ps], lambda e, c=c: e.transpose(
                    out=ps[:, 0:32], in_=stg[0:32, c * 128 - p0:(c + 1) * 128 - p0], identity=ident32[0:32, 0:32]))
                k.op("dve", [ps], [sab], lambda e, c=c: e.tensor_copy(out=sab[:, c, :], in_=ps[:, 0:32]))
        b_supertile(SEQ // TB, *tilesb[-1], None)
        ckpt("b sample")
        k.barrier()


def _consts():
    c = {}
    c["c_ident"] = np.eye(128, dtype=np.float32)
    c["c_ones"] = np.full((128, 128), 1.0 / 128.0, np.float32)
    rp = np.ones((128, TA), np.float32)
    rp[:, ::64] = 0.0
    c["c_rmask_p"] = rp
    rs = np.ones((128, 64), np.float32)
    rs[:, ::ST] = 0.0
    c["c_rmask_s"] = rs
    s = np.arange(128)[:, None]
    t = np.arange(128)[None, :]
    c["c_cmask_p"] = ((s // 64 == t // 64) & (s <= t)).astype(np.float32)
    s = np.arange(64)[:, None]
    t = np.arange(64)[None, :]
    c["c_cmask_s"] = ((s // ST == t // ST) & (s <= t)).astype(np.float32)
    ic = np.zeros((128, 2, 16), np.float32)
    for c2 in range(2):
        for half in range(2):
            w = 2 ** (2 * c2 + half + 1)
            ic[half * 64:(half + 1) * 64, c2, :] = 1.0 / np.minimum(np.arange(16) + 1, w)
    c["c_invcnt"] = ic
    tt = np.arange(64)
    c["c_seqm_tm"] = (tt[:, None] // ST == np.arange(16)[None, :]).astype(np.float32)
    c["c_seqm_fm"] = np.broadcast_to((np.arange(16)[:, None] == tt[None, :] // ST).astype(np.float32), (128, 16, 64)).copy()
    return c


_NC_CACHE = {}


def kernel(x_prompt, x_sample, state_pool, state_hgrn, state_ffn_conv, cache_mem_k, cache_mem_v, mem_prompt,
           lb_logits, w_in, w_pool_grp, pool_scale, hg_norm_g, w_mem_k, w_mem_v, w_br_pool, w_br_hg, w_br_mem,
           w_out, ln1_g, ln1_b, w_gate, w_up, conv_w, conv_b, w_down, ln2_g, ln2_b):
    f = lambda a: np.ascontiguousarray(np.asarray(a, dtype=np.float32))
    if "nc" not in _NC_CACHE:
        _NC_CACHE["nc"] = _build()
    nc = _NC_CACHE["nc"]
    shared = {
        "lbl": f(lb_logits), "w_in": f(w_in)[0], "w_grp": f(w_pool_grp)[0], "pool_scale": f(pool_scale)[0],
        "hg_g": f(hg_norm_g)[0], "w_mem_k": f(w_mem_k)[0], "w_mem_v": f(w_mem_v)[0], "w_br_pool": f(w_br_pool)[0],
        "w_br_hg": f(w_br_hg)[0], "w_br_mem": f(w_br_mem)[0], "w_out": f(w_out)[0], "ln1_g": f(ln1_g)[0],
        "ln1_b": f(ln1_b)[0], "w_gate": f(w_gate)[0], "w_up": f(w_up)[0], "conv_w": f(conv_w)[0], "conv_b": f(conv_b)[0],
        "w_down": f(w_down)[0], "ln2_g": f(ln2_g)[0], "ln2_b": f(ln2_b)[0],
    }
    shared.update(_consts())
    xp_, xs_ = f(x_prompt), f(x_sample)
    sp_, sh_, sc_ = f(state_pool)[0], f(state_hgrn)[0], f(state_ffn_conv)[0]
    ck_, cv_, mp_ = f(cache_mem_k)[0], f(cache_mem_v)[0], f(mem_prompt)
    in_maps = []
    for c in range(NCORES):
        b = slice(c * NS, (c + 1) * NS)
        m = dict(shared)
        m["xp"] = xp_[c]
        m["xs"] = xs_[b].reshape(NS * ST, D)
        m["spool"] = sp_[b]
        m["shg"] = sh_[b]
        m["sconv"] = sc_[b].reshape(NS * 2, DFF)
        m["cmk"] = ck_[b].reshape(NS, 256, 256)
        m["cmv"] = cv_[b].reshape(NS, 256, 256)
        m["memp"] = mp_[c]
        in_maps.append(m)
    res = run_bass_kernel_spmd(nc, in_maps, core_ids=list(range(NCORES)))
    R = res.results
    g = lambda n: np.stack([np.asarray(R[c][n], dtype=np.float32) for c in range(NCORES)])
    y_p = g("yp")
    y_s = g("ys").reshape(NCORES * NS, ST, D)
    o_pool_p = g("npool_p")[None]
    o_hg_p = g("nhg_p")[None]
    o_conv_p = g("nconv_p")[None]
    o_mk_p = g("nmk_p").reshape(1, NCORES, 256, 4, 64)
    o_mv_p = g("nmv_p").reshape(1, NCORES, 256, 4, 64)
    o_pool_s = g("npool_s").reshape(1, NCORES * NS, 15, 256)
    o_hg_s = g("nhg_s").reshape(1, NCORES * NS, 4, 128, 128)
    o_conv_s = g("nconv_s").reshape(1, NCORES * NS, 2, DFF)
    return (y_p, y_s, o_pool_p, o_hg_p, o_conv_p, o_mk_p, o_mv_p, o_pool_s, o_hg_s, o_conv_s)
```
